# Optimizing a Trainium2 kernel written in Bass

```python
import math
import jax, jax.numpy as jnp
from jax import lax
import numpy as np

D_MODEL = 1024
BATCH = 4
SEQ = 4096
DEPTH = 1

GLA_HEADS = 4
GLA_DK = (D_MODEL // 2) // GLA_HEADS
GLA_DV = D_MODEL // GLA_HEADS
GLA_GATE_RANK = 16
GLA_GATE_NORMALIZER = 16.0
HGRN_EXPAND = 128
HGRN_HEADS = D_MODEL // HGRN_EXPAND
HGRN_DK = HGRN_EXPAND
HGRN_DV = D_MODEL // HGRN_HEADS

GLA_QK = GLA_HEADS * GLA_DK
GLA_V = GLA_HEADS * GLA_DV
HG_K = HGRN_HEADS * HGRN_DK
HG_V = HGRN_HEADS * HGRN_DV
SPLIT_SIZES = (GLA_QK, GLA_QK, GLA_V, GLA_V, GLA_GATE_RANK, GLA_GATE_RANK,
               HG_K, HG_K, HG_K, HG_V, HG_V, D_MODEL, D_MODEL)
PROJ_DIM = sum(SPLIT_SIZES)

CHUNK = 64
RMS_EPS = 1e-6
LN_EPS = 1e-5
DEEPNORM_ALPHA = (2.0 * DEPTH) ** 0.25
DEEPNORM_BETA = (8.0 * DEPTH) ** -0.25

kernel_name = 'hybrid_gla_hgrn2_bidir_deepnorm'


def _split_columns(p):
    points = np.cumsum(np.array(SPLIT_SIZES))[:-1].tolist()
    return jnp.split(p, points, axis=-1)


def _heads(a, h):
    b, t, c = a.shape
    return a.reshape(b, t, h, c // h)


def chunk_gated_linear_attn(q, k, v, g):
    B, T, H, dk = q.shape
    dv = v.shape[-1]
    n = T // CHUNK

    def to_chunks(a):
        return a.reshape(B, n, CHUNK, H, a.shape[-1]).transpose(1, 0, 3, 2, 4)

    qc, kc, vc, gc = to_chunks(q), to_chunks(k), to_chunks(v), to_chunks(g)
    mask = jnp.tril(jnp.ones((CHUNK, CHUNK), dtype=bool))[None, None, :, :, None]

    def step(S, inp):
        qi, ki, vi, gi = inp
        b = jnp.cumsum(gi, axis=2)
        b_last = b[:, :, -1:, :]
        o_inter = jnp.einsum('bhcd,bhde->bhce', qi * jnp.exp(b), S)
        rel = jnp.where(mask, b[:, :, :, None, :] - b[:, :, None, :, :], -jnp.inf)
        scores = jnp.einsum('bhid,bhjd,bhijd->bhij', qi, ki, jnp.exp(rel))
        o = o_inter + jnp.einsum('bhij,bhje->bhie', scores, vi)
        S = (jnp.exp(b_last[:, :, 0, :])[..., None] * S
             + jnp.einsum('bhjd,bhje->bhde', ki * jnp.exp(b_last - b), vi))
        return S, o

    S0 = jnp.zeros((B, H, dk, dv), jnp.float32)
    _, o = lax.scan(step, S0, (qc, kc, vc, gc))
    return o.transpose(1, 0, 3, 2, 4).reshape(B, T, H, dv)


def bidirectional_scan(q, k_f, k_b, v, g_f, g_b):
    fwd = chunk_gated_linear_attn(q, k_f, v, g_f)
    flip = lambda a: jnp.flip(a, axis=1)
    bwd = flip(chunk_gated_linear_attn(flip(q), flip(k_b), flip(v), flip(g_b)))
    return fwd + bwd


def head_rmsnorm(o, gain):
    return o * lax.rsqrt(jnp.mean(o * o, axis=-1, keepdims=True) + RMS_EPS) * gain


def layer_norm(x, g, b):
    mu = jnp.mean(x, axis=-1, keepdims=True)
    xc = x - mu
    var = jnp.mean(xc * xc, axis=-1, keepdims=True)
    return xc * lax.rsqrt(var + LN_EPS) * g + b


def hybrid_layer(h, layer, w_in, gk_up_f, gk_bias_f, gk_up_b, gk_bias_b, gla_norm_g,
                 lb_logits_f, lb_logits_b, hgrn_norm_g, w_branch_gla, w_branch_hgrn,
                 w_out, ln_g, ln_b):
    f32 = jnp.float32
    B, T, _ = h.shape
    p = jnp.einsum('btd,dp->btp', h, w_in)
    (a_q, a_k, a_v, a_gate, a_lr_f, a_lr_b,
     h_q, h_f_f, h_f_b, h_i, h_gate, m_gla, m_hgrn) = _split_columns(p)

    g_f = jax.nn.log_sigmoid(a_lr_f @ gk_up_f + gk_bias_f) / GLA_GATE_NORMALIZER
    g_b = jax.nn.log_sigmoid(a_lr_b @ gk_up_b + gk_bias_b) / GLA_GATE_NORMALIZER
    qa = _heads(a_q, GLA_HEADS) * (GLA_DK ** -0.5)
    ka = _heads(a_k, GLA_HEADS)
    o_a = bidirectional_scan(qa, ka, ka, _heads(a_v, GLA_HEADS),
                             _heads(g_f, GLA_HEADS), _heads(g_b, GLA_HEADS))
    o_a = head_rmsnorm(o_a, gla_norm_g).reshape(B, T, GLA_V) * jax.nn.silu(a_gate)
    y_gla = o_a @ w_branch_gla

    lb_f = jnp.cumsum(jax.nn.softmax(lb_logits_f, axis=0), axis=0)[layer]
    lb_b = jnp.cumsum(jax.nn.softmax(lb_logits_b, axis=0), axis=0)[layer]
    f_f = lb_f + (1.0 - lb_f) * jax.nn.sigmoid(h_f_f)
    f_b = lb_b + (1.0 - lb_b) * jax.nn.sigmoid(h_f_b)
    qh = _heads(jax.nn.silu(h_q), HGRN_HEADS) * (HGRN_DK ** -0.5)
    o_h = bidirectional_scan(qh, _heads(1.0 - f_f, HGRN_HEADS), _heads(1.0 - f_b, HGRN_HEADS),
                             _heads(h_i, HGRN_HEADS),
                             _heads(jnp.log(f_f), HGRN_HEADS), _heads(jnp.log(f_b), HGRN_HEADS))
    o_h = head_rmsnorm(o_h, hgrn_norm_g).reshape(B, T, HG_V) * jax.nn.silu(h_gate)
    y_hgrn = o_h @ w_branch_hgrn

    y = jax.nn.sigmoid(m_gla) * y_gla + jax.nn.sigmoid(m_hgrn) * y_hgrn
    y = y @ w_out
    return layer_norm(DEEPNORM_ALPHA * h + y, ln_g, ln_b).astype(f32)


def setup_inputs(seed: int = 0) -> dict:
    key = jax.random.key(seed)
    ks = jax.random.split(key, 16)
    n = jax.random.normal
    D = D_MODEL
    return {
        'x': n(ks[0], (BATCH, SEQ, D), jnp.float32),
        'w_in': n(ks[1], (DEPTH, D, PROJ_DIM), jnp.float32) * D ** -0.5,
        'gla_gk_up_f': n(ks[2], (DEPTH, GLA_GATE_RANK, GLA_QK), jnp.float32) * GLA_GATE_RANK ** -0.5,
        'gla_gk_bias_f': n(ks[3], (DEPTH, GLA_QK), jnp.float32) * 0.02,
        'gla_gk_up_b': n(ks[4], (DEPTH, GLA_GATE_RANK, GLA_QK), jnp.float32) * GLA_GATE_RANK ** -0.5,
        'gla_gk_bias_b': n(ks[5], (DEPTH, GLA_QK), jnp.float32) * 0.02,
        'gla_norm_g': 1.0 + 0.02 * n(ks[6], (DEPTH, GLA_DV), jnp.float32),
        'hgrn_lb_logits_f': n(ks[7], (DEPTH + 1, HG_K), jnp.float32) * 0.1,
        'hgrn_lb_logits_b': n(ks[8], (DEPTH + 1, HG_K), jnp.float32) * 0.1,
        'hgrn_norm_g': 1.0 + 0.02 * n(ks[9], (DEPTH, HGRN_DV), jnp.float32),
        'w_branch_gla': n(ks[10], (DEPTH, GLA_V, D), jnp.float32) * (GLA_V ** -0.5) * DEEPNORM_BETA,
        'w_branch_hgrn': n(ks[11], (DEPTH, HG_V, D), jnp.float32) * (HG_V ** -0.5) * DEEPNORM_BETA,
        'w_out': n(ks[12], (DEPTH, D, D), jnp.float32) * (D ** -0.5) * DEEPNORM_BETA,
        'ln_g': 1.0 + 0.02 * n(ks[13], (DEPTH, D), jnp.float32),
        'ln_b': 0.02 * n(ks[14], (DEPTH, D), jnp.float32),
    }


def reference(x, w_in, gla_gk_up_f, gla_gk_bias_f, gla_gk_up_b, gla_gk_bias_b, gla_norm_g,
              hgrn_lb_logits_f, hgrn_lb_logits_b, hgrn_norm_g, w_branch_gla, w_branch_hgrn,
              w_out, ln_g, ln_b):
    f32 = jnp.float32
    c = lambda a: a.astype(f32)
    h = c(x)
    lbf = c(hgrn_lb_logits_f)
    lbb = c(hgrn_lb_logits_b)
    for layer in range(DEPTH):
        h = hybrid_layer(h, layer, c(w_in[layer]), c(gla_gk_up_f[layer]), c(gla_gk_bias_f[layer]),
                         c(gla_gk_up_b[layer]), c(gla_gk_bias_b[layer]), c(gla_norm_g[layer]),
                         lbf, lbb, c(hgrn_norm_g[layer]), c(w_branch_gla[layer]),
                         c(w_branch_hgrn[layer]), c(w_out[layer]), c(ln_g[layer]), c(ln_b[layer]))
    return h.astype(x.dtype)
```

```python
import math
from contextlib import ExitStack

import numpy as np
import concourse.bass as bass
import concourse.mybir as mybir
from concourse.bass_utils import run_bass_kernel_spmd

F32 = mybir.dt.float32
BF16 = mybir.dt.bfloat16
AF = mybir.ActivationFunctionType
ALU = mybir.AluOpType

P = 128
D = 1024
NT = 16
NCORES = 8
W1C, W2C, W3C = 5136, 3088, 2048
WFLAT = 8 * W1C
Q_SCALE = 128.0 ** -0.5
LN_QS = math.log(Q_SCALE)
ALPHA = 2.0 ** 0.25
RMS_EPS = 1e-6
LN_EPS = 1e-5
SAME_ENGINE_WAITS = True
PRIO = 3


class Sched:
    def __init__(self, nc, es):
        self.nc = nc
        self.es = es
        self.ops = []
        self.last_w = {}
        self.readers = {}
        self.eng_sem = {}
        self.eng_cnt = {}
        self.chan = {}
        self.pending = {}
        self.last_access = {}
        self.bar_deps = []
        for e in ("pe", "act", "dve", "pool"):
            self.eng_sem[e] = es.enter_context(nc.semaphore("s_" + e))
            self.eng_cnt[e] = 0

    def _deps(self, reads, writes, skip_waw=False):
        deps = []
        for r in reads:
            if r in self.last_w:
                deps.append((self.last_w[r], "raw"))
        for w in writes:
            if w in self.last_w and not skip_waw:
                deps.append((self.last_w[w], "waw"))
            for rd in self.readers.get(w, ()):
                deps.append((rd, "war"))
        return deps

    def add(self, eng, fn, reads=(), writes=(), chan=None, inc=True, cc=False, skip_waw=False):
        idx = len(self.ops)
        deps = self._deps(reads, writes, skip_waw) + list(self.bar_deps)
        for r in list(reads) + list(writes):
            if r.startswith("ps"):
                la = self.last_access.get(r)
                if la is not None and self.ops[la]["eng"] != eng:
                    deps.append((la, "psx"))
                self.last_access[r] = idx
        op = dict(eng=eng, fn=fn, deps=deps, done=None, kind="c", own=(chan is not None or inc))
        if chan is not None:
            if chan not in self.chan:
                self.chan[chan] = [self.es.enter_context(self.nc.semaphore("d_" + chan)), 0]
            c = self.chan[chan]
            step = 1 if cc else 16
            c[1] += step
            op["done"] = (c[0], c[1])
            op["kind"] = "cc" if cc else "dma"
        elif inc:
            self.eng_cnt[eng] += 1
            op["done"] = (self.eng_sem[eng], self.eng_cnt[eng])
            for po in self.pending.get(eng, ()):
                po["done"] = op["done"]
            self.pending[eng] = []
        else:
            self.pending.setdefault(eng, []).append(op)
        self.ops.append(op)
        for w in writes:
            self.last_w[w] = idx
            self.readers[w] = []
        for r in reads:
            self.readers.setdefault(r, []).append(idx)
        return idx

    def barrier(self):
        last = {}
        for i, o in enumerate(self.ops):
            if o["done"] is not None:
                last[(o["eng"], id(o["done"][0]))] = i
        self.bar_deps = [(i, "bar") for i in last.values()]

    def emit(self, block):
        nc = self.nc
        handles = {"pe": "tensor", "act": "scalar", "dve": "vector", "pool": "gpsimd", "sp": "sync"}
        for eng, hname in handles.items():
            my = [o for o in self.ops if o["eng"] == eng]

            def body(h, my=my, eng=eng):
                waited = {}
                for op in my:
                    needs = {}
                    for (di, kind) in op["deps"]:
                        dop = self.ops[di]
                        if dop["done"] is None:
                            continue
                        if dop["eng"] == eng and dop["kind"] == "c":
                            if not SAME_ENGINE_WAITS or eng == "pe":
                                continue
                        sem, val = dop["done"]
                        key = id(sem)
                        if key not in needs or needs[key][1] < val:
                            needs[key] = (sem, val)
                    for key, (sem, val) in needs.items():
                        if waited.get(key, 0) >= val:
                            continue
                        waited[key] = val
                        h.wait_ge(sem, val)
                    ins = op["fn"](h)
                    if op["done"] is not None and op["own"]:
                        sem, val = op["done"]
                        if op["kind"] == "dma":
                            ins.then_inc(sem, 16)
                        elif op["kind"] == "cc":
                            ins.then_inc(sem)
                        else:
                            ins.then_inc(sem, 1)
                if eng == "sp":
                    for name, (sem, val) in self.chan.items():
                        h.wait_ge(sem, val)

            getattr(block, hname)(body)


def build_program(NT=NT, stop=None):
    nc = bass.Bass("TRN2", target_bir_lowering=False)
    dt_in = lambda name, shape: nc.dram_tensor(name, shape, F32, kind="ExternalInput").ap()
    x_d = dt_in("x", [NT * P, D])
    w1_d = dt_in("w1", [P, 8, W1C])
    w2_d = dt_in("w2", [P, 8, W2C])
    w3_d = dt_in("w3", [P, 8, W3C])
    wb_d = dt_in("wb", [P, 24, D])
    gk_d = [dt_in("gk1", [17, 512]), dt_in("gk2", [17, 512])]
    lbl_d = [dt_in("lbl1", [P, 16]), dt_in("lbl2", [P, 16])]
    gn_d = dt_in("gn", [P, 3])
    lng_d = dt_in("lng", [P, D])
    lnb_d = dt_in("lnb", [P, D])
    sel_d = dt_in("sel", [P, 2])
    cst_d = dt_in("cst", [P, 3 * 128 + 512])
    out_d = nc.dram_tensor("out", [NT * P, D], F32, kind="ExternalOutput").ap()

    qT_s = nc.dram_tensor("qT_s", [NT, P, 512], BF16)
    kT_s = nc.dram_tensor("kT_s", [NT, P, 512], BF16)
    hq_s = nc.dram_tensor("hq_s", [NT, P, 1024], BF16)
    v_s = nc.dram_tensor("v_s", [NT, P, 1024], BF16)
    hi_s = nc.dram_tensor("hi_s", [NT, P, 1024], BF16)
    o1_s = nc.dram_tensor("o1_s", [NT, P, 2048], BF16)
    og_s = nc.dram_tensor("og_s", [NT, P, 2048], BF16)
    g_s = nc.dram_tensor("g_s", [NT, P, 2048], BF16)
    cc_in = nc.dram_tensor("cc_in", [P, 2048], F32)
    cc_out = nc.dram_tensor("cc_out", [2 * P, 2048], F32)

    with ExitStack() as es:
        sb = lambda name, shape, dt: es.enter_context(nc.sbuf_tensor(name, shape, dt))
        wflat = sb("wflat", [P, WFLAT], BF16)
        W1 = wflat[:, 0:8 * W1C].rearrange("p (k c) -> p k c", c=W1C)
        W2 = wflat[:, 0:8 * W2C].rearrange("p (k c) -> p k c", c=W2C)
        W3 = wflat[:, 0:8 * W3C].rearrange("p (k c) -> p k c", c=W3C)
        WB = wflat[:, WFLAT - 24 * D:WFLAT].rearrange("p (k c) -> p k c", c=D)

        cst_f = sb("cst_f", [P, 512], F32)
        cst_b = sb("cst_b", [P, 3 * 128], BF16)
        ident = cst_b[:, 0:128]
        tri = [cst_b[:, 128:256], cst_b[:, 256:384]]
        smask = cst_f[:, 0:512]
        gkb = [sb("gk1b", [17, 512], BF16), sb("gk2b", [17, 512], BF16)]
        lbl = [sb("lbl1s", [P, 16], F32), sb("lbl2s", [P, 16], F32)]
        lbv = [sb("lbv1", [P, 8], F32), sb("lbv2", [P, 8], F32)]
        omlv = [sb("omlv1", [P, 8], F32), sb("omlv2", [P, 8], F32)]
        lbt = sb("lbt", [P, 16], F32)
        gn = sb("gns", [P, 3], F32)
        sel = sb("sels", [P, 2], F32)

        x_f = [sb("x_f%d" % i, [P, D], F32) for i in range(3)]
        x_bf = [sb("x_bf%d" % i, [P, D], BF16) for i in range(2)]
        xT = [sb("xT%d" % i, [P, D], BF16) for i in range(2)]
        v_bf = [sb("v_bf%d" % i, [P, 1024], BF16) for i in range(2)]
        hi_bf = [sb("hi_bf%d" % i, [P, 1024], BF16) for i in range(2)]
        hq_raw = [sb("hq_raw%d" % i, [P, 1024], BF16) for i in range(2)]
        o1b = [sb("o1b%d" % i, [P, 2048], BF16) for i in range(2)]
        sg = [sb("sg%d" % i, [P, 2048], BF16) for i in range(2)]
        lrT = sb("lrT", [17, P], BF16)
        T4s = [sb("T4_%d" % i, [P, 2048], F32) for i in range(3)]
        TS = [[T4s[u][:, i * 512:(i + 1) * 512] for i in range(4)] for u in range(3)]
        TN = [["T%d_%d" % (u, i) for i in range(4)] for u in range(3)]
        lng, lnb = T4s[0][:, 0:1024], T4s[0][:, 1024:2048]
        U = sb("U", [P, 10752], BF16)
        QT = [U[:, (0 + u) * 512:(1 + u) * 512] for u in range(3)]
        KT = [U[:, (3 + u) * 512:(4 + u) * 512] for u in range(3)]
        KH = [U[:, (6 + u) * 512:(7 + u) * 512] for u in range(3)]
        KHt = [U[:, (9 + i) * 512:(10 + i) * 512] for i in range(3)]
        scT = [U[:, (12 + i) * 512:(13 + i) * 512] for i in range(3)]
        qT_raw = U[:, 15 * 512:16 * 512]
        kT_raw = U[:, 16 * 512:17 * 512]
        Sb = U[:, 17 * 512:21 * 512]
        T4b = T4s[1][:, :].bitcast(BF16)
        ogT = [U[:, 0:2048], T4b[:, 0:2048]]
        sigm = [U[:, 2048:4096], T4b[:, 2048:4096]]
        y_bf = U[:, 4096:5120]
        yT = U[:, 5120:6144]
        ores = [U[:, 6144 + i * 2048:6144 + (i + 1) * 2048].bitcast(F32) for i in range(2)]
        elast = [[sb("elast%d_%d" % (u, i), [P, 4], F32) for i in range(2)] for u in range(3)]
        nbl = [sb("nbl%d" % u, [P, 4], F32) for u in range(3)]
        S = sb("S", [P, 2048], F32)
        otot = sb("otot", [P, 2048], F32)
        sq = sb("sq", [P, 1024], F32)
        og = [sb("og%d" % i, [P, 2048], BF16) for i in range(2)]
        ss = [sb("ss%d" % i, [P, 4], F32) for i in range(3)]
        rstd = [sb("rstd%d" % i, [P, 4], F32) for i in range(3)]
        bst = sb("bst", [P, 12], F32)
        mv = sb("mv", [P, 2], F32)
        lnr = sb("lnr", [P, 2], F32)

        banks = [es.enter_context(nc.psum_tensor("ps%d" % i, [P, 512], F32)) for i in range(8)]
        sc = Sched(nc, es)
        free_banks = list(range(8))

        def nb():
            assert free_banks, "out of PSUM banks at emission time"
            i = free_banks.pop(0)
            return banks[i], "ps%d" % i

        def fb(*names):
            for n in names:
                i = int(n[2:])
                assert i not in free_banks
                free_banks.append(i)

        def dma(eng, out, in_, reads, writes, chan, skip_waw=False):
            sc.add(eng, lambda h: h.dma_start(out=out, in_=in_), reads, writes, chan=chan, skip_waw=skip_waw)

        def mm(out, lhsT, rhs, start, stop, reads, writes, inc):
            sc.add("pe", lambda h: h.matmul(out, lhsT=lhsT, rhs=rhs, start=start, stop=stop),
                   reads, writes, inc=inc)

        def tr(out, in_, reads, writes, inc):
            sc.add("pe", lambda h: h.transpose(out, in_, ident), reads + ["cst_b"], writes, inc=inc)

        def act(out, in_, func, reads, writes, scale=1.0, bias=0.0, accum=None):
            if accum is None:
                sc.add("act", lambda h: h.activation(out=out, in_=in_, func=func, scale=scale, bias=bias),
                       reads, writes)
            else:
                sc.add("act", lambda h: h.activation(out=out, in_=in_, func=func, scale=scale, bias=bias,
                                                     accum_out=accum), reads, writes)

        def tt(eng, out, in0, in1, op, reads, writes):
            sc.add(eng, lambda h: h.tensor_tensor(out=out, in0=in0, in1=in1, op=op), reads, writes)

        def cp(eng, out, in_, reads, writes):
            if eng == "act":
                sc.add(eng, lambda h: h.copy(out=out, in_=in_), reads, writes)
            else:
                sc.add(eng, lambda h: h.tensor_copy(out=out, in_=in_), reads, writes)

        def h3(ap):
            return ap.rearrange("p (h t) -> p h t", t=128)

        dma("sp", cst_f[:, :], cst_d[:, 384:896], [], ["cst_f"], "c0")
        dma("pool", cst_b[:, :], cst_d[:, 0:384], [], ["cst_b"], "c0b")
        for i in range(2):
            dma("pool", gkb[i][:, :], gk_d[i], [], ["gkb%d" % i], "c1_%d" % i)
            dma("sp", lbl[i][:, :], lbl_d[i], [], ["lbl%d" % i], "c2_%d" % i)
        dma("sp", gn[:, :], gn_d, [], ["gn"], "c3")
        dma("sp", sel[:, :], sel_d, [], ["sel"], "c6")
        sc.add("pool", lambda h: h.memset(lrT[:, :], 1.0), [], ["lrT"])
        sc.add("pool", lambda h: h.memset(S[:, :], 0.0), [], ["S0", "S1", "S2"])
        sc.add("pool", lambda h: h.memset(Sb, 0.0), [], ["Sb0", "Sb1", "Sb2"])
        for i in range(2):
            act(lbt[:, :], lbl[i][:, :], AF.Exp, ["lbl%d" % i], ["lbt"])
            tt("dve", lbv[i][:, :], lbt[:, 0:8], lbt[:, 8:16], ALU.add, ["lbt"], ["lbv%d" % i])
            sc.add("dve", lambda h, i=i: h.reciprocal(out=lbv[i][:, :], in_=lbv[i][:, :]),
                   ["lbv%d" % i], ["lbv%d" % i])
            tt("dve", omlv[i][:, :], lbt[:, 8:16], lbv[i][:, :], ALU.mult, ["lbt", "lbv%d" % i], ["omlv%d" % i])
            tt("dve", lbv[i][:, :], lbt[:, 0:8], lbv[i][:, :], ALU.mult, ["lbt", "lbv%d" % i], ["lbv%d" % i])

        W1P = [(0, 1040, "W1a"), (1040, 3088, "W1b"), (3088, 5136, "W1c")]
        W2P = [(0, 1040, "W2a"), (1040, 3088, "W2b")]
        W3P = [(0, 2048, "W3")]
        ALLW = ["W1a", "W1b", "W1c", "W2a", "W2b", "W3", "WBlo"]

        def wname(W, col):
            ps = W1P if W is W1 else (W2P if W is W2 else W3P)
            for c0, c1, nm in ps:
                if c0 <= col < c1:
                    return nm
            raise AssertionError

        def load_w(dst3, src, pieces, tag, kcs=None, war_all=True, nowaw=False):
            first = True
            for (p0, p1, nm) in pieces:
                c0 = p0
                pfirst = True
                while c0 < p1:
                    c1 = min(p1, c0 + 2048)
                    for kc in (kcs if kcs is not None else range(dst3.shape[1])):
                        wr = [nm] + ([x for x in ALLW if x != nm] if (first and war_all) else [])
                        dma("pool", dst3[:, kc, c0:c1], src[:, kc, c0:c1], [], wr, "w" + tag + nm,
                            skip_waw=(not first) or nowaw)
                        first = False
                        pfirst = False
                    c0 = c1

        load_w(W1, w1_d, W1P, "1", war_all=False)

        flags = {}

        def need(k):
            while len(free_banks) < k:
                yield ("nobank",)

        def run_rr(gens, prio=0):
            gens = list(gens)
            bs = gens[3:6] if prio and len(gens) >= 6 else []
            order = bs + gens
            pend = {}
            alive = set(id(g) for g in gens)
            while alive:
                progressed = False
                for g in order:
                    if id(g) not in alive:
                        continue
                    k = pend.get(id(g))
                    if k is not None:
                        if not flags.get(k):
                            continue
                        pend[id(g)] = None
                    try:
                        r = next(g)
                    except StopIteration:
                        alive.discard(id(g))
                        progressed = True
                        continue
                    if isinstance(r, tuple) and r[0] == "wait":
                        pend[id(g)] = r[1]
                        progressed = True
                    elif isinstance(r, tuple) and r[0] == "nobank":
                        pass
                    else:
                        progressed = True
                assert progressed, "emission deadlock"

        def dma_x(t, xf, xfn):
            dma("sp", xf, x_d[t * P:(t + 1) * P, :], [], [xfn], "x" + xfn)

        def gen_load_x(t, b, xf=None, xfn=None, do_dma=True, ceng="act"):
            if xf is None:
                xf, xfn = x_f[b][:, :], "x_f%d" % b
            if do_dma:
                dma_x(t, xf, xfn)
            cp(ceng, x_bf[b][:, :], xf, [xfn], ["x_bf%d" % b])
            yield from need(1)
            bk, bn = nb()
            bkb = bk[:, :].bitcast(BF16)
            for kc in range(8):
                tr(bkb[:, kc * P:(kc + 1) * P], x_bf[b][:, kc * P:(kc + 1) * P], ["x_bf%d" % b], [bn], inc=(kc == 7))
            cp("act", xT[b][:, :], bkb[:, :], [bn], ["xT%d" % b])
            fb(bn)
            yield

        def proj_fm(W, col0, nheads, b, m=P):
            bk, bn = nb()
            for h in range(nheads):
                for kc in range(8):
                    mm(bk[0:m, h * P:(h + 1) * P], W[:, kc, col0 + h * m:col0 + (h + 1) * m],
                       xT[b][:, kc * P:(kc + 1) * P], kc == 0, kc == 7, ["xT%d" % b, wname(W, col0)], [bn],
                       inc=(h == nheads - 1 and kc == 7))
            return bk, bn

        def proj_tm(W, col0, b):
            bk, bn = nb()
            for kc in range(8):
                mm(bk[:, :], xT[b][:, kc * P:(kc + 1) * P], W[:, kc, col0:col0 + 512], kc == 0, kc == 7,
                   ["xT%d" % b, wname(W, col0)], [bn], inc=(kc == 7))
            return bk, bn

        def decay_chain(sw, u, par, Lbuf, Lname, scale, other, oname):
            B, Bn = TS[u][1], TN[u][1]
            Dd, Dn = TS[u][3], TN[u][3]
            el, eln = elast[u][par], "elast%d_%d" % (u, par)
            if sw == 1:
                sc.add("dve", lambda h: h.tensor_tensor_scan(out=B, data0=smask, data1=Lbuf,
                                                             initial=0.0, op0=ALU.mult, op1=ALU.add),
                       [Lname, "cst_f"], [Bn])
                tot = h3(B)[:, :, 127]
            else:
                sc.add("dve", lambda h: h.tensor_tensor_scan(out=B[:, ::-1], data0=smask, data1=Lbuf[:, ::-1],
                                                             initial=0.0, op0=ALU.mult, op1=ALU.add),
                       [Lname, "cst_f"], [Bn])
                tot = h3(B)[:, :, 0]
            yield
            act(el[:, :], tot, AF.Exp, [Bn], [eln], scale=-scale)
            act(Dd, B, AF.Exp, [Bn], [Dn], scale=-scale)
            yield
            if sw == 1:
                act(B, B, AF.Exp, [Bn, eln], [Bn], scale=scale)
                yield
                el_bc = el[:, :].unsqueeze(2).broadcast_to([P, 4, P])
                tt("dve", h3(other), h3(B), el_bc, ALU.mult, [Bn, eln], [oname])
                yield
            else:
                act(nbl[u][:, :], tot, AF.Identity, [Bn], ["nbl%d" % u], scale=-scale)
                for h in range(4):
                    sc.add("act", lambda hh, h=h: hh.activation(
                        out=other[:, h * P:(h + 1) * P], in_=B[:, h * P:(h + 1) * P], func=AF.Exp, scale=scale,
                        bias=nbl[u][:, h:h + 1]), [Bn, "nbl%d" % u] + ([oname] if oname != Bn else []), [oname])
                yield
                act(B, B, AF.Exp, [Bn, eln, oname], [Bn], scale=scale)
                yield
            return (Dd, Dn), (B, Bn), (other, oname)

        def stage_A_gla(sw, key, i, t, b, W, lrcol):
            u = 0
            TA, TC = TS[u][0], TS[u][2]
            TAn, TCn = TN[u][0], TN[u][2]
            yield from need(1)
            bk, bn = nb()
            for kc in range(8):
                mm(bk[0:16, 0:P], W[:, kc, lrcol:lrcol + 16], xT[b][:, kc * P:(kc + 1) * P], kc == 0, kc == 7,
                   ["xT%d" % b, wname(W, lrcol)], [bn], inc=(kc == 7))
            cp("act", lrT[0:16, :], bk[0:16, 0:P], [bn], ["lrT"])
            fb(bn)
            yield
            yield from need(1)
            bz, bzn = nb()
            for h in range(4):
                mm(bz[:, h * P:(h + 1) * P], gkb[sw - 1][:, h * P:(h + 1) * P], lrT[:, :], True, True,
                   ["lrT", "gkb%d" % (sw - 1)], [bzn], inc=(h == 3))
            act(TA, bz[:, :], AF.Exp, [bzn], [TAn], scale=-1.0)
            fb(bzn)
            yield
            act(TA, TA, AF.Ln, [TAn], [TAn], scale=1.0, bias=1.0)
            yield
            (eq, eqn), (ek, ekn), (ekh, ekhn) = yield from decay_chain(sw, u, i % 2, TA, TAn, 1.0 / 16.0, TC, TCn)
            if i >= 1:
                yield ("wait", (key, "B", i - 1, u))
            if sw == 1:
                yield from need(1)
                bq, bqn = proj_fm(W, 4112, 4, b)
                cp("act", qT_raw, bq[:, :], [bqn], ["qT_raw"])
                fb(bqn)
                yield
                yield from need(1)
                bkk, bkn = proj_fm(W, 4624, 4, b)
                cp("act", kT_raw, bkk[:, :], [bkn], ["kT_raw"])
                fb(bkn)
                dma("sp", qT_s.ap()[t], qT_raw, ["qT_raw"], ["qT_s%d" % t], "sq")
                dma("sp", kT_s.ap()[t], kT_raw, ["kT_raw"], ["kT_s%d" % t], "sk")
                yield
            else:
                dma("sp", qT_raw, qT_s.ap()[t], ["qT_s%d" % t], ["qT_raw"], "lq")
                dma("sp", kT_raw, kT_s.ap()[t], ["kT_s%d" % t], ["kT_raw"], "lk")
            tt("dve", QT[u], qT_raw, eq, ALU.mult, ["qT_raw", eqn], ["QT%d" % u])
            yield
            tt("dve", KT[u], kT_raw, ek, ALU.mult, ["kT_raw", ekn], ["KT%d" % u])
            yield
            tt("dve", KH[u], kT_raw, ekh, ALU.mult, ["kT_raw", ekhn], ["KH%d" % u])
            yield

        def stage_A_hgrn(sw, key, i, t, b, u, W, hfcol):
            hh = u - 1
            TA, TB, TC, TD = TS[u]
            TAn, TBn, TCn, TDn = TN[u]
            yield from need(1)
            bf, bfn = proj_fm(W, hfcol + hh * 512, 4, b)
            lb_bc = lbv[sw - 1][:, hh * 4:(hh + 1) * 4].unsqueeze(2).broadcast_to([P, 4, P])
            oml_bc = omlv[sw - 1][:, hh * 4:(hh + 1) * 4].unsqueeze(2).broadcast_to([P, 4, P])
            lbn, omn = "lbv%d" % (sw - 1), "omlv%d" % (sw - 1)
            act(TA, bf[:, :], AF.Exp, [bfn], [TAn], scale=-1.0)
            yield
            act(TB, TA, AF.Ln, [TAn], [TBn], scale=1.0, bias=1.0)
            tt("pool", h3(TC), h3(TA), lb_bc, ALU.mult, [TAn, lbn], [TCn])
            yield
            tt("dve", TA, bf[:, :], TB, ALU.add, [bfn, TBn], [TAn])
            fb(bfn)
            act(TC, TC, AF.Ln, [TCn], [TCn], scale=1.0, bias=1.0)
            yield
            act(TA, TA, AF.Exp, [TAn], [TAn], scale=-1.0)
            tt("pool", TC, TB, TC, ALU.subtract, [TBn, TCn], [TCn])
            yield
            tt("pool", h3(TA), h3(TA), oml_bc, ALU.mult, [TAn, omn], [TAn])
            (eq, eqn), (ek, ekn), (ekh, ekhn) = yield from decay_chain(sw, u, i % 2, TC, TCn, 1.0, TC, TCn)
            if i >= 1:
                yield ("wait", (key, "B", i - 1, u))
            yield ("wait", (key, "hq", i))
            hqn = "hq_raw%d_%d" % (b, hh)
            tt("dve", QT[u], hq_raw[b][:, hh * 512:(hh + 1) * 512], eq, ALU.mult, [hqn, eqn], ["QT%d" % u])
            yield
            tt("dve", KT[u], TA, ek, ALU.mult, [TAn, ekn], ["KT%d" % u])
            yield
            tt("dve", KH[u], TA, ekh, ALU.mult, [TAn, ekhn], ["KH%d" % u])
            yield

        def stage_B(sw, key, i, t, b, u):
            dv = 256 if u == 0 else 128
            s0 = 0 if u == 0 else 1024 + (u - 1) * 512
            if u == 0:
                V, vname = v_bf[b][:, :], "v_bf%d" % b
            else:
                V, vname = hi_bf[b][:, (u - 1) * 512:u * 512], "hi_bf%d" % b
            Sn, Sbn = "S%d" % u, "Sb%d" % u
            qt, kt, kh, kht, sct, el = QT[u], KT[u], KH[u], KHt[u], scT[u], elast[u][i % 2]
            qn, kn, khn, khtn, scn, eln = ("QT%d" % u, "KT%d" % u, "KH%d" % u, "KHt%d" % u, "scT%d" % u,
                                           "elast%d_%d" % (u, i % 2))
            yield from need(2)
            bk, bn = nb()
            for h in range(4):
                mm(bk[:, h * P:(h + 1) * P], kt[:, h * P:(h + 1) * P], qt[:, h * P:(h + 1) * P], True, True,
                   [kn, qn], [bn], inc=(h == 3))
            bk2, bn2 = nb()
            bkb = bk2[:, :].bitcast(BF16)
            for h in range(4):
                tr(bkb[:, h * P:(h + 1) * P], kh[:, h * P:(h + 1) * P], [khn], [bn2], inc=(h == 3))
            tri_bc = tri[sw - 1].unsqueeze(1).broadcast_to([P, 4, P])
            tt("dve", h3(sct), h3(bk[:, :]), tri_bc, ALU.mult, [bn, "cst_b"], [scn])
            cp("act", kht, bkb[:, 0:512], [bn2], [khtn])
            fb(bn, bn2)
            yield
            nbk = 2 if u == 0 else 1
            yield from need(nbk)
            obanks = [nb() for _ in range(nbk)]
            for h in range(4):
                ob, obn = obanks[(h * dv) // 512]
                oc = (h * dv) % 512
                mm(ob[:, oc:oc + dv], sct[:, h * P:(h + 1) * P], V[:, h * dv:(h + 1) * dv], True, False,
                   [scn, vname], [obn], inc=False)
                mm(ob[:, oc:oc + dv], qt[:, h * P:(h + 1) * P], Sb[:, s0 + h * dv:s0 + (h + 1) * dv], False, sw == 1,
                   [qn, Sbn], [obn], inc=(sw == 1))
                if sw == 2:
                    mm(ob[:, oc:oc + dv], ident, o1b[b][:, s0 + h * dv:s0 + (h + 1) * dv], False, True,
                       ["cst_b", "o1b%d_%d" % (b, u)], [obn], inc=True)
            flags[(key, "B", i, u)] = True
            yield
            ucols = slice(s0, s0 + 4 * dv)
            for j, (ob, obn) in enumerate(obanks):
                cols = slice(s0 + j * 512, s0 + (j + 1) * 512)
                if sw == 1:
                    cp("act", o1b[b][:, cols], ob[:, :], [obn], ["o1b%d_%d" % (b, u)])
                else:
                    cp("act", otot[:, cols], ob[:, :], [obn], ["otot%d" % u])
                fb(obn)
            if sw == 1:
                dma("sp", o1_s.ap()[t][:, ucols], o1b[b][:, ucols], ["o1b%d_%d" % (b, u)], ["o1_s%d_%d" % (t, u)],
                    "so1%d_%d" % (b, u))
            yield
            yield from need(nbk)
            pbanks = [nb() for _ in range(nbk)]
            for h in range(4):
                pb, pbn = pbanks[(h * dv) // 512]
                pc = (h * dv) % 512
                mm(pb[:, pc:pc + dv], kht[:, h * P:(h + 1) * P], V[:, h * dv:(h + 1) * dv], True, True,
                   [khtn, vname], [pbn], inc=True)
            yield
            for h in range(4):
                pb, pbn = pbanks[(h * dv) // 512]
                pc = (h * dv) % 512
                scol = slice(s0 + h * dv, s0 + (h + 1) * dv)
                sc.add("dve", lambda hh, scol=scol, pb=pb, pc=pc, h=h, el=el: hh.scalar_tensor_tensor(
                    out=S[:, scol], in0=S[:, scol], scalar=el[:, h:h + 1], in1=pb[:, pc:pc + dv],
                    op0=ALU.mult, op1=ALU.add), [Sn, eln, pbn], [Sn])
                if h % 2 == 1:
                    yield
            fb(*[pbn for (_, pbn) in pbanks])
            cp("pool", Sb[:, s0:s0 + 4 * dv], S[:, s0:s0 + 4 * dv], [Sn], [Sbn])
            yield
            if sw == 2:
                on = "otot%d" % u
                act(sq[:, 0:4 * dv], otot[:, ucols], AF.Square, [on], ["sq"])
                sc.add("dve", lambda h: h.tensor_reduce(
                    out=ss[u][:, :], in_=sq[:, 0:4 * dv].rearrange("p (h e) -> p h e", e=dv),
                    axis=mybir.AxisListType.X, op=ALU.add), ["sq"], ["ss%d" % u])
                yield
                act(rstd[u][:, :], ss[u][:, :], AF.Ln, ["ss%d" % u], ["rstd%d" % u], scale=1.0 / dv, bias=RMS_EPS)
                act(rstd[u][:, :], rstd[u][:, :], AF.Exp, ["rstd%d" % u], ["rstd%d" % u], scale=-0.5)
                yield
                r_bc = rstd[u][:, :].unsqueeze(2).broadcast_to([P, 4, dv])
                o3 = otot[:, ucols].rearrange("p (h e) -> p h e", e=dv)
                g3 = og[b][:, ucols].rearrange("p (h e) -> p h e", e=dv)
                for h in range(4):
                    sc.add("act", lambda hh, h=h: hh.activation(
                        out=og[b][:, s0 + h * dv:s0 + (h + 1) * dv], in_=otot[:, s0 + h * dv:s0 + (h + 1) * dv],
                        func=AF.Identity, scale=rstd[u][:, h:h + 1]), [on, "rstd%d" % u], ["og%d_%d" % (b, u)])
                dma("sp", og_s.ap()[t][:, ucols], og[b][:, ucols], ["og%d_%d" % (b, u)], ["og_s%d_%d" % (t, u)],
                    "sog%d_%d" % (b, u))
                yield

        def run_sweep(sw, key, order, pre_T, gen_Tv, after_proj, W, lrcol, hfcol):
            n = len(order)
            xs = lambda j: (x_f[j % 3][:, :], "x_f%d" % (j % 3))
            dma_x(order[0], *xs(0))
            if n > 1:
                dma_x(order[1], *xs(1))
            ceng = "dve" if sw == 1 else "act"
            run_rr([gen_load_x(order[0], 0, *xs(0), do_dma=False, ceng=ceng)])
            for i in range(n + 1):
                gens = []
                if i + 2 < n:
                    dma_x(order[i + 2], *xs(i + 2))
                if i < n:
                    t, b = order[i], i % 2
                    pre_T(t, b, i)
                    gens += [stage_A_gla(sw, key, i, t, b, W, lrcol),
                             stage_A_hgrn(sw, key, i, t, b, 1, W, hfcol),
                             stage_A_hgrn(sw, key, i, t, b, 2, W, hfcol)]
                if i >= 1:
                    pt, pb = order[i - 1], (i - 1) % 2
                    gens += [stage_B(sw, key, i - 1, pt, pb, 0), stage_B(sw, key, i - 1, pt, pb, 1),
                             stage_B(sw, key, i - 1, pt, pb, 2)]
                if i == n:
                    after_proj()
                if i < n:
                    gens.append(gen_Tv(t, b, i))
                if i + 1 < n:
                    gens.append(gen_load_x(order[i + 1], (i + 1) % 2, *xs(i + 1), do_dma=False, ceng=ceng))
                run_rr(gens, prio=PRIO if i < n else 0)

        def pre_T1(t, b, i):
            pass

        def gen_Tv1(t, b, i):
            for j in range(2):
                yield from need(1)
                bv, bvn = proj_tm(W1, 1040 + j * 512, b)
                act(v_bf[b][:, j * 512:(j + 1) * 512], bv[:, :], AF.Identity, [bvn], ["v_bf%d" % b], scale=Q_SCALE)
                fb(bvn)
                yield
            for j in range(2):
                yield from need(1)
                bv, bvn = proj_tm(W1, 2064 + j * 512, b)
                act(hi_bf[b][:, j * 512:(j + 1) * 512], bv[:, :], AF.Identity, [bvn], ["hi_bf%d" % b], scale=Q_SCALE)
                fb(bvn)
                yield
            yield from need(2)
            hb = [proj_fm(W1, 3088 + hh * 512, 4, b) for hh in range(2)]
            for hh in range(2):
                act(hq_raw[b][:, hh * 512:(hh + 1) * 512], hb[hh][0][:, :], AF.Silu, [hb[hh][1]],
                    ["hq_raw%d_%d" % (b, hh)])
                fb(hb[hh][1])
            flags[("s1", "hq", i)] = True
            dma("sp", v_s.ap()[t], v_bf[b][:, :], ["v_bf%d" % b], ["v_s%d" % t], "sv%d" % b)
            dma("sp", hi_s.ap()[t], hi_bf[b][:, :], ["hi_bf%d" % b], ["hi_s%d" % t], "shi%d" % b)
            dma("sp", hq_s.ap()[t], hq_raw[b][:, :], ["hq_raw%d_0" % b, "hq_raw%d_1" % b], ["hq_s%d" % t],
                "shq%d" % b)
            yield

        def after1():
            load_w(W2, w2_d, W2P, "2")

        if stop is None or stop >= 1:
            run_sweep(1, "s1", list(range(NT)), pre_T1, gen_Tv1, after1, W1, 0, 16)

        def exchange():
            dma("sp", cc_in.ap(), S[:, :], ["S0", "S1", "S2"], ["cc_in"], "xs")
            sc.add("pool", lambda h: h.collective_compute(
                "AllGather", ALU.bypass, replica_groups=[[0, 1], [2, 3], [4, 5], [6, 7]],
                ins=[cc_in.ap().opt()], outs=[cc_out.ap().opt()]), ["cc_in"], ["cc_out"], chan="cc", cc=True)
            dma("sp", otot[:, :], cc_out.ap()[0:P, :], ["cc_out"], ["otot0", "otot1", "otot2"], "xl0")
            sc.add("dve", lambda h: h.tensor_scalar(out=S[:, :], in0=otot[:, :], scalar1=sel[:, 0:1], scalar2=None,
                                                    op0=ALU.mult), ["otot0", "otot1", "otot2", "sel"], ["S0", "S1", "S2"])
            dma("sp", otot[:, :], cc_out.ap()[P:2 * P, :], ["cc_out"], ["otot0", "otot1", "otot2"], "xl1")
            sc.add("dve", lambda h: h.scalar_tensor_tensor(out=S[:, :], in0=otot[:, :], scalar=sel[:, 1:2],
                                                           in1=S[:, :], op0=ALU.mult, op1=ALU.add),
                   ["otot0", "otot1", "otot2", "sel", "S0", "S1", "S2"], ["S0", "S1", "S2"])
            cp("act", Sb, S[:, :], ["S0", "S1", "S2"], ["Sb0", "Sb1", "Sb2"])

        def pre_T2(t, b, i):
            dma("sp", v_bf[b][:, :], v_s.ap()[t], ["v_s%d" % t], ["v_bf%d" % b], "lv%d" % b)
            dma("sp", hi_bf[b][:, :], hi_s.ap()[t], ["hi_s%d" % t], ["hi_bf%d" % b], "lhi%d" % b)
            dma("sp", hq_raw[b][:, :], hq_s.ap()[t], ["hq_s%d" % t], ["hq_raw%d_0" % b, "hq_raw%d_1" % b],
                "lhq%d" % b)
            dma("sp", o1b[b][:, :], o1_s.ap()[t], ["o1_s%d_%d" % (t, u) for u in range(3)],
                ["o1b%d_%d" % (b, u) for u in range(3)], "lo1%d" % b)
            flags[("s2", "hq", i)] = True
            if i == 1:
                load_w(WB, wb_d, [(0, D, "WBhi")], "bh", kcs=range(8, 24), war_all=False)

        def gen_Tv2(t, b, i):
            for j in range(4):
                yield from need(1)
                gbk, gbn = proj_tm(W2, 1040 + j * 512, b)
                cp("act", sg[b][:, j * 512:(j + 1) * 512], gbk[:, :], [gbn], ["sg%d" % b])
                fb(gbn)
                yield
            dma("sp", g_s.ap()[t], sg[b][:, :], ["sg%d" % b], ["g_s%d" % t], "sg_s%d" % b)
            yield

        def after2():
            load_w(W3, w3_d, W3P, "3")
            load_w(WB, wb_d, [(0, D, "WBlo")], "b", kcs=range(8), war_all=False, nowaw=True)

        def gen_P1(t, b):
            yield from need(4)
            gb = [proj_tm(W3, j * 512, b) for j in range(4)]
            for j in range(4):
                act(sigm[b][:, j * 512:(j + 1) * 512], gb[j][0][:, :], AF.Sigmoid, [gb[j][1]], ["sigm%d" % b])
                fb(gb[j][1])
            act(sgt, sg[b][:, :], AF.Sigmoid, ["sg%d" % b], ["sgt"])
            yield
            tt("dve", sg[b][:, :], sg[b][:, :], sgt, ALU.mult, ["sg%d" % b, "sgt"], ["sg%d" % b])
            tt("dve", og[b][:, :], og[b][:, :], sg[b][:, :], ALU.mult, ["og%d" % b, "sg%d" % b], ["og%d" % b])
            yield
            for half in range(2):
                yield from need(1)
                bk, bn = nb()
                bkb = bk[:, :].bitcast(BF16)
                for c in range(8):
                    cc = half * 8 + c
                    tr(bkb[:, c * P:(c + 1) * P], og[b][:, cc * P:(cc + 1) * P], ["og%d" % b], [bn], inc=(c == 7))
                cp("act" if half == 0 else "dve", ogT[b][:, half * 1024:(half + 1) * 1024], bkb[:, :], [bn],
                   ["ogT%d_%d" % (b, half)])
                fb(bn)
                yield

        x_f3 = [x_f[0][:, :], x_f[1][:, :], x_f[2][:, :], sq[:, :]]
        x_f3n = ["x_f0", "x_f1", "x_f2", "sq"]
        y_bf2 = [y_bf, T4s[2][:, :].bitcast(BF16)[:, 0:1024]]
        sgt = T4s[2][:, :].bitcast(BF16)[:, 1024:3072]

        def gen_P2a(t, b):
            yb = y_bf2[b]
            for j in range(2):
                yield from need(2)
                bg, bgn = nb()
                for kc in range(8):
                    mm(bg[:, :], ogT[b][:, kc * P:(kc + 1) * P], WB[:, kc, j * 512:(j + 1) * 512], kc == 0, kc == 7,
                       ["ogT%d_0" % b, "WBlo"], [bgn], inc=(kc == 7))
                bh, bhn = nb()
                for kc in range(8):
                    mm(bh[:, :], ogT[b][:, (8 + kc) * P:(9 + kc) * P], WB[:, 8 + kc, j * 512:(j + 1) * 512],
                       kc == 0, kc == 7, ["ogT%d_1" % b, "WBhi"], [bhn], inc=(kc == 7))
                c0 = slice(j * 512, (j + 1) * 512)
                c1 = slice(1024 + j * 512, 1024 + (j + 1) * 512)
                tt("dve", S[:, c0], bg[:, :], sigm[b][:, c0], ALU.mult, [bgn, "sigm%d" % b], ["t1_%d" % j])
                tt("dve", S[:, c1], bh[:, :], sigm[b][:, c1], ALU.mult, [bhn, "sigm%d" % b], ["t2_%d" % j])
                fb(bgn, bhn)
                yield
                tt("pool" if j == 0 else "dve", yb[:, c0], S[:, c0], S[:, c1], ALU.add, ["t1_%d" % j, "t2_%d" % j],
                   ["y_bf%d_%d" % (b, j)])
                yield

        def gen_P2b(t, b):
            yb = y_bf2[b]
            xf, xfn = x_f3[t % 4], x_f3n[t % 4]
            yield from need(1)
            bk, bn = nb()
            bkb = bk[:, :].bitcast(BF16)
            for c in range(8):
                tr(bkb[:, c * P:(c + 1) * P], yb[:, c * P:(c + 1) * P], ["y_bf%d_%d" % (b, c // 4)], [bn], inc=(c == 7))
            cp("act", yT, bkb[:, :], [bn], ["yT"])
            fb(bn)
            yield
            for j in range(2):
                yield from need(1)
                bo, bon = nb()
                for kc in range(8):
                    mm(bo[:, :], yT[:, kc * P:(kc + 1) * P], WB[:, 16 + kc, j * 512:(j + 1) * 512],
                       kc == 0, kc == 7, ["yT", "WBhi"], [bon], inc=(kc == 7))
                c0 = slice(j * 512, (j + 1) * 512)
                sc.add("dve", lambda h, c0=c0, bo=bo, xf=xf: h.scalar_tensor_tensor(
                    out=otot[:, c0], in0=xf[:, c0], scalar=ALPHA, in1=bo[:, :], op0=ALU.mult, op1=ALU.add),
                    [xfn, bon], ["r%d" % j])
                sc.add("dve", lambda h, c0=c0, j=j: h.bn_stats(out=bst[:, j * 6:(j + 1) * 6], in_=otot[:, c0]),
                       ["r%d" % j], ["bst%d" % j])
                fb(bon)
                yield
            sc.add("dve", lambda h: h.bn_aggr(out=mv[:, :], in_=bst[:, :]), ["bst0", "bst1"], ["mv"])
            act(lnr[:, 0:1], mv[:, 1:2], AF.Ln, ["mv"], ["lnr"], scale=1.0, bias=LN_EPS)
            act(lnr[:, 0:1], lnr[:, 0:1], AF.Exp, ["lnr"], ["lnr"], scale=-0.5)
            sc.add("dve", lambda h: h.scalar_tensor_tensor(
                out=lnr[:, 1:2], in0=mv[:, 0:1], scalar=-1.0, in1=lnr[:, 0:1], op0=ALU.mult, op1=ALU.mult),
                ["mv", "lnr"], ["lnr"])
            yield
            sc.add("act", lambda h: h.activation(out=otot[:, 0:1024], in_=otot[:, 0:1024], func=AF.Identity,
                                                 scale=lnr[:, 0:1], bias=lnr[:, 1:2]),
                   ["r0", "r1", "lnr"], ["r0", "r1"])
            yield
            tt("pool", otot[:, 0:1024], otot[:, 0:1024], lng, ALU.mult, ["r0", "r1", "lng"], ["r0", "r1"])
            yield
            tt("dve", ores[b], otot[:, 0:1024], lnb, ALU.add, ["r0", "r1", "lnb"], ["ores%d" % b])
            dma("sp", out_d[t * P:(t + 1) * P, :], ores[b], ["ores%d" % b], [], "out%d" % b)
            yield

        def sweep3():
            sc.barrier()
            dma("sp", lng, lng_d, [], ["lng"], "c4")
            dma("sp", lnb, lnb_d, [], ["lnb"], "c5")
            for kc in range(16):
                gcol = (kc % 2) if kc < 8 else 2
                wr = "WBlo" if kc < 8 else "WBhi"
                sc.add("dve", lambda h, kc=kc, gcol=gcol: h.tensor_scalar(
                    out=WB[:, kc, :], in0=WB[:, kc, :], scalar1=gn[:, gcol:gcol + 1], scalar2=None, op0=ALU.mult),
                    [wr, "gn"], [wr])
            def p1_loads(t, b):
                dma_x(t, x_f3[t % 4], x_f3n[t % 4])
                dma("sp", og[b][:, :], og_s.ap()[t], ["og_s%d_%d" % (t, u) for u in range(3)], ["og%d" % b],
                    "log%d" % b)
                dma("sp", sg[b][:, :], g_s.ap()[t], ["g_s%d" % t], ["sg%d" % b], "lg%d" % b)

            p1_loads(0, 0)
            for k in range(NT + 2):
                gens = []
                if k + 1 < NT:
                    p1_loads(k + 1, (k + 1) % 2)
                if k - 2 >= 0:
                    gens.append(gen_P2b(k - 2, (k - 2) % 2))
                if 0 <= k - 1 < NT:
                    gens.append(gen_P2a(k - 1, (k - 1) % 2))
                if k < NT:
                    gens.append(seq(gen_load_x(k, k % 2, x_f3[k % 4], x_f3n[k % 4], do_dma=False), gen_P1(k, k % 2)))
                run_rr(gens)

        def seq(*gens):
            for g in gens:
                yield from g

        if stop is None or stop >= 2:
            exchange()
            run_sweep(2, "s2", list(range(NT - 1, -1, -1)), pre_T2, gen_Tv2, after2, W2, 0, 16)
        if stop is None or stop >= 3:
            sweep3()

        block = es.enter_context(nc.Block())
        sc.emit(block)
    return nc


_COLS = dict(a_q=0, a_k=512, a_v=1024, a_gate=2048, lr_f=3072, lr_b=3088, h_q=3104, h_f_f=4128,
             h_f_b=5152, h_i=6176, h_gate=7200, m_gla=8224, m_hgrn=9248)


def _tile_rows(w):
    k = w.shape[0] // P
    return np.ascontiguousarray(w.reshape(k, P, w.shape[1]).transpose(1, 0, 2))


def make_in_maps(x, w_in, gla_gk_up_f, gla_gk_bias_f, gla_gk_up_b, gla_gk_bias_b, gla_norm_g,
                 hgrn_lb_logits_f, hgrn_lb_logits_b, hgrn_norm_g, w_branch_gla, w_branch_hgrn,
                 w_out, ln_g, ln_b):
    f32 = np.float32
    x = np.asarray(x, f32)
    win = np.asarray(w_in, f32)[0]
    c = _COLS

    def cols(name, n):
        return win[:, c[name]:c[name] + n]

    dirs = {
        "f": dict(lr=cols("lr_f", 16), hf=cols("h_f_f", 1024),
                  gk=np.concatenate([np.asarray(gla_gk_up_f, f32)[0], np.asarray(gla_gk_bias_f, f32)[0][None]], 0),
                  lbl=np.asarray(hgrn_lb_logits_f, f32)),
        "b": dict(lr=cols("lr_b", 16), hf=cols("h_f_b", 1024),
                  gk=np.concatenate([np.asarray(gla_gk_up_b, f32)[0], np.asarray(gla_gk_bias_b, f32)[0][None]], 0),
                  lbl=np.asarray(hgrn_lb_logits_b, f32)),
    }

    def lbl_layout(l):
        return np.ascontiguousarray(l.reshape(2, 8, P).transpose(2, 0, 1).reshape(P, 16))

    w3 = _tile_rows(np.concatenate([cols("m_gla", 1024), cols("m_hgrn", 1024)], 1))
    wb = _tile_rows(np.concatenate([np.asarray(w_branch_gla, f32)[0], np.asarray(w_branch_hgrn, f32)[0],
                                    np.asarray(w_out, f32)[0]], 0))
    gng = np.asarray(gla_norm_g, f32)[0]
    gn = np.stack([gng[0:128], gng[128:256], np.asarray(hgrn_norm_g, f32)[0]], 1)
    lng = np.ascontiguousarray(np.broadcast_to(np.asarray(ln_g, f32)[0][None], (P, D)))
    lnb = np.ascontiguousarray(np.broadcast_to(np.asarray(ln_b, f32)[0][None], (P, D)))
    ii = np.arange(P)
    smask = np.ones((P, 512), f32)
    smask[:, 0::128] = 0.0
    cst = np.concatenate([np.eye(P, dtype=f32), (ii[:, None] <= ii[None, :]).astype(f32),
                          (ii[:, None] >= ii[None, :]).astype(f32), smask], 1)

    per_dir = {}
    for d1, d2 in (("f", "b"), ("b", "f")):
        w1 = _tile_rows(np.concatenate([dirs[d1]["lr"], dirs[d1]["hf"], cols("a_v", 1024), cols("h_i", 1024),
                                        cols("h_q", 1024), cols("a_q", 512), cols("a_k", 512)], 1))
        w2 = _tile_rows(np.concatenate([dirs[d2]["lr"], dirs[d2]["hf"], cols("a_gate", 1024),
                                        cols("h_gate", 1024)], 1))
        per_dir[d1] = dict(w1=w1, w2=w2, gk1=dirs[d1]["gk"], gk2=dirs[d2]["gk"],
                           lbl1=lbl_layout(dirs[d1]["lbl"]), lbl2=lbl_layout(dirs[d2]["lbl"]))

    in_maps = []
    for core in range(NCORES):
        b, half = core // 2, core % 2
        TL = x.shape[1] // 2
        xl = x[b, half * TL:(half + 1) * TL]
        if half == 1:
            xl = xl[::-1]
        pd = per_dir["f" if half == 0 else "b"]
        sel = np.zeros((P, 2), f32)
        sel[:, 1 - half] = 1.0
        in_maps.append(dict(x=np.ascontiguousarray(xl), w1=pd["w1"], w2=pd["w2"], w3=w3, wb=wb,
                            gk1=pd["gk1"], gk2=pd["gk2"], lbl1=pd["lbl1"], lbl2=pd["lbl2"], gn=gn,
                            lng=lng, lnb=lnb, sel=sel, cst=cst))

    return in_maps


def assemble(outs, T):
    out = np.empty((4, T, D), np.float32)
    TL = T // 2
    for core in range(NCORES):
        b, half = core // 2, core % 2
        o = np.asarray(outs[core], np.float32)
        if half == 1:
            o = o[::-1]
        out[b, half * TL:(half + 1) * TL] = o
    return out


def kernel(**inputs):
    in_maps = make_in_maps(**inputs)
    nc = build_program()
    res = run_bass_kernel_spmd(nc, in_maps, core_ids=list(range(NCORES)))
    return assemble([res.results[c]["out"] for c in range(NCORES)], 4096)
```

```python
import math
from contextlib import ExitStack

import numpy as np
import concourse.bass as bass
import concourse.mybir as mybir
from concourse.bass_utils import run_bass_kernel_spmd

F32 = mybir.dt.float32
BF16 = mybir.dt.bfloat16
AF = mybir.ActivationFunctionType
ALU = mybir.AluOpType

P = 128
D = 1024
NT = 16
NCORES = 8
W1C, W2C, W3C = 5136, 3088, 2048
WFLAT = 8 * W1C
Q_SCALE = 128.0 ** -0.5
LN_QS = math.log(Q_SCALE)
ALPHA = 2.0 ** 0.25
RMS_EPS = 1e-6
LN_EPS = 1e-5
SAME_ENGINE_WAITS = True
PRIO = 3
RR_ORDER = [0, 1, 2, 3, 4, 5, 6, 7, 3]
RR_ORDER2 = RR_ORDER


class Sched:
    def __init__(self, nc, es):
        self.nc = nc
        self.es = es
        self.ops = []
        self.last_w = {}
        self.readers = {}
        self.eng_sem = {}
        self.eng_cnt = {}
        self.chan = {}
        self.pending = {}
        self.last_access = {}
        self.bar_deps = []
        for e in ("pe", "act", "dve", "pool"):
            self.eng_sem[e] = es.enter_context(nc.semaphore("s_" + e))
            self.eng_cnt[e] = 0

    def _deps(self, reads, writes, skip_waw=False):
        deps = []
        for r in reads:
            if r in self.last_w:
                deps.append((self.last_w[r], "raw"))
        for w in writes:
            if w in self.last_w and not skip_waw:
                deps.append((self.last_w[w], "waw"))
            for rd in self.readers.get(w, ()):
                deps.append((rd, "war"))
        return deps

    def add(self, eng, fn, reads=(), writes=(), chan=None, inc=True, cc=False, skip_waw=False):
        idx = len(self.ops)
        deps = self._deps(reads, writes, skip_waw) + list(self.bar_deps)
        for r in list(reads) + list(writes):
            if r.startswith("ps"):
                la = self.last_access.get(r)
                if la is not None and self.ops[la]["eng"] != eng:
                    deps.append((la, "psx"))
                self.last_access[r] = idx
        op = dict(eng=eng, fn=fn, deps=deps, done=None, kind="c", own=(chan is not None or inc))
        if chan is not None:
            if chan not in self.chan:
                self.chan[chan] = [self.es.enter_context(self.nc.semaphore("d_" + chan)), 0]
            c = self.chan[chan]
            step = 1 if cc else 16
            c[1] += step
            op["done"] = (c[0], c[1])
            op["kind"] = "cc" if cc else "dma"
        elif inc:
            self.eng_cnt[eng] += 1
            op["done"] = (self.eng_sem[eng], self.eng_cnt[eng])
            for po in self.pending.get(eng, ()):
                po["done"] = op["done"]
            self.pending[eng] = []
        else:
            self.pending.setdefault(eng, []).append(op)
        self.ops.append(op)
        for w in writes:
            self.last_w[w] = idx
            self.readers[w] = []
        for r in reads:
            self.readers.setdefault(r, []).append(idx)
        return idx

    def barrier(self):
        last = {}
        for i, o in enumerate(self.ops):
            if o["done"] is not None:
                last[(o["eng"], id(o["done"][0]))] = i
        self.bar_deps = [(i, "bar") for i in last.values()]

    def emit(self, block):
        nc = self.nc
        handles = {"pe": "tensor", "act": "scalar", "dve": "vector", "pool": "gpsimd", "sp": "sync"}
        for eng, hname in handles.items():
            my = [o for o in self.ops if o["eng"] == eng]

            def body(h, my=my, eng=eng):
                waited = {}
                for op in my:
                    needs = {}
                    for (di, kind) in op["deps"]:
                        dop = self.ops[di]
                        if dop["done"] is None:
                            continue
                        if dop["eng"] == eng and dop["kind"] == "c":
                            if not SAME_ENGINE_WAITS or eng == "pe":
                                continue
                        sem, val = dop["done"]
                        key = id(sem)
                        if key not in needs or needs[key][1] < val:
                            needs[key] = (sem, val)
                    for key, (sem, val) in needs.items():
                        if waited.get(key, 0) >= val:
                            continue
                        waited[key] = val
                        h.wait_ge(sem, val)
                    ins = op["fn"](h)
                    if op["done"] is not None and op["own"]:
                        sem, val = op["done"]
                        if op["kind"] == "dma":
                            ins.then_inc(sem, 16)
                        elif op["kind"] == "cc":
                            ins.then_inc(sem)
                        else:
                            ins.then_inc(sem, 1)
                if eng == "sp":
                    for name, (sem, val) in self.chan.items():
                        h.wait_ge(sem, val)

            getattr(block, hname)(body)


def build_program(NT=NT, stop=None):
    nc = bass.Bass("TRN2", target_bir_lowering=False)
    dt_in = lambda name, shape: nc.dram_tensor(name, shape, F32, kind="ExternalInput").ap()
    x_d = dt_in("x", [NT * P, D])
    w1_d = dt_in("w1", [P, 8, W1C])
    w2_d = dt_in("w2", [P, 8, W2C])
    w3_d = dt_in("w3", [P, 8, W3C])
    wb_d = dt_in("wb", [P, 24, D])
    gk_d = [dt_in("gk1", [17, 512]), dt_in("gk2", [17, 512])]
    lbl_d = [dt_in("lbl1", [P, 16]), dt_in("lbl2", [P, 16])]
    gn_d = dt_in("gn", [P, 3])
    lng_d = dt_in("lng", [P, D])
    lnb_d = dt_in("lnb", [P, D])
    sel_d = dt_in("sel", [P, 2])
    cst_d = dt_in("cst", [P, 3 * 128 + 512])
    out_d = nc.dram_tensor("out", [NT * P, D], F32, kind="ExternalOutput").ap()

    qT_s = nc.dram_tensor("qT_s", [NT, P, 512], BF16)
    kT_s = nc.dram_tensor("kT_s", [NT, P, 512], BF16)
    hq_s = nc.dram_tensor("hq_s", [NT, P, 1024], BF16)
    v_s = nc.dram_tensor("v_s", [NT, P, 1024], BF16)
    hi_s = nc.dram_tensor("hi_s", [NT, P, 1024], BF16)
    o1_s = nc.dram_tensor("o1_s", [NT, P, 2048], BF16)
    og_s = nc.dram_tensor("og_s", [NT, P, 2048], BF16)
    g_s = nc.dram_tensor("g_s", [NT, P, 2048], BF16)
    cc_in = nc.dram_tensor("cc_in", [P, 2048], F32)
    cc_out = nc.dram_tensor("cc_out", [2 * P, 2048], F32)

    with ExitStack() as es:
        sb = lambda name, shape, dt: es.enter_context(nc.sbuf_tensor(name, shape, dt))
        wflat = sb("wflat", [P, WFLAT], BF16)
        W1 = wflat[:, 0:8 * W1C].rearrange("p (k c) -> p k c", c=W1C)
        W2 = wflat[:, 0:8 * W2C].rearrange("p (k c) -> p k c", c=W2C)
        W3 = wflat[:, 0:8 * W3C].rearrange("p (k c) -> p k c", c=W3C)
        WB = wflat[:, WFLAT - 24 * D:WFLAT].rearrange("p (k c) -> p k c", c=D)

        cst_f = sb("cst_f", [P, 512], F32)
        cst_b = sb("cst_b", [P, 3 * 128], BF16)
        ident = cst_b[:, 0:128]
        tri = [cst_b[:, 128:256], cst_b[:, 256:384]]
        smask = cst_f[:, 0:512]
        gkb = [sb("gk1b", [17, 512], BF16), sb("gk2b", [17, 512], BF16)]
        lbl = [sb("lbl1s", [P, 16], F32), sb("lbl2s", [P, 16], F32)]
        lbv = [sb("lbv1", [P, 8], F32), sb("lbv2", [P, 8], F32)]
        omlv = [sb("omlv1", [P, 8], F32), sb("omlv2", [P, 8], F32)]
        lbt = sb("lbt", [P, 16], F32)
        gn = sb("gns", [P, 3], F32)
        sel = sb("sels", [P, 2], F32)

        x_f = [sb("x_f%d" % i, [P, D], F32) for i in range(3)]
        x_bf = [sb("x_bf%d" % i, [P, D], BF16) for i in range(2)]
        xT = [sb("xT%d" % i, [P, D], BF16) for i in range(2)]
        v_bf = [sb("v_bf%d" % i, [P, 1024], BF16) for i in range(2)]
        hi_bf = [sb("hi_bf%d" % i, [P, 1024], BF16) for i in range(2)]
        hq_raw = [sb("hq_raw%d" % i, [P, 1024], BF16) for i in range(2)]
        o1b = [sb("o1b%d" % i, [P, 2048], BF16) for i in range(2)]
        sg = [sb("sg%d" % i, [P, 2048], BF16) for i in range(2)]
        lrT = sb("lrT", [17, P], BF16)
        T4s = [sb("T4_%d" % i, [P, 2048], F32) for i in range(3)]
        TS = [[T4s[u][:, i * 512:(i + 1) * 512] for i in range(4)] for u in range(3)]
        TN = [["T%d_%d" % (u, i) for i in range(4)] for u in range(3)]
        lng, lnb = T4s[0][:, 0:1024], T4s[0][:, 1024:2048]
        U = sb("U", [P, 10752], BF16)
        QT = [U[:, (0 + u) * 512:(1 + u) * 512] for u in range(3)]
        KT = [U[:, (3 + u) * 512:(4 + u) * 512] for u in range(3)]
        KH = [U[:, (6 + u) * 512:(7 + u) * 512] for u in range(3)]
        KHt = [U[:, (9 + i) * 512:(10 + i) * 512] for i in range(3)]
        scT = [U[:, (12 + i) * 512:(13 + i) * 512] for i in range(3)]
        qT_raw = U[:, 15 * 512:16 * 512]
        kT_raw = U[:, 16 * 512:17 * 512]
        Sb = U[:, 17 * 512:21 * 512]
        T4b = T4s[1][:, :].bitcast(BF16)
        ogT = [U[:, 0:2048], T4b[:, 0:2048]]
        sigm = [U[:, 2048:4096], T4b[:, 2048:4096]]
        y_bf = U[:, 4096:5120]
        yT = U[:, 5120:6144]
        ores = [U[:, 6144 + i * 2048:6144 + (i + 1) * 2048].bitcast(F32) for i in range(2)]
        elast = [[sb("elast%d_%d" % (u, i), [P, 4], F32) for i in range(2)] for u in range(3)]
        nbl = [sb("nbl%d" % u, [P, 4], F32) for u in range(3)]
        S = sb("S", [P, 2048], F32)
        otot = sb("otot", [P, 2048], F32)
        sq = sb("sq", [P, 1024], F32)
        og = [sb("og%d" % i, [P, 2048], BF16) for i in range(2)]
        ss = [sb("ss%d" % i, [P, 4], F32) for i in range(3)]
        rstd = [sb("rstd%d" % i, [P, 4], F32) for i in range(3)]
        bst = sb("bst", [P, 12], F32)
        mv = sb("mv", [P, 2], F32)
        lnr = sb("lnr", [P, 2], F32)

        banks = [es.enter_context(nc.psum_tensor("ps%d" % i, [P, 512], F32)) for i in range(8)]
        sc = Sched(nc, es)
        free_banks = list(range(8))

        def nb():
            assert free_banks, "out of PSUM banks at emission time"
            i = free_banks.pop(0)
            return banks[i], "ps%d" % i

        def fb(*names):
            for n in names:
                i = int(n[2:])
                assert i not in free_banks
                free_banks.append(i)

        def dma(eng, out, in_, reads, writes, chan, skip_waw=False):
            sc.add(eng, lambda h: h.dma_start(out=out, in_=in_), reads, writes, chan=chan, skip_waw=skip_waw)

        def mm(out, lhsT, rhs, start, stop, reads, writes, inc):
            sc.add("pe", lambda h: h.matmul(out, lhsT=lhsT, rhs=rhs, start=start, stop=stop),
                   reads, writes, inc=inc)

        def tr(out, in_, reads, writes, inc):
            sc.add("pe", lambda h: h.transpose(out, in_, ident), reads + ["cst_b"], writes, inc=inc)

        def act(out, in_, func, reads, writes, scale=1.0, bias=0.0, accum=None):
            if accum is None:
                sc.add("act", lambda h: h.activation(out=out, in_=in_, func=func, scale=scale, bias=bias),
                       reads, writes)
            else:
                sc.add("act", lambda h: h.activation(out=out, in_=in_, func=func, scale=scale, bias=bias,
                                                     accum_out=accum), reads, writes)

        def tt(eng, out, in0, in1, op, reads, writes):
            sc.add(eng, lambda h: h.tensor_tensor(out=out, in0=in0, in1=in1, op=op), reads, writes)

        def cp(eng, out, in_, reads, writes):
            if eng == "act":
                sc.add(eng, lambda h: h.copy(out=out, in_=in_), reads, writes)
            else:
                sc.add(eng, lambda h: h.tensor_copy(out=out, in_=in_), reads, writes)

        def h3(ap):
            return ap.rearrange("p (h t) -> p h t", t=128)

        dma("sp", cst_f[:, :], cst_d[:, 384:896], [], ["cst_f"], "c0")
        dma("pool", cst_b[:, :], cst_d[:, 0:384], [], ["cst_b"], "c0b")
        for i in range(2):
            dma("pool", gkb[i][:, :], gk_d[i], [], ["gkb%d" % i], "c1_%d" % i)
            dma("sp", lbl[i][:, :], lbl_d[i], [], ["lbl%d" % i], "c2_%d" % i)
        dma("sp", gn[:, :], gn_d, [], ["gn"], "c3")
        dma("sp", sel[:, :], sel_d, [], ["sel"], "c6")
        sc.add("pool", lambda h: h.memset(lrT[:, :], 1.0), [], ["lrT"])
        sc.add("pool", lambda h: h.memset(S[:, :], 0.0), [], ["S0", "S1", "S2"])
        sc.add("pool", lambda h: h.memset(Sb, 0.0), [], ["Sb0", "Sb1", "Sb2"])
        for i in range(2):
            act(lbt[:, :], lbl[i][:, :], AF.Exp, ["lbl%d" % i], ["lbt"])
            tt("dve", lbv[i][:, :], lbt[:, 0:8], lbt[:, 8:16], ALU.add, ["lbt"], ["lbv%d" % i])
            sc.add("dve", lambda h, i=i: h.reciprocal(out=lbv[i][:, :], in_=lbv[i][:, :]),
                   ["lbv%d" % i], ["lbv%d" % i])
            tt("dve", omlv[i][:, :], lbt[:, 8:16], lbv[i][:, :], ALU.mult, ["lbt", "lbv%d" % i], ["omlv%d" % i])
            tt("dve", lbv[i][:, :], lbt[:, 0:8], lbv[i][:, :], ALU.mult, ["lbt", "lbv%d" % i], ["lbv%d" % i])

        W1P = [(0, 1040, "W1a"), (1040, 3088, "W1b"), (3088, 5136, "W1c")]
        W2P = [(0, 1040, "W2a"), (1040, 3088, "W2b")]
        W3P = [(0, 2048, "W3")]
        ALLW = ["W1a", "W1b", "W1c", "W2a", "W2b", "W3", "WBlo"]

        def wname(W, col):
            ps = W1P if W is W1 else (W2P if W is W2 else W3P)
            for c0, c1, nm in ps:
                if c0 <= col < c1:
                    return nm
            raise AssertionError

        def load_w(dst3, src, pieces, tag, kcs=None, war_all=True, nowaw=False):
            first = True
            for (p0, p1, nm) in pieces:
                c0 = p0
                pfirst = True
                while c0 < p1:
                    c1 = min(p1, c0 + 2048)
                    for kc in (kcs if kcs is not None else range(dst3.shape[1])):
                        wr = [nm] + ([x for x in ALLW if x != nm] if (first and war_all) else [])
                        dma("pool", dst3[:, kc, c0:c1], src[:, kc, c0:c1], [], wr, "w" + tag + nm,
                            skip_waw=(not first) or nowaw)
                        first = False
                        pfirst = False
                    c0 = c1

        load_w(W1, w1_d, W1P, "1", war_all=False)

        flags = {}

        def need(k):
            while len(free_banks) < k:
                yield ("nobank",)

        def run_rr(gens, prio=0):
            gens = list(gens)
            if prio and len(gens) >= 8:
                order = [gens[j] for j in (RR_ORDER if prio == 1 else RR_ORDER2)]
            else:
                order = gens + (gens[3:6] if prio and len(gens) >= 6 else [])
            pend = {}
            alive = set(id(g) for g in gens)
            while alive:
                progressed = False
                for g in order:
                    if id(g) not in alive:
                        continue
                    k = pend.get(id(g))
                    if k is not None:
                        if not flags.get(k):
                            continue
                        pend[id(g)] = None
                    try:
                        r = next(g)
                    except StopIteration:
                        alive.discard(id(g))
                        progressed = True
                        continue
                    if isinstance(r, tuple) and r[0] == "wait":
                        pend[id(g)] = r[1]
                        progressed = True
                    elif isinstance(r, tuple) and r[0] == "nobank":
                        pass
                    else:
                        progressed = True
                assert progressed, "emission deadlock"

        def dma_x(t, xf, xfn):
            dma("sp", xf, x_d[t * P:(t + 1) * P, :], [], [xfn], "x" + xfn)

        def gen_load_x(t, b, xf=None, xfn=None, do_dma=True, ceng="act"):
            if xf is None:
                xf, xfn = x_f[b][:, :], "x_f%d" % b
            if do_dma:
                dma_x(t, xf, xfn)
            cp(ceng, x_bf[b][:, :], xf, [xfn], ["x_bf%d" % b])
            yield from need(1)
            bk, bn = nb()
            bkb = bk[:, :].bitcast(BF16)
            for kc in range(8):
                tr(bkb[:, kc * P:(kc + 1) * P], x_bf[b][:, kc * P:(kc + 1) * P], ["x_bf%d" % b], [bn], inc=(kc == 7))
            cp("act", xT[b][:, :], bkb[:, :], [bn], ["xT%d" % b])
            fb(bn)
            yield

        def proj_fm(W, col0, nheads, b, m=P):
            bk, bn = nb()
            for h in range(nheads):
                for kc in range(8):
                    mm(bk[0:m, h * P:(h + 1) * P], W[:, kc, col0 + h * m:col0 + (h + 1) * m],
                       xT[b][:, kc * P:(kc + 1) * P], kc == 0, kc == 7, ["xT%d" % b, wname(W, col0)], [bn],
                       inc=(h == nheads - 1 and kc == 7))
            return bk, bn

        def proj_tm(W, col0, b):
            bk, bn = nb()
            for kc in range(8):
                mm(bk[:, :], xT[b][:, kc * P:(kc + 1) * P], W[:, kc, col0:col0 + 512], kc == 0, kc == 7,
                   ["xT%d" % b, wname(W, col0)], [bn], inc=(kc == 7))
            return bk, bn

        def decay_chain(sw, u, par, Lbuf, Lname, scale, other, oname):
            B, Bn = TS[u][1], TN[u][1]
            Dd, Dn = TS[u][3], TN[u][3]
            el, eln = elast[u][par], "elast%d_%d" % (u, par)
            if sw == 1:
                sc.add("dve", lambda h: h.tensor_tensor_scan(out=B, data0=smask, data1=Lbuf,
                                                             initial=0.0, op0=ALU.mult, op1=ALU.add),
                       [Lname, "cst_f"], [Bn])
                tot = h3(B)[:, :, 127]
            else:
                sc.add("dve", lambda h: h.tensor_tensor_scan(out=B[:, ::-1], data0=smask, data1=Lbuf[:, ::-1],
                                                             initial=0.0, op0=ALU.mult, op1=ALU.add),
                       [Lname, "cst_f"], [Bn])
                tot = h3(B)[:, :, 0]
            yield
            act(el[:, :], tot, AF.Exp, [Bn], [eln], scale=-scale)
            act(Dd, B, AF.Exp, [Bn], [Dn], scale=-scale)
            yield
            if sw == 1:
                act(B, B, AF.Exp, [Bn, eln], [Bn], scale=scale)
                yield
                el_bc = el[:, :].unsqueeze(2).broadcast_to([P, 4, P])
                tt("dve", h3(other), h3(B), el_bc, ALU.mult, [Bn, eln], [oname])
                yield
            else:
                act(nbl[u][:, :], tot, AF.Identity, [Bn], ["nbl%d" % u], scale=-scale)
                for h in range(4):
                    sc.add("act", lambda hh, h=h: hh.activation(
                        out=other[:, h * P:(h + 1) * P], in_=B[:, h * P:(h + 1) * P], func=AF.Exp, scale=scale,
                        bias=nbl[u][:, h:h + 1]), [Bn, "nbl%d" % u] + ([oname] if oname != Bn else []), [oname])
                yield
                act(B, B, AF.Exp, [Bn, eln, oname], [Bn], scale=scale)
                yield
            return (Dd, Dn), (B, Bn), (other, oname)

        def stage_A_gla(sw, key, i, t, b, W, lrcol):
            u = 0
            TA, TC = TS[u][0], TS[u][2]
            TAn, TCn = TN[u][0], TN[u][2]
            yield from need(1)
            bk, bn = nb()
            for kc in range(8):
                mm(bk[0:16, 0:P], W[:, kc, lrcol:lrcol + 16], xT[b][:, kc * P:(kc + 1) * P], kc == 0, kc == 7,
                   ["xT%d" % b, wname(W, lrcol)], [bn], inc=(kc == 7))
            cp("act", lrT[0:16, :], bk[0:16, 0:P], [bn], ["lrT"])
            fb(bn)
            yield
            yield from need(1)
            bz, bzn = nb()
            for h in range(4):
                mm(bz[:, h * P:(h + 1) * P], gkb[sw - 1][:, h * P:(h + 1) * P], lrT[:, :], True, True,
                   ["lrT", "gkb%d" % (sw - 1)], [bzn], inc=(h == 3))
            act(TA, bz[:, :], AF.Exp, [bzn], [TAn], scale=-1.0)
            fb(bzn)
            yield
            act(TA, TA, AF.Ln, [TAn], [TAn], scale=1.0, bias=1.0)
            yield
            (eq, eqn), (ek, ekn), (ekh, ekhn) = yield from decay_chain(sw, u, i % 2, TA, TAn, 1.0 / 16.0, TC, TCn)
            if i >= 1:
                yield ("wait", (key, "B", i - 1, u))
            if sw == 1:
                yield from need(1)
                bq, bqn = proj_fm(W, 4112, 4, b)
                cp("act", qT_raw, bq[:, :], [bqn], ["qT_raw"])
                fb(bqn)
                yield
                yield from need(1)
                bkk, bkn = proj_fm(W, 4624, 4, b)
                cp("act", kT_raw, bkk[:, :], [bkn], ["kT_raw"])
                fb(bkn)
                dma("sp", qT_s.ap()[t], qT_raw, ["qT_raw"], ["qT_s%d" % t], "sq")
                dma("sp", kT_s.ap()[t], kT_raw, ["kT_raw"], ["kT_s%d" % t], "sk")
                yield
            else:
                dma("sp", qT_raw, qT_s.ap()[t], ["qT_s%d" % t], ["qT_raw"], "lq")
                dma("sp", kT_raw, kT_s.ap()[t], ["kT_s%d" % t], ["kT_raw"], "lk")
            tt("dve", QT[u], qT_raw, eq, ALU.mult, ["qT_raw", eqn], ["QT%d" % u])
            yield
            tt("dve", KT[u], kT_raw, ek, ALU.mult, ["kT_raw", ekn], ["KT%d" % u])
            yield
            tt("dve", KH[u], kT_raw, ekh, ALU.mult, ["kT_raw", ekhn], ["KH%d" % u])
            yield

        def stage_A_hgrn(sw, key, i, t, b, u, W, hfcol):
            hh = u - 1
            TA, TB, TC, TD = TS[u]
            TAn, TBn, TCn, TDn = TN[u]
            yield from need(1)
            bf, bfn = proj_fm(W, hfcol + hh * 512, 4, b)
            lb_bc = lbv[sw - 1][:, hh * 4:(hh + 1) * 4].unsqueeze(2).broadcast_to([P, 4, P])
            oml_bc = omlv[sw - 1][:, hh * 4:(hh + 1) * 4].unsqueeze(2).broadcast_to([P, 4, P])
            lbn, omn = "lbv%d" % (sw - 1), "omlv%d" % (sw - 1)
            act(TA, bf[:, :], AF.Exp, [bfn], [TAn], scale=-1.0)
            yield
            act(TB, TA, AF.Ln, [TAn], [TBn], scale=1.0, bias=1.0)
            tt("pool", h3(TC), h3(TA), lb_bc, ALU.mult, [TAn, lbn], [TCn])
            yield
            tt("dve", TA, bf[:, :], TB, ALU.add, [bfn, TBn], [TAn])
            fb(bfn)
            act(TC, TC, AF.Ln, [TCn], [TCn], scale=1.0, bias=1.0)
            yield
            act(TA, TA, AF.Exp, [TAn], [TAn], scale=-1.0)
            tt("pool", TC, TB, TC, ALU.subtract, [TBn, TCn], [TCn])
            yield
            tt("pool", h3(TA), h3(TA), oml_bc, ALU.mult, [TAn, omn], [TAn])
            (eq, eqn), (ek, ekn), (ekh, ekhn) = yield from decay_chain(sw, u, i % 2, TC, TCn, 1.0, TC, TCn)
            if i >= 1:
                yield ("wait", (key, "B", i - 1, u))
            yield ("wait", (key, "hq", i))
            hqn = "hq_raw%d_%d" % (b, hh)
            tt("dve", QT[u], hq_raw[b][:, hh * 512:(hh + 1) * 512], eq, ALU.mult, [hqn, eqn], ["QT%d" % u])
            yield
            tt("dve", KT[u], TA, ek, ALU.mult, [TAn, ekn], ["KT%d" % u])
            yield
            tt("dve", KH[u], TA, ekh, ALU.mult, [TAn, ekhn], ["KH%d" % u])
            yield

        def stage_B(sw, key, i, t, b, u):
            dv = 256 if u == 0 else 128
            s0 = 0 if u == 0 else 1024 + (u - 1) * 512
            if u == 0:
                V, vname = v_bf[b][:, :], "v_bf%d" % b
            else:
                V, vname = hi_bf[b][:, (u - 1) * 512:u * 512], "hi_bf%d" % b
            Sn, Sbn = "S%d" % u, "Sb%d" % u
            qt, kt, kh, kht, sct, el = QT[u], KT[u], KH[u], KHt[u], scT[u], elast[u][i % 2]
            qn, kn, khn, khtn, scn, eln = ("QT%d" % u, "KT%d" % u, "KH%d" % u, "KHt%d" % u, "scT%d" % u,
                                           "elast%d_%d" % (u, i % 2))
            yield from need(2)
            bk, bn = nb()
            for h in range(4):
                mm(bk[:, h * P:(h + 1) * P], kt[:, h * P:(h + 1) * P], qt[:, h * P:(h + 1) * P], True, True,
                   [kn, qn], [bn], inc=(h == 3))
            bk2, bn2 = nb()
            bkb = bk2[:, :].bitcast(BF16)
            for h in range(4):
                tr(bkb[:, h * P:(h + 1) * P], kh[:, h * P:(h + 1) * P], [khn], [bn2], inc=(h == 3))
            tri_bc = tri[sw - 1].unsqueeze(1).broadcast_to([P, 4, P])
            tt("dve", h3(sct), h3(bk[:, :]), tri_bc, ALU.mult, [bn, "cst_b"], [scn])
            cp("act", kht, bkb[:, 0:512], [bn2], [khtn])
            fb(bn, bn2)
            yield
            nbk = 2 if u == 0 else 1
            yield from need(nbk)
            obanks = [nb() for _ in range(nbk)]
            for h in range(4):
                ob, obn = obanks[(h * dv) // 512]
                oc = (h * dv) % 512
                mm(ob[:, oc:oc + dv], sct[:, h * P:(h + 1) * P], V[:, h * dv:(h + 1) * dv], True, False,
                   [scn, vname], [obn], inc=False)
                mm(ob[:, oc:oc + dv], qt[:, h * P:(h + 1) * P], Sb[:, s0 + h * dv:s0 + (h + 1) * dv], False, sw == 1,
                   [qn, Sbn], [obn], inc=(sw == 1))
                if sw == 2:
                    mm(ob[:, oc:oc + dv], ident, o1b[b][:, s0 + h * dv:s0 + (h + 1) * dv], False, True,
                       ["cst_b", "o1b%d_%d" % (b, u)], [obn], inc=True)
            flags[(key, "B", i, u)] = True
            yield
            ucols = slice(s0, s0 + 4 * dv)
            for j, (ob, obn) in enumerate(obanks):
                cols = slice(s0 + j * 512, s0 + (j + 1) * 512)
                if sw == 1:
                    cp("act", o1b[b][:, cols], ob[:, :], [obn], ["o1b%d_%d" % (b, u)])
                else:
                    cp("act", otot[:, cols], ob[:, :], [obn], ["otot%d" % u])
                fb(obn)
            if sw == 1:
                dma("sp", o1_s.ap()[t][:, ucols], o1b[b][:, ucols], ["o1b%d_%d" % (b, u)], ["o1_s%d_%d" % (t, u)],
                    "so1%d_%d" % (b, u))
            yield
            yield from need(nbk)
            pbanks = [nb() for _ in range(nbk)]
            for h in range(4):
                pb, pbn = pbanks[(h * dv) // 512]
                pc = (h * dv) % 512
                mm(pb[:, pc:pc + dv], kht[:, h * P:(h + 1) * P], V[:, h * dv:(h + 1) * dv], True, True,
                   [khtn, vname], [pbn], inc=True)
            yield
            for h in range(4):
                pb, pbn = pbanks[(h * dv) // 512]
                pc = (h * dv) % 512
                scol = slice(s0 + h * dv, s0 + (h + 1) * dv)
                sc.add("dve", lambda hh, scol=scol, pb=pb, pc=pc, h=h, el=el: hh.scalar_tensor_tensor(
                    out=S[:, scol], in0=S[:, scol], scalar=el[:, h:h + 1], in1=pb[:, pc:pc + dv],
                    op0=ALU.mult, op1=ALU.add), [Sn, eln, pbn], [Sn])
                if h % 2 == 1:
                    yield
            fb(*[pbn for (_, pbn) in pbanks])
            cp("pool", Sb[:, s0:s0 + 4 * dv], S[:, s0:s0 + 4 * dv], [Sn], [Sbn])
            yield
            if sw == 2:
                on = "otot%d" % u
                act(sq[:, 0:4 * dv], otot[:, ucols], AF.Square, [on], ["sq"])
                sc.add("dve", lambda h: h.tensor_reduce(
                    out=ss[u][:, :], in_=sq[:, 0:4 * dv].rearrange("p (h e) -> p h e", e=dv),
                    axis=mybir.AxisListType.X, op=ALU.add), ["sq"], ["ss%d" % u])
                yield
                act(rstd[u][:, :], ss[u][:, :], AF.Ln, ["ss%d" % u], ["rstd%d" % u], scale=1.0 / dv, bias=RMS_EPS)
                act(rstd[u][:, :], rstd[u][:, :], AF.Exp, ["rstd%d" % u], ["rstd%d" % u], scale=-0.5)
                yield
                r_bc = rstd[u][:, :].unsqueeze(2).broadcast_to([P, 4, dv])
                o3 = otot[:, ucols].rearrange("p (h e) -> p h e", e=dv)
                g3 = og[b][:, ucols].rearrange("p (h e) -> p h e", e=dv)
                for h in range(4):
                    sc.add("act", lambda hh, h=h: hh.activation(
                        out=og[b][:, s0 + h * dv:s0 + (h + 1) * dv], in_=otot[:, s0 + h * dv:s0 + (h + 1) * dv],
                        func=AF.Identity, scale=rstd[u][:, h:h + 1]), [on, "rstd%d" % u], ["og%d_%d" % (b, u)])
                dma("sp", og_s.ap()[t][:, ucols], og[b][:, ucols], ["og%d_%d" % (b, u)], ["og_s%d_%d" % (t, u)],
                    "sog%d_%d" % (b, u))
                yield

        def run_sweep(sw, key, order, pre_T, gen_Tv, after_proj, W, lrcol, hfcol):
            n = len(order)
            xs = lambda j: (x_f[j % 3][:, :], "x_f%d" % (j % 3))
            dma_x(order[0], *xs(0))
            if n > 1:
                dma_x(order[1], *xs(1))
            ceng = "dve" if sw == 1 else "act"
            run_rr([gen_load_x(order[0], 0, *xs(0), do_dma=False, ceng=ceng)])
            for i in range(n + 1):
                gens = []
                if i + 2 < n:
                    dma_x(order[i + 2], *xs(i + 2))
                if i < n:
                    t, b = order[i], i % 2
                    pre_T(t, b, i)
                    gens += [stage_A_gla(sw, key, i, t, b, W, lrcol),
                             stage_A_hgrn(sw, key, i, t, b, 1, W, hfcol),
                             stage_A_hgrn(sw, key, i, t, b, 2, W, hfcol)]
                if i >= 1:
                    pt, pb = order[i - 1], (i - 1) % 2
                    gens += [stage_B(sw, key, i - 1, pt, pb, 0), stage_B(sw, key, i - 1, pt, pb, 1),
                             stage_B(sw, key, i - 1, pt, pb, 2)]
                if i == n:
                    after_proj()
                if i < n:
                    gens.append(gen_Tv(t, b, i))
                if i + 1 < n:
                    gens.append(gen_load_x(order[i + 1], (i + 1) % 2, *xs(i + 1), do_dma=False, ceng=ceng))
                run_rr(gens, prio=sw if i < n else 0)

        def pre_T1(t, b, i):
            pass

        def gen_Tv1(t, b, i):
            for j in range(2):
                yield from need(1)
                bv, bvn = proj_tm(W1, 1040 + j * 512, b)
                act(v_bf[b][:, j * 512:(j + 1) * 512], bv[:, :], AF.Identity, [bvn], ["v_bf%d" % b], scale=Q_SCALE)
                fb(bvn)
                yield
            for j in range(2):
                yield from need(1)
                bv, bvn = proj_tm(W1, 2064 + j * 512, b)
                act(hi_bf[b][:, j * 512:(j + 1) * 512], bv[:, :], AF.Identity, [bvn], ["hi_bf%d" % b], scale=Q_SCALE)
                fb(bvn)
                yield
            yield from need(2)
            hb = [proj_fm(W1, 3088 + hh * 512, 4, b) for hh in range(2)]
            for hh in range(2):
                act(hq_raw[b][:, hh * 512:(hh + 1) * 512], hb[hh][0][:, :], AF.Silu, [hb[hh][1]],
                    ["hq_raw%d_%d" % (b, hh)])
                fb(hb[hh][1])
            flags[("s1", "hq", i)] = True
            dma("sp", v_s.ap()[t], v_bf[b][:, :], ["v_bf%d" % b], ["v_s%d" % t], "sv%d" % b)
            dma("sp", hi_s.ap()[t], hi_bf[b][:, :], ["hi_bf%d" % b], ["hi_s%d" % t], "shi%d" % b)
            dma("sp", hq_s.ap()[t], hq_raw[b][:, :], ["hq_raw%d_0" % b, "hq_raw%d_1" % b], ["hq_s%d" % t],
                "shq%d" % b)
            yield

        def after1():
            load_w(W2, w2_d, W2P, "2")

        if stop is None or stop >= 1:
            run_sweep(1, "s1", list(range(NT)), pre_T1, gen_Tv1, after1, W1, 0, 16)

        def exchange():
            dma("sp", cc_in.ap(), S[:, :], ["S0", "S1", "S2"], ["cc_in"], "xs")
            sc.add("pool", lambda h: h.collective_compute(
                "AllGather", ALU.bypass, replica_groups=[[0, 1], [2, 3], [4, 5], [6, 7]],
                ins=[cc_in.ap().opt()], outs=[cc_out.ap().opt()]), ["cc_in"], ["cc_out"], chan="cc", cc=True)
            dma("sp", otot[:, :], cc_out.ap()[0:P, :], ["cc_out"], ["otot0", "otot1", "otot2"], "xl0")
            sc.add("dve", lambda h: h.tensor_scalar(out=S[:, :], in0=otot[:, :], scalar1=sel[:, 0:1], scalar2=None,
                                                    op0=ALU.mult), ["otot0", "otot1", "otot2", "sel"], ["S0", "S1", "S2"])
            dma("sp", otot[:, :], cc_out.ap()[P:2 * P, :], ["cc_out"], ["otot0", "otot1", "otot2"], "xl1")
            sc.add("dve", lambda h: h.scalar_tensor_tensor(out=S[:, :], in0=otot[:, :], scalar=sel[:, 1:2],
                                                           in1=S[:, :], op0=ALU.mult, op1=ALU.add),
                   ["otot0", "otot1", "otot2", "sel", "S0", "S1", "S2"], ["S0", "S1", "S2"])
            cp("act", Sb, S[:, :], ["S0", "S1", "S2"], ["Sb0", "Sb1", "Sb2"])

        def pre_T2(t, b, i):
            dma("sp", v_bf[b][:, :], v_s.ap()[t], ["v_s%d" % t], ["v_bf%d" % b], "lv%d" % b)
            dma("sp", hi_bf[b][:, :], hi_s.ap()[t], ["hi_s%d" % t], ["hi_bf%d" % b], "lhi%d" % b)
            dma("sp", hq_raw[b][:, :], hq_s.ap()[t], ["hq_s%d" % t], ["hq_raw%d_0" % b, "hq_raw%d_1" % b],
                "lhq%d" % b)
            dma("sp", o1b[b][:, :], o1_s.ap()[t], ["o1_s%d_%d" % (t, u) for u in range(3)],
                ["o1b%d_%d" % (b, u) for u in range(3)], "lo1%d" % b)
            flags[("s2", "hq", i)] = True
            if i == 1:
                load_w(WB, wb_d, [(0, D, "WBhi")], "bh", kcs=range(8, 24), war_all=False)

        def gen_Tv2(t, b, i):
            for j in range(4):
                yield from need(1)
                gbk, gbn = proj_tm(W2, 1040 + j * 512, b)
                cp("act", sg[b][:, j * 512:(j + 1) * 512], gbk[:, :], [gbn], ["sg%d" % b])
                fb(gbn)
                yield
            dma("sp", g_s.ap()[t], sg[b][:, :], ["sg%d" % b], ["g_s%d" % t], "sg_s%d" % b)
            yield

        def after2():
            load_w(W3, w3_d, W3P, "3")
            load_w(WB, wb_d, [(0, D, "WBlo")], "b", kcs=range(8), war_all=False, nowaw=True)

        def gen_P1(t, b):
            yield from need(4)
            gb = [proj_tm(W3, j * 512, b) for j in range(4)]
            for j in range(4):
                act(sigm[b][:, j * 512:(j + 1) * 512], gb[j][0][:, :], AF.Sigmoid, [gb[j][1]], ["sigm%d" % b])
                fb(gb[j][1])
            act(sgt, sg[b][:, :], AF.Sigmoid, ["sg%d" % b], ["sgt"])
            yield
            tt("dve", sg[b][:, :], sg[b][:, :], sgt, ALU.mult, ["sg%d" % b, "sgt"], ["sg%d" % b])
            tt("dve", og[b][:, :], og[b][:, :], sg[b][:, :], ALU.mult, ["og%d" % b, "sg%d" % b], ["og%d" % b])
            yield
            for half in range(2):
                yield from need(1)
                bk, bn = nb()
                bkb = bk[:, :].bitcast(BF16)
                for c in range(8):
                    cc = half * 8 + c
                    tr(bkb[:, c * P:(c + 1) * P], og[b][:, cc * P:(cc + 1) * P], ["og%d" % b], [bn], inc=(c == 7))
                cp("act" if half == 0 else "dve", ogT[b][:, half * 1024:(half + 1) * 1024], bkb[:, :], [bn],
                   ["ogT%d_%d" % (b, half)])
                fb(bn)
                yield

        x_f3 = [x_f[0][:, :], x_f[1][:, :], x_f[2][:, :], sq[:, :]]
        x_f3n = ["x_f0", "x_f1", "x_f2", "sq"]
        y_bf2 = [y_bf, T4s[2][:, :].bitcast(BF16)[:, 0:1024]]
        sgt = T4s[2][:, :].bitcast(BF16)[:, 1024:3072]

        def gen_P2a(t, b):
            yb = y_bf2[b]
            for j in range(2):
                yield from need(2)
                bg, bgn = nb()
                for kc in range(8):
                    mm(bg[:, :], ogT[b][:, kc * P:(kc + 1) * P], WB[:, kc, j * 512:(j + 1) * 512], kc == 0, kc == 7,
                       ["ogT%d_0" % b, "WBlo"], [bgn], inc=(kc == 7))
                bh, bhn = nb()
                for kc in range(8):
                    mm(bh[:, :], ogT[b][:, (8 + kc) * P:(9 + kc) * P], WB[:, 8 + kc, j * 512:(j + 1) * 512],
                       kc == 0, kc == 7, ["ogT%d_1" % b, "WBhi"], [bhn], inc=(kc == 7))
                c0 = slice(j * 512, (j + 1) * 512)
                c1 = slice(1024 + j * 512, 1024 + (j + 1) * 512)
                tt("dve", S[:, c0], bg[:, :], sigm[b][:, c0], ALU.mult, [bgn, "sigm%d" % b], ["t1_%d" % j])
                tt("dve", S[:, c1], bh[:, :], sigm[b][:, c1], ALU.mult, [bhn, "sigm%d" % b], ["t2_%d" % j])
                fb(bgn, bhn)
                yield
                tt("pool" if j == 0 else "dve", yb[:, c0], S[:, c0], S[:, c1], ALU.add, ["t1_%d" % j, "t2_%d" % j],
                   ["y_bf%d_%d" % (b, j)])
                yield

        def gen_P2b(t, b):
            yb = y_bf2[b]
            xf, xfn = x_f3[t % 4], x_f3n[t % 4]
            yield from need(1)
            bk, bn = nb()
            bkb = bk[:, :].bitcast(BF16)
            for c in range(8):
                tr(bkb[:, c * P:(c + 1) * P], yb[:, c * P:(c + 1) * P], ["y_bf%d_%d" % (b, c // 4)], [bn], inc=(c == 7))
            cp("act", yT, bkb[:, :], [bn], ["yT"])
            fb(bn)
            yield
            for j in range(2):
                yield from need(1)
                bo, bon = nb()
                for kc in range(8):
                    mm(bo[:, :], yT[:, kc * P:(kc + 1) * P], WB[:, 16 + kc, j * 512:(j + 1) * 512],
                       kc == 0, kc == 7, ["yT", "WBhi"], [bon], inc=(kc == 7))
                c0 = slice(j * 512, (j + 1) * 512)
                sc.add("dve", lambda h, c0=c0, bo=bo, xf=xf: h.scalar_tensor_tensor(
                    out=otot[:, c0], in0=xf[:, c0], scalar=ALPHA, in1=bo[:, :], op0=ALU.mult, op1=ALU.add),
                    [xfn, bon], ["r%d" % j])
                sc.add("dve", lambda h, c0=c0, j=j: h.bn_stats(out=bst[:, j * 6:(j + 1) * 6], in_=otot[:, c0]),
                       ["r%d" % j], ["bst%d" % j])
                fb(bon)
                yield
            sc.add("dve", lambda h: h.bn_aggr(out=mv[:, :], in_=bst[:, :]), ["bst0", "bst1"], ["mv"])
            act(lnr[:, 0:1], mv[:, 1:2], AF.Ln, ["mv"], ["lnr"], scale=1.0, bias=LN_EPS)
            act(lnr[:, 0:1], lnr[:, 0:1], AF.Exp, ["lnr"], ["lnr"], scale=-0.5)
            sc.add("dve", lambda h: h.scalar_tensor_tensor(
                out=lnr[:, 1:2], in0=mv[:, 0:1], scalar=-1.0, in1=lnr[:, 0:1], op0=ALU.mult, op1=ALU.mult),
                ["mv", "lnr"], ["lnr"])
            yield
            sc.add("act", lambda h: h.activation(out=otot[:, 0:1024], in_=otot[:, 0:1024], func=AF.Identity,
                                                 scale=lnr[:, 0:1], bias=lnr[:, 1:2]),
                   ["r0", "r1", "lnr"], ["r0", "r1"])
            yield
            tt("pool", otot[:, 0:1024], otot[:, 0:1024], lng, ALU.mult, ["r0", "r1", "lng"], ["r0", "r1"])
            yield
            tt("dve", ores[b], otot[:, 0:1024], lnb, ALU.add, ["r0", "r1", "lnb"], ["ores%d" % b])
            dma("sp", out_d[t * P:(t + 1) * P, :], ores[b], ["ores%d" % b], [], "out%d" % b)
            yield

        def sweep3():
            sc.barrier()
            dma("sp", lng, lng_d, [], ["lng"], "c4")
            dma("sp", lnb, lnb_d, [], ["lnb"], "c5")
            for kc in range(16):
                gcol = (kc % 2) if kc < 8 else 2
                wr = "WBlo" if kc < 8 else "WBhi"
                sc.add("dve", lambda h, kc=kc, gcol=gcol: h.tensor_scalar(
                    out=WB[:, kc, :], in0=WB[:, kc, :], scalar1=gn[:, gcol:gcol + 1], scalar2=None, op0=ALU.mult),
                    [wr, "gn"], [wr])
            def p1_loads(t, b):
                dma_x(t, x_f3[t % 4], x_f3n[t % 4])
                dma("sp", og[b][:, :], og_s.ap()[t], ["og_s%d_%d" % (t, u) for u in range(3)], ["og%d" % b],
                    "log%d" % b)
                dma("sp", sg[b][:, :], g_s.ap()[t], ["g_s%d" % t], ["sg%d" % b], "lg%d" % b)

            p1_loads(0, 0)
            for k in range(NT + 2):
                gens = []
                if k + 1 < NT:
                    p1_loads(k + 1, (k + 1) % 2)
                if k - 2 >= 0:
                    gens.append(gen_P2b(k - 2, (k - 2) % 2))
                if 0 <= k - 1 < NT:
                    gens.append(gen_P2a(k - 1, (k - 1) % 2))
                if k < NT:
                    gens.append(seq(gen_load_x(k, k % 2, x_f3[k % 4], x_f3n[k % 4], do_dma=False), gen_P1(k, k % 2)))
                run_rr(gens)

        def seq(*gens):
            for g in gens:
                yield from g

        if stop is None or stop >= 2:
            exchange()
            run_sweep(2, "s2", list(range(NT - 1, -1, -1)), pre_T2, gen_Tv2, after2, W2, 0, 16)
        if stop is None or stop >= 3:
            sweep3()

        block = es.enter_context(nc.Block())
        sc.emit(block)
    return nc


_COLS = dict(a_q=0, a_k=512, a_v=1024, a_gate=2048, lr_f=3072, lr_b=3088, h_q=3104, h_f_f=4128,
             h_f_b=5152, h_i=6176, h_gate=7200, m_gla=8224, m_hgrn=9248)


def _tile_rows(w):
    k = w.shape[0] // P
    return np.ascontiguousarray(w.reshape(k, P, w.shape[1]).transpose(1, 0, 2))


def make_in_maps(x, w_in, gla_gk_up_f, gla_gk_bias_f, gla_gk_up_b, gla_gk_bias_b, gla_norm_g,
                 hgrn_lb_logits_f, hgrn_lb_logits_b, hgrn_norm_g, w_branch_gla, w_branch_hgrn,
                 w_out, ln_g, ln_b):
    f32 = np.float32
    x = np.asarray(x, f32)
    win = np.asarray(w_in, f32)[0]
    c = _COLS

    def cols(name, n):
        return win[:, c[name]:c[name] + n]

    dirs = {
        "f": dict(lr=cols("lr_f", 16), hf=cols("h_f_f", 1024),
                  gk=np.concatenate([np.asarray(gla_gk_up_f, f32)[0], np.asarray(gla_gk_bias_f, f32)[0][None]], 0),
                  lbl=np.asarray(hgrn_lb_logits_f, f32)),
        "b": dict(lr=cols("lr_b", 16), hf=cols("h_f_b", 1024),
                  gk=np.concatenate([np.asarray(gla_gk_up_b, f32)[0], np.asarray(gla_gk_bias_b, f32)[0][None]], 0),
                  lbl=np.asarray(hgrn_lb_logits_b, f32)),
    }

    def lbl_layout(l):
        return np.ascontiguousarray(l.reshape(2, 8, P).transpose(2, 0, 1).reshape(P, 16))

    w3 = _tile_rows(np.concatenate([cols("m_gla", 1024), cols("m_hgrn", 1024)], 1))
    wb = _tile_rows(np.concatenate([np.asarray(w_branch_gla, f32)[0], np.asarray(w_branch_hgrn, f32)[0],
                                    np.asarray(w_out, f32)[0]], 0))
    gng = np.asarray(gla_norm_g, f32)[0]
    gn = np.stack([gng[0:128], gng[128:256], np.asarray(hgrn_norm_g, f32)[0]], 1)
    lng = np.ascontiguousarray(np.broadcast_to(np.asarray(ln_g, f32)[0][None], (P, D)))
    lnb = np.ascontiguousarray(np.broadcast_to(np.asarray(ln_b, f32)[0][None], (P, D)))
    ii = np.arange(P)
    smask = np.ones((P, 512), f32)
    smask[:, 0::128] = 0.0
    cst = np.concatenate([np.eye(P, dtype=f32), (ii[:, None] <= ii[None, :]).astype(f32),
                          (ii[:, None] >= ii[None, :]).astype(f32), smask], 1)

    per_dir = {}
    for d1, d2 in (("f", "b"), ("b", "f")):
        w1 = _tile_rows(np.concatenate([dirs[d1]["lr"], dirs[d1]["hf"], cols("a_v", 1024), cols("h_i", 1024),
                                        cols("h_q", 1024), cols("a_q", 512), cols("a_k", 512)], 1))
        w2 = _tile_rows(np.concatenate([dirs[d2]["lr"], dirs[d2]["hf"], cols("a_gate", 1024),
                                        cols("h_gate", 1024)], 1))
        per_dir[d1] = dict(w1=w1, w2=w2, gk1=dirs[d1]["gk"], gk2=dirs[d2]["gk"],
                           lbl1=lbl_layout(dirs[d1]["lbl"]), lbl2=lbl_layout(dirs[d2]["lbl"]))

    in_maps = []
    for core in range(NCORES):
        b, half = core // 2, core % 2
        TL = x.shape[1] // 2
        xl = x[b, half * TL:(half + 1) * TL]
        if half == 1:
            xl = xl[::-1]
        pd = per_dir["f" if half == 0 else "b"]
        sel = np.zeros((P, 2), f32)
        sel[:, 1 - half] = 1.0
        in_maps.append(dict(x=np.ascontiguousarray(xl), w1=pd["w1"], w2=pd["w2"], w3=w3, wb=wb,
                            gk1=pd["gk1"], gk2=pd["gk2"], lbl1=pd["lbl1"], lbl2=pd["lbl2"], gn=gn,
                            lng=lng, lnb=lnb, sel=sel, cst=cst))

    return in_maps


def assemble(outs, T):
    out = np.empty((4, T, D), np.float32)
    TL = T // 2
    for core in range(NCORES):
        b, half = core // 2, core % 2
        o = np.asarray(outs[core], np.float32)
        if half == 1:
            o = o[::-1]
        out[b, half * TL:(half + 1) * TL] = o
    return out


def kernel(**inputs):
    in_maps = make_in_maps(**inputs)
    nc = build_program()
    res = run_bass_kernel_spmd(nc, in_maps, core_ids=list(range(NCORES)))
    return assemble([res.results[c]["out"] for c in range(NCORES)], 4096)
```

```python
import math
from contextlib import ExitStack

import numpy as np
import concourse.bass as bass
import concourse.mybir as mybir
from concourse.bass_utils import run_bass_kernel_spmd

F32 = mybir.dt.float32
BF16 = mybir.dt.bfloat16
AF = mybir.ActivationFunctionType
ALU = mybir.AluOpType

P = 128
D = 1024
NT = 16
NCORES = 8
W1C, W2C, W3C = 5136, 3088, 2048
WFLAT = 8 * W1C
Q_SCALE = 128.0 ** -0.5
LN_QS = math.log(Q_SCALE)
ALPHA = 2.0 ** 0.25
RMS_EPS = 1e-6
LN_EPS = 1e-5
SAME_ENGINE_WAITS = True
PRIO = 3
RR_ORDER = [0, 1, 2, 3, 4, 5, 6, 7, 3]
RR_ORDER2 = [0, 1, 2, 3, 4, 5, 6, 3, 7, 0]


class Sched:
    def __init__(self, nc, es):
        self.nc = nc
        self.es = es
        self.ops = []
        self.last_w = {}
        self.readers = {}
        self.eng_sem = {}
        self.eng_cnt = {}
        self.chan = {}
        self.pending = {}
        self.last_access = {}
        self.bar_deps = []
        for e in ("pe", "act", "dve", "pool"):
            self.eng_sem[e] = es.enter_context(nc.semaphore("s_" + e))
            self.eng_cnt[e] = 0

    def _deps(self, reads, writes, skip_waw=False):
        deps = []
        for r in reads:
            if r in self.last_w:
                deps.append((self.last_w[r], "raw"))
        for w in writes:
            if w in self.last_w and not skip_waw:
                deps.append((self.last_w[w], "waw"))
            for rd in self.readers.get(w, ()):
                deps.append((rd, "war"))
        return deps

    def add(self, eng, fn, reads=(), writes=(), chan=None, inc=True, cc=False, skip_waw=False):
        idx = len(self.ops)
        deps = self._deps(reads, writes, skip_waw) + list(self.bar_deps)
        for r in list(reads) + list(writes):
            if r.startswith("ps"):
                la = self.last_access.get(r)
                if la is not None and self.ops[la]["eng"] != eng:
                    deps.append((la, "psx"))
                self.last_access[r] = idx
        op = dict(eng=eng, fn=fn, deps=deps, done=None, kind="c", own=(chan is not None or inc))
        if chan is not None:
            if chan not in self.chan:
                self.chan[chan] = [self.es.enter_context(self.nc.semaphore("d_" + chan)), 0]
            c = self.chan[chan]
            step = 1 if cc else 16
            c[1] += step
            op["done"] = (c[0], c[1])
            op["kind"] = "cc" if cc else "dma"
        elif inc:
            self.eng_cnt[eng] += 1
            op["done"] = (self.eng_sem[eng], self.eng_cnt[eng])
            for po in self.pending.get(eng, ()):
                po["done"] = op["done"]
            self.pending[eng] = []
        else:
            self.pending.setdefault(eng, []).append(op)
        self.ops.append(op)
        for w in writes:
            self.last_w[w] = idx
            self.readers[w] = []
        for r in reads:
            self.readers.setdefault(r, []).append(idx)
        return idx

    def barrier(self):
        last = {}
        for i, o in enumerate(self.ops):
            if o["done"] is not None:
                last[(o["eng"], id(o["done"][0]))] = i
        self.bar_deps = [(i, "bar") for i in last.values()]

    def emit(self, block):
        nc = self.nc
        handles = {"pe": "tensor", "act": "scalar", "dve": "vector", "pool": "gpsimd", "sp": "sync"}
        for eng, hname in handles.items():
            my = [o for o in self.ops if o["eng"] == eng]

            def body(h, my=my, eng=eng):
                waited = {}
                for op in my:
                    needs = {}
                    for (di, kind) in op["deps"]:
                        dop = self.ops[di]
                        if dop["done"] is None:
                            continue
                        if dop["eng"] == eng and dop["kind"] == "c":
                            if not SAME_ENGINE_WAITS or eng == "pe":
                                continue
                        sem, val = dop["done"]
                        key = id(sem)
                        if key not in needs or needs[key][1] < val:
                            needs[key] = (sem, val)
                    for key, (sem, val) in needs.items():
                        if waited.get(key, 0) >= val:
                            continue
                        waited[key] = val
                        h.wait_ge(sem, val)
                    ins = op["fn"](h)
                    if op["done"] is not None and op["own"]:
                        sem, val = op["done"]
                        if op["kind"] == "dma":
                            ins.then_inc(sem, 16)
                        elif op["kind"] == "cc":
                            ins.then_inc(sem)
                        else:
                            ins.then_inc(sem, 1)
                if eng == "sp":
                    for name, (sem, val) in self.chan.items():
                        h.wait_ge(sem, val)

            getattr(block, hname)(body)


def build_program(NT=NT, stop=None):
    nc = bass.Bass("TRN2", target_bir_lowering=False)
    dt_in = lambda name, shape: nc.dram_tensor(name, shape, F32, kind="ExternalInput").ap()
    x_d = dt_in("x", [NT * P, D])
    w1_d = dt_in("w1", [P, 8, W1C])
    w2_d = dt_in("w2", [P, 8, W2C])
    w3_d = dt_in("w3", [P, 8, W3C])
    wb_d = dt_in("wb", [P, 24, D])
    gk_d = [dt_in("gk1", [17, 512]), dt_in("gk2", [17, 512])]
    lbl_d = [dt_in("lbl1", [P, 16]), dt_in("lbl2", [P, 16])]
    gn_d = dt_in("gn", [P, 3])
    lng_d = dt_in("lng", [P, D])
    lnb_d = dt_in("lnb", [P, D])
    sel_d = dt_in("sel", [P, 2])
    cst_d = dt_in("cst", [P, 3 * 128 + 512])
    out_d = nc.dram_tensor("out", [NT * P, D], F32, kind="ExternalOutput").ap()

    qT_s = nc.dram_tensor("qT_s", [NT, P, 512], BF16)
    kT_s = nc.dram_tensor("kT_s", [NT, P, 512], BF16)
    hq_s = nc.dram_tensor("hq_s", [NT, P, 1024], BF16)
    v_s = nc.dram_tensor("v_s", [NT, P, 1024], BF16)
    hi_s = nc.dram_tensor("hi_s", [NT, P, 1024], BF16)
    o1_s = nc.dram_tensor("o1_s", [NT, P, 2048], BF16)
    og_s = nc.dram_tensor("og_s", [NT, P, 2048], BF16)
    g_s = nc.dram_tensor("g_s", [NT, P, 2048], BF16)
    cc_in = nc.dram_tensor("cc_in", [P, 2048], F32)
    cc_out = nc.dram_tensor("cc_out", [2 * P, 2048], F32)

    with ExitStack() as es:
        sb = lambda name, shape, dt: es.enter_context(nc.sbuf_tensor(name, shape, dt))
        wflat = sb("wflat", [P, WFLAT], BF16)
        W1 = wflat[:, 0:8 * W1C].rearrange("p (k c) -> p k c", c=W1C)
        W2 = wflat[:, 0:8 * W2C].rearrange("p (k c) -> p k c", c=W2C)
        W3 = wflat[:, 0:8 * W3C].rearrange("p (k c) -> p k c", c=W3C)
        WB = wflat[:, WFLAT - 24 * D:WFLAT].rearrange("p (k c) -> p k c", c=D)

        cst_f = sb("cst_f", [P, 512], F32)
        cst_b = sb("cst_b", [P, 3 * 128], BF16)
        ident = cst_b[:, 0:128]
        tri = [cst_b[:, 128:256], cst_b[:, 256:384]]
        smask = cst_f[:, 0:512]
        gkb = [sb("gk1b", [17, 512], BF16), sb("gk2b", [17, 512], BF16)]
        lbl = [sb("lbl1s", [P, 16], F32), sb("lbl2s", [P, 16], F32)]
        lbv = [sb("lbv1", [P, 8], F32), sb("lbv2", [P, 8], F32)]
        omlv = [sb("omlv1", [P, 8], F32), sb("omlv2", [P, 8], F32)]
        lbt = sb("lbt", [P, 16], F32)
        gn = sb("gns", [P, 3], F32)
        sel = sb("sels", [P, 2], F32)

        x_f = [sb("x_f%d" % i, [P, D], F32) for i in range(3)]
        x_bf = [sb("x_bf%d" % i, [P, D], BF16) for i in range(2)]
        xT = [sb("xT%d" % i, [P, D], BF16) for i in range(2)]
        v_bf = [sb("v_bf%d" % i, [P, 1024], BF16) for i in range(2)]
        hi_bf = [sb("hi_bf%d" % i, [P, 1024], BF16) for i in range(2)]
        hq_raw = [sb("hq_raw%d" % i, [P, 1024], BF16) for i in range(2)]
        o1b = [sb("o1b%d" % i, [P, 2048], BF16) for i in range(2)]
        sg = [sb("sg%d" % i, [P, 2048], BF16) for i in range(2)]
        lrT = sb("lrT", [17, P], BF16)
        T4s = [sb("T4_%d" % i, [P, 2048], F32) for i in range(3)]
        TS = [[T4s[u][:, i * 512:(i + 1) * 512] for i in range(4)] for u in range(3)]
        TN = [["T%d_%d" % (u, i) for i in range(4)] for u in range(3)]
        lng, lnb = T4s[0][:, 0:1024], T4s[0][:, 1024:2048]
        U = sb("U", [P, 10752], BF16)
        QT = [U[:, (0 + u) * 512:(1 + u) * 512] for u in range(3)]
        KT = [U[:, (3 + u) * 512:(4 + u) * 512] for u in range(3)]
        KH = [U[:, (6 + u) * 512:(7 + u) * 512] for u in range(3)]
        KHt = [U[:, (9 + i) * 512:(10 + i) * 512] for i in range(3)]
        scT = [U[:, (12 + i) * 512:(13 + i) * 512] for i in range(3)]
        qT_raw = U[:, 15 * 512:16 * 512]
        kT_raw = U[:, 16 * 512:17 * 512]
        Sb = U[:, 17 * 512:21 * 512]
        T4b = T4s[1][:, :].bitcast(BF16)
        ogT = [U[:, 0:2048], T4b[:, 0:2048]]
        sigm = [U[:, 2048:4096], T4b[:, 2048:4096]]
        y_bf = U[:, 4096:5120]
        yT = U[:, 5120:6144]
        ores = [U[:, 6144 + i * 2048:6144 + (i + 1) * 2048].bitcast(F32) for i in range(2)]
        elast = [[sb("elast%d_%d" % (u, i), [P, 4], F32) for i in range(2)] for u in range(3)]
        nbl = [sb("nbl%d" % u, [P, 4], F32) for u in range(3)]
        S = sb("S", [P, 2048], F32)
        otot = sb("otot", [P, 2048], F32)
        sq = sb("sq", [P, 1024], F32)
        og = [sb("og%d" % i, [P, 2048], BF16) for i in range(2)]
        ss = [sb("ss%d" % i, [P, 4], F32) for i in range(3)]
        rstd = [sb("rstd%d" % i, [P, 4], F32) for i in range(3)]
        bst = sb("bst", [P, 12], F32)
        mv = sb("mv", [P, 2], F32)
        lnr = sb("lnr", [P, 2], F32)

        banks = [es.enter_context(nc.psum_tensor("ps%d" % i, [P, 512], F32)) for i in range(8)]
        sc = Sched(nc, es)
        free_banks = list(range(8))

        def nb():
            assert free_banks, "out of PSUM banks at emission time"
            i = free_banks.pop(0)
            return banks[i], "ps%d" % i

        def fb(*names):
            for n in names:
                i = int(n[2:])
                assert i not in free_banks
                free_banks.append(i)

        def dma(eng, out, in_, reads, writes, chan, skip_waw=False):
            sc.add(eng, lambda h: h.dma_start(out=out, in_=in_), reads, writes, chan=chan, skip_waw=skip_waw)

        def mm(out, lhsT, rhs, start, stop, reads, writes, inc):
            sc.add("pe", lambda h: h.matmul(out, lhsT=lhsT, rhs=rhs, start=start, stop=stop),
                   reads, writes, inc=inc)

        def tr(out, in_, reads, writes, inc):
            sc.add("pe", lambda h: h.transpose(out, in_, ident), reads + ["cst_b"], writes, inc=inc)

        def act(out, in_, func, reads, writes, scale=1.0, bias=0.0, accum=None):
            if accum is None:
                sc.add("act", lambda h: h.activation(out=out, in_=in_, func=func, scale=scale, bias=bias),
                       reads, writes)
            else:
                sc.add("act", lambda h: h.activation(out=out, in_=in_, func=func, scale=scale, bias=bias,
                                                     accum_out=accum), reads, writes)

        def tt(eng, out, in0, in1, op, reads, writes):
            sc.add(eng, lambda h: h.tensor_tensor(out=out, in0=in0, in1=in1, op=op), reads, writes)

        def cp(eng, out, in_, reads, writes):
            if eng == "act":
                sc.add(eng, lambda h: h.copy(out=out, in_=in_), reads, writes)
            else:
                sc.add(eng, lambda h: h.tensor_copy(out=out, in_=in_), reads, writes)

        def h3(ap):
            return ap.rearrange("p (h t) -> p h t", t=128)

        dma("sp", cst_f[:, :], cst_d[:, 384:896], [], ["cst_f"], "c0")
        dma("pool", cst_b[:, :], cst_d[:, 0:384], [], ["cst_b"], "c0b")
        for i in range(2):
            dma("pool", gkb[i][:, :], gk_d[i], [], ["gkb%d" % i], "c1_%d" % i)
            dma("sp", lbl[i][:, :], lbl_d[i], [], ["lbl%d" % i], "c2_%d" % i)
        dma("sp", gn[:, :], gn_d, [], ["gn"], "c3")
        dma("sp", sel[:, :], sel_d, [], ["sel"], "c6")
        sc.add("pool", lambda h: h.memset(lrT[:, :], 1.0), [], ["lrT"])
        sc.add("pool", lambda h: h.memset(S[:, :], 0.0), [], ["S0", "S1", "S2"])
        sc.add("pool", lambda h: h.memset(Sb, 0.0), [], ["Sb0", "Sb1", "Sb2"])
        for i in range(2):
            act(lbt[:, :], lbl[i][:, :], AF.Exp, ["lbl%d" % i], ["lbt"])
            tt("dve", lbv[i][:, :], lbt[:, 0:8], lbt[:, 8:16], ALU.add, ["lbt"], ["lbv%d" % i])
            sc.add("dve", lambda h, i=i: h.reciprocal(out=lbv[i][:, :], in_=lbv[i][:, :]),
                   ["lbv%d" % i], ["lbv%d" % i])
            tt("dve", omlv[i][:, :], lbt[:, 8:16], lbv[i][:, :], ALU.mult, ["lbt", "lbv%d" % i], ["omlv%d" % i])
            tt("dve", lbv[i][:, :], lbt[:, 0:8], lbv[i][:, :], ALU.mult, ["lbt", "lbv%d" % i], ["lbv%d" % i])

        W1P = [(0, 1040, "W1a"), (1040, 3088, "W1b"), (3088, 5136, "W1c")]
        W2P = [(0, 1040, "W2a"), (1040, 3088, "W2b")]
        W3P = [(0, 2048, "W3")]
        ALLW = ["W1a", "W1b", "W1c", "W2a", "W2b", "W3", "WBlo"]

        def wname(W, col):
            ps = W1P if W is W1 else (W2P if W is W2 else W3P)
            for c0, c1, nm in ps:
                if c0 <= col < c1:
                    return nm
            raise AssertionError

        def load_w(dst3, src, pieces, tag, kcs=None, war_all=True, nowaw=False):
            first = True
            for (p0, p1, nm) in pieces:
                c0 = p0
                pfirst = True
                while c0 < p1:
                    c1 = min(p1, c0 + 2048)
                    for kc in (kcs if kcs is not None else range(dst3.shape[1])):
                        wr = [nm] + ([x for x in ALLW if x != nm] if (first and war_all) else [])
                        dma("pool", dst3[:, kc, c0:c1], src[:, kc, c0:c1], [], wr, "w" + tag + nm,
                            skip_waw=(not first) or nowaw)
                        first = False
                        pfirst = False
                    c0 = c1

        load_w(W1, w1_d, W1P, "1", war_all=False)

        flags = {}

        def need(k):
            while len(free_banks) < k:
                yield ("nobank",)

        def run_rr(gens, prio=0):
            gens = list(gens)
            if prio and len(gens) >= 8:
                order = [gens[j] for j in (RR_ORDER if prio == 1 else RR_ORDER2)]
            else:
                order = gens + (gens[3:6] if prio and len(gens) >= 6 else [])
            pend = {}
            alive = set(id(g) for g in gens)
            while alive:
                progressed = False
                for g in order:
                    if id(g) not in alive:
                        continue
                    k = pend.get(id(g))
                    if k is not None:
                        if not flags.get(k):
                            continue
                        pend[id(g)] = None
                    try:
                        r = next(g)
                    except StopIteration:
                        alive.discard(id(g))
                        progressed = True
                        continue
                    if isinstance(r, tuple) and r[0] == "wait":
                        pend[id(g)] = r[1]
                        progressed = True
                    elif isinstance(r, tuple) and r[0] == "nobank":
                        pass
                    else:
                        progressed = True
                assert progressed, "emission deadlock"

        def dma_x(t, xf, xfn):
            dma("sp", xf, x_d[t * P:(t + 1) * P, :], [], [xfn], "x" + xfn)

        def gen_load_x(t, b, xf=None, xfn=None, do_dma=True, ceng="act"):
            if xf is None:
                xf, xfn = x_f[b][:, :], "x_f%d" % b
            if do_dma:
                dma_x(t, xf, xfn)
            cp(ceng, x_bf[b][:, :], xf, [xfn], ["x_bf%d" % b])
            yield from need(1)
            bk, bn = nb()
            bkb = bk[:, :].bitcast(BF16)
            for kc in range(8):
                tr(bkb[:, kc * P:(kc + 1) * P], x_bf[b][:, kc * P:(kc + 1) * P], ["x_bf%d" % b], [bn], inc=(kc == 7))
            cp("act", xT[b][:, :], bkb[:, :], [bn], ["xT%d" % b])
            fb(bn)
            yield

        def proj_fm(W, col0, nheads, b, m=P):
            bk, bn = nb()
            for h in range(nheads):
                for kc in range(8):
                    mm(bk[0:m, h * P:(h + 1) * P], W[:, kc, col0 + h * m:col0 + (h + 1) * m],
                       xT[b][:, kc * P:(kc + 1) * P], kc == 0, kc == 7, ["xT%d" % b, wname(W, col0)], [bn],
                       inc=(h == nheads - 1 and kc == 7))
            return bk, bn

        def proj_tm(W, col0, b):
            bk, bn = nb()
            for kc in range(8):
                mm(bk[:, :], xT[b][:, kc * P:(kc + 1) * P], W[:, kc, col0:col0 + 512], kc == 0, kc == 7,
                   ["xT%d" % b, wname(W, col0)], [bn], inc=(kc == 7))
            return bk, bn

        def decay_chain(sw, u, par, Lbuf, Lname, scale, other, oname):
            B, Bn = TS[u][1], TN[u][1]
            Dd, Dn = TS[u][3], TN[u][3]
            el, eln = elast[u][par], "elast%d_%d" % (u, par)
            if sw == 1:
                sc.add("dve", lambda h: h.tensor_tensor_scan(out=B, data0=smask, data1=Lbuf,
                                                             initial=0.0, op0=ALU.mult, op1=ALU.add),
                       [Lname, "cst_f"], [Bn])
                tot = h3(B)[:, :, 127]
            else:
                sc.add("dve", lambda h: h.tensor_tensor_scan(out=B[:, ::-1], data0=smask, data1=Lbuf[:, ::-1],
                                                             initial=0.0, op0=ALU.mult, op1=ALU.add),
                       [Lname, "cst_f"], [Bn])
                tot = h3(B)[:, :, 0]
            yield
            act(el[:, :], tot, AF.Exp, [Bn], [eln], scale=-scale)
            act(Dd, B, AF.Exp, [Bn], [Dn], scale=-scale)
            yield
            if sw == 1:
                act(B, B, AF.Exp, [Bn, eln], [Bn], scale=scale)
                yield
                el_bc = el[:, :].unsqueeze(2).broadcast_to([P, 4, P])
                tt("dve", h3(other), h3(B), el_bc, ALU.mult, [Bn, eln], [oname])
                yield
            else:
                act(nbl[u][:, :], tot, AF.Identity, [Bn], ["nbl%d" % u], scale=-scale)
                for h in range(4):
                    sc.add("act", lambda hh, h=h: hh.activation(
                        out=other[:, h * P:(h + 1) * P], in_=B[:, h * P:(h + 1) * P], func=AF.Exp, scale=scale,
                        bias=nbl[u][:, h:h + 1]), [Bn, "nbl%d" % u] + ([oname] if oname != Bn else []), [oname])
                yield
                act(B, B, AF.Exp, [Bn, eln, oname], [Bn], scale=scale)
                yield
            return (Dd, Dn), (B, Bn), (other, oname)

        def stage_A_gla(sw, key, i, t, b, W, lrcol):
            u = 0
            TA, TC = TS[u][0], TS[u][2]
            TAn, TCn = TN[u][0], TN[u][2]
            yield from need(1)
            bk, bn = nb()
            for kc in range(8):
                mm(bk[0:16, 0:P], W[:, kc, lrcol:lrcol + 16], xT[b][:, kc * P:(kc + 1) * P], kc == 0, kc == 7,
                   ["xT%d" % b, wname(W, lrcol)], [bn], inc=(kc == 7))
            cp("act", lrT[0:16, :], bk[0:16, 0:P], [bn], ["lrT"])
            fb(bn)
            yield
            yield from need(1)
            bz, bzn = nb()
            for h in range(4):
                mm(bz[:, h * P:(h + 1) * P], gkb[sw - 1][:, h * P:(h + 1) * P], lrT[:, :], True, True,
                   ["lrT", "gkb%d" % (sw - 1)], [bzn], inc=(h == 3))
            act(TA, bz[:, :], AF.Exp, [bzn], [TAn], scale=-1.0)
            fb(bzn)
            yield
            act(TA, TA, AF.Ln, [TAn], [TAn], scale=1.0, bias=1.0)
            yield
            (eq, eqn), (ek, ekn), (ekh, ekhn) = yield from decay_chain(sw, u, i % 2, TA, TAn, 1.0 / 16.0, TC, TCn)
            if i >= 1:
                yield ("wait", (key, "B", i - 1, u))
            if sw == 1:
                yield from need(1)
                bq, bqn = proj_fm(W, 4112, 4, b)
                cp("act", qT_raw, bq[:, :], [bqn], ["qT_raw"])
                fb(bqn)
                yield
                yield from need(1)
                bkk, bkn = proj_fm(W, 4624, 4, b)
                cp("act", kT_raw, bkk[:, :], [bkn], ["kT_raw"])
                fb(bkn)
                dma("sp", qT_s.ap()[t], qT_raw, ["qT_raw"], ["qT_s%d" % t], "sq")
                dma("sp", kT_s.ap()[t], kT_raw, ["kT_raw"], ["kT_s%d" % t], "sk")
                yield
            else:
                dma("sp", qT_raw, qT_s.ap()[t], ["qT_s%d" % t], ["qT_raw"], "lq")
                dma("sp", kT_raw, kT_s.ap()[t], ["kT_s%d" % t], ["kT_raw"], "lk")
            tt("dve", QT[u], qT_raw, eq, ALU.mult, ["qT_raw", eqn], ["QT%d" % u])
            yield
            tt("dve", KT[u], kT_raw, ek, ALU.mult, ["kT_raw", ekn], ["KT%d" % u])
            yield
            tt("dve", KH[u], kT_raw, ekh, ALU.mult, ["kT_raw", ekhn], ["KH%d" % u])
            yield

        def stage_A_hgrn(sw, key, i, t, b, u, W, hfcol):
            hh = u - 1
            TA, TB, TC, TD = TS[u]
            TAn, TBn, TCn, TDn = TN[u]
            yield from need(1)
            bf, bfn = proj_fm(W, hfcol + hh * 512, 4, b)
            lb_bc = lbv[sw - 1][:, hh * 4:(hh + 1) * 4].unsqueeze(2).broadcast_to([P, 4, P])
            oml_bc = omlv[sw - 1][:, hh * 4:(hh + 1) * 4].unsqueeze(2).broadcast_to([P, 4, P])
            lbn, omn = "lbv%d" % (sw - 1), "omlv%d" % (sw - 1)
            act(TA, bf[:, :], AF.Exp, [bfn], [TAn], scale=-1.0)
            yield
            act(TB, TA, AF.Ln, [TAn], [TBn], scale=1.0, bias=1.0)
            tt("pool", h3(TC), h3(TA), lb_bc, ALU.mult, [TAn, lbn], [TCn])
            yield
            tt("dve", TA, bf[:, :], TB, ALU.add, [bfn, TBn], [TAn])
            fb(bfn)
            act(TC, TC, AF.Ln, [TCn], [TCn], scale=1.0, bias=1.0)
            yield
            act(TA, TA, AF.Exp, [TAn], [TAn], scale=-1.0)
            tt("pool", TC, TB, TC, ALU.subtract, [TBn, TCn], [TCn])
            yield
            tt("pool", h3(TA), h3(TA), oml_bc, ALU.mult, [TAn, omn], [TAn])
            (eq, eqn), (ek, ekn), (ekh, ekhn) = yield from decay_chain(sw, u, i % 2, TC, TCn, 1.0, TC, TCn)
            if i >= 1:
                yield ("wait", (key, "B", i - 1, u))
            yield ("wait", (key, "hq", i))
            hqn = "hq_raw%d_%d" % (b, hh)
            tt("dve", QT[u], hq_raw[b][:, hh * 512:(hh + 1) * 512], eq, ALU.mult, [hqn, eqn], ["QT%d" % u])
            yield
            tt("dve", KT[u], TA, ek, ALU.mult, [TAn, ekn], ["KT%d" % u])
            yield
            tt("dve", KH[u], TA, ekh, ALU.mult, [TAn, ekhn], ["KH%d" % u])
            yield

        def stage_B(sw, key, i, t, b, u):
            dv = 256 if u == 0 else 128
            s0 = 0 if u == 0 else 1024 + (u - 1) * 512
            if u == 0:
                V, vname = v_bf[b][:, :], "v_bf%d" % b
            else:
                V, vname = hi_bf[b][:, (u - 1) * 512:u * 512], "hi_bf%d" % b
            Sn, Sbn = "S%d" % u, "Sb%d" % u
            qt, kt, kh, kht, sct, el = QT[u], KT[u], KH[u], KHt[u], scT[u], elast[u][i % 2]
            qn, kn, khn, khtn, scn, eln = ("QT%d" % u, "KT%d" % u, "KH%d" % u, "KHt%d" % u, "scT%d" % u,
                                           "elast%d_%d" % (u, i % 2))
            yield from need(2)
            bk, bn = nb()
            for h in range(4):
                mm(bk[:, h * P:(h + 1) * P], kt[:, h * P:(h + 1) * P], qt[:, h * P:(h + 1) * P], True, True,
                   [kn, qn], [bn], inc=(h == 3))
            bk2, bn2 = nb()
            bkb = bk2[:, :].bitcast(BF16)
            for h in range(4):
                tr(bkb[:, h * P:(h + 1) * P], kh[:, h * P:(h + 1) * P], [khn], [bn2], inc=(h == 3))
            tri_bc = tri[sw - 1].unsqueeze(1).broadcast_to([P, 4, P])
            tt("dve", h3(sct), h3(bk[:, :]), tri_bc, ALU.mult, [bn, "cst_b"], [scn])
            cp("act", kht, bkb[:, 0:512], [bn2], [khtn])
            fb(bn, bn2)
            yield
            nbk = 2 if u == 0 else 1
            yield from need(nbk)
            obanks = [nb() for _ in range(nbk)]
            for h in range(4):
                ob, obn = obanks[(h * dv) // 512]
                oc = (h * dv) % 512
                mm(ob[:, oc:oc + dv], sct[:, h * P:(h + 1) * P], V[:, h * dv:(h + 1) * dv], True, False,
                   [scn, vname], [obn], inc=False)
                mm(ob[:, oc:oc + dv], qt[:, h * P:(h + 1) * P], Sb[:, s0 + h * dv:s0 + (h + 1) * dv], False, sw == 1,
                   [qn, Sbn], [obn], inc=(sw == 1))
                if sw == 2:
                    mm(ob[:, oc:oc + dv], ident, o1b[b][:, s0 + h * dv:s0 + (h + 1) * dv], False, True,
                       ["cst_b", "o1b%d_%d" % (b, u)], [obn], inc=True)
            flags[(key, "B", i, u)] = True
            yield
            ucols = slice(s0, s0 + 4 * dv)
            for j, (ob, obn) in enumerate(obanks):
                cols = slice(s0 + j * 512, s0 + (j + 1) * 512)
                if sw == 1:
                    cp("act", o1b[b][:, cols], ob[:, :], [obn], ["o1b%d_%d" % (b, u)])
                else:
                    cp("act", otot[:, cols], ob[:, :], [obn], ["otot%d" % u])
                fb(obn)
            if sw == 1:
                dma("sp", o1_s.ap()[t][:, ucols], o1b[b][:, ucols], ["o1b%d_%d" % (b, u)], ["o1_s%d_%d" % (t, u)],
                    "so1%d_%d" % (b, u))
            yield
            yield from need(nbk)
            pbanks = [nb() for _ in range(nbk)]
            for h in range(4):
                pb, pbn = pbanks[(h * dv) // 512]
                pc = (h * dv) % 512
                mm(pb[:, pc:pc + dv], kht[:, h * P:(h + 1) * P], V[:, h * dv:(h + 1) * dv], True, True,
                   [khtn, vname], [pbn], inc=True)
            yield
            for h in range(4):
                pb, pbn = pbanks[(h * dv) // 512]
                pc = (h * dv) % 512
                scol = slice(s0 + h * dv, s0 + (h + 1) * dv)
                sc.add("dve", lambda hh, scol=scol, pb=pb, pc=pc, h=h, el=el: hh.scalar_tensor_tensor(
                    out=S[:, scol], in0=S[:, scol], scalar=el[:, h:h + 1], in1=pb[:, pc:pc + dv],
                    op0=ALU.mult, op1=ALU.add), [Sn, eln, pbn], [Sn])
                if h % 2 == 1:
                    yield
            fb(*[pbn for (_, pbn) in pbanks])
            cp("pool", Sb[:, s0:s0 + 4 * dv], S[:, s0:s0 + 4 * dv], [Sn], [Sbn])
            yield
            if sw == 2:
                on = "otot%d" % u
                act(sq[:, 0:4 * dv], otot[:, ucols], AF.Square, [on], ["sq"])
                sc.add("dve", lambda h: h.tensor_reduce(
                    out=ss[u][:, :], in_=sq[:, 0:4 * dv].rearrange("p (h e) -> p h e", e=dv),
                    axis=mybir.AxisListType.X, op=ALU.add), ["sq"], ["ss%d" % u])
                yield
                act(rstd[u][:, :], ss[u][:, :], AF.Ln, ["ss%d" % u], ["rstd%d" % u], scale=1.0 / dv, bias=RMS_EPS)
                act(rstd[u][:, :], rstd[u][:, :], AF.Exp, ["rstd%d" % u], ["rstd%d" % u], scale=-0.5)
                yield
                r_bc = rstd[u][:, :].unsqueeze(2).broadcast_to([P, 4, dv])
                o3 = otot[:, ucols].rearrange("p (h e) -> p h e", e=dv)
                g3 = og[b][:, ucols].rearrange("p (h e) -> p h e", e=dv)
                for h in range(4):
                    sc.add("act", lambda hh, h=h: hh.activation(
                        out=og[b][:, s0 + h * dv:s0 + (h + 1) * dv], in_=otot[:, s0 + h * dv:s0 + (h + 1) * dv],
                        func=AF.Identity, scale=rstd[u][:, h:h + 1]), [on, "rstd%d" % u], ["og%d_%d" % (b, u)])
                dma("sp", og_s.ap()[t][:, ucols], og[b][:, ucols], ["og%d_%d" % (b, u)], ["og_s%d_%d" % (t, u)],
                    "sog%d_%d" % (b, u))
                yield

        def run_sweep(sw, key, order, pre_T, gen_Tv, after_proj, W, lrcol, hfcol):
            n = len(order)
            xs = lambda j: (x_f[j % 3][:, :], "x_f%d" % (j % 3))
            dma_x(order[0], *xs(0))
            if n > 1:
                dma_x(order[1], *xs(1))
            ceng = "dve" if sw == 1 else "act"
            run_rr([gen_load_x(order[0], 0, *xs(0), do_dma=False, ceng=ceng)])
            for i in range(n + 1):
                gens = []
                if i + 2 < n:
                    dma_x(order[i + 2], *xs(i + 2))
                if i < n:
                    t, b = order[i], i % 2
                    pre_T(t, b, i)
                    gens += [stage_A_gla(sw, key, i, t, b, W, lrcol),
                             stage_A_hgrn(sw, key, i, t, b, 1, W, hfcol),
                             stage_A_hgrn(sw, key, i, t, b, 2, W, hfcol)]
                if i >= 1:
                    pt, pb = order[i - 1], (i - 1) % 2
                    gens += [stage_B(sw, key, i - 1, pt, pb, 0), stage_B(sw, key, i - 1, pt, pb, 1),
                             stage_B(sw, key, i - 1, pt, pb, 2)]
                if i == n:
                    after_proj()
                if i < n:
                    gens.append(gen_Tv(t, b, i))
                if i + 1 < n:
                    gens.append(gen_load_x(order[i + 1], (i + 1) % 2, *xs(i + 1), do_dma=False, ceng=ceng))
                run_rr(gens, prio=sw if i < n else 0)

        def pre_T1(t, b, i):
            pass

        def gen_Tv1(t, b, i):
            for j in range(2):
                yield from need(1)
                bv, bvn = proj_tm(W1, 1040 + j * 512, b)
                act(v_bf[b][:, j * 512:(j + 1) * 512], bv[:, :], AF.Identity, [bvn], ["v_bf%d" % b], scale=Q_SCALE)
                fb(bvn)
                yield
            for j in range(2):
                yield from need(1)
                bv, bvn = proj_tm(W1, 2064 + j * 512, b)
                act(hi_bf[b][:, j * 512:(j + 1) * 512], bv[:, :], AF.Identity, [bvn], ["hi_bf%d" % b], scale=Q_SCALE)
                fb(bvn)
                yield
            yield from need(2)
            hb = [proj_fm(W1, 3088 + hh * 512, 4, b) for hh in range(2)]
            for hh in range(2):
                act(hq_raw[b][:, hh * 512:(hh + 1) * 512], hb[hh][0][:, :], AF.Silu, [hb[hh][1]],
                    ["hq_raw%d_%d" % (b, hh)])
                fb(hb[hh][1])
            flags[("s1", "hq", i)] = True
            dma("sp", v_s.ap()[t], v_bf[b][:, :], ["v_bf%d" % b], ["v_s%d" % t], "sv%d" % b)
            dma("sp", hi_s.ap()[t], hi_bf[b][:, :], ["hi_bf%d" % b], ["hi_s%d" % t], "shi%d" % b)
            dma("sp", hq_s.ap()[t], hq_raw[b][:, :], ["hq_raw%d_0" % b, "hq_raw%d_1" % b], ["hq_s%d" % t],
                "shq%d" % b)
            yield

        def after1():
            load_w(W2, w2_d, W2P, "2")

        if stop is None or stop >= 1:
            run_sweep(1, "s1", list(range(NT)), pre_T1, gen_Tv1, after1, W1, 0, 16)

        def exchange():
            dma("sp", cc_in.ap(), S[:, :], ["S0", "S1", "S2"], ["cc_in"], "xs")
            sc.add("pool", lambda h: h.collective_compute(
                "AllGather", ALU.bypass, replica_groups=[[0, 1], [2, 3], [4, 5], [6, 7]],
                ins=[cc_in.ap().opt()], outs=[cc_out.ap().opt()]), ["cc_in"], ["cc_out"], chan="cc", cc=True)
            dma("sp", otot[:, :], cc_out.ap()[0:P, :], ["cc_out"], ["otot0", "otot1", "otot2"], "xl0")
            sc.add("dve", lambda h: h.tensor_scalar(out=S[:, :], in0=otot[:, :], scalar1=sel[:, 0:1], scalar2=None,
                                                    op0=ALU.mult), ["otot0", "otot1", "otot2", "sel"], ["S0", "S1", "S2"])
            dma("sp", otot[:, :], cc_out.ap()[P:2 * P, :], ["cc_out"], ["otot0", "otot1", "otot2"], "xl1")
            sc.add("dve", lambda h: h.scalar_tensor_tensor(out=S[:, :], in0=otot[:, :], scalar=sel[:, 1:2],
                                                           in1=S[:, :], op0=ALU.mult, op1=ALU.add),
                   ["otot0", "otot1", "otot2", "sel", "S0", "S1", "S2"], ["S0", "S1", "S2"])
            cp("act", Sb, S[:, :], ["S0", "S1", "S2"], ["Sb0", "Sb1", "Sb2"])

        def pre_T2(t, b, i):
            dma("sp", v_bf[b][:, :], v_s.ap()[t], ["v_s%d" % t], ["v_bf%d" % b], "lv%d" % b)
            dma("sp", hi_bf[b][:, :], hi_s.ap()[t], ["hi_s%d" % t], ["hi_bf%d" % b], "lhi%d" % b)
            dma("sp", hq_raw[b][:, :], hq_s.ap()[t], ["hq_s%d" % t], ["hq_raw%d_0" % b, "hq_raw%d_1" % b],
                "lhq%d" % b)
            dma("sp", o1b[b][:, :], o1_s.ap()[t], ["o1_s%d_%d" % (t, u) for u in range(3)],
                ["o1b%d_%d" % (b, u) for u in range(3)], "lo1%d" % b)
            flags[("s2", "hq", i)] = True
            if i == 1:
                load_w(WB, wb_d, [(0, D, "WBhi")], "bh", kcs=range(8, 24), war_all=False)

        def gen_Tv2(t, b, i):
            for j in range(4):
                yield from need(1)
                gbk, gbn = proj_tm(W2, 1040 + j * 512, b)
                cp("act", sg[b][:, j * 512:(j + 1) * 512], gbk[:, :], [gbn], ["sg%d" % b])
                fb(gbn)
                yield
            dma("sp", g_s.ap()[t], sg[b][:, :], ["sg%d" % b], ["g_s%d" % t], "sg_s%d" % b)
            yield

        def after2():
            load_w(W3, w3_d, W3P, "3")
            load_w(WB, wb_d, [(0, D, "WBlo")], "b", kcs=range(8), war_all=False, nowaw=True)

        def gen_P1(t, b):
            yield from need(4)
            gb = [proj_tm(W3, j * 512, b) for j in range(4)]
            for j in range(4):
                act(sigm[b][:, j * 512:(j + 1) * 512], gb[j][0][:, :], AF.Sigmoid, [gb[j][1]], ["sigm%d" % b])
                fb(gb[j][1])
            act(sgt, sg[b][:, :], AF.Sigmoid, ["sg%d" % b], ["sgt"])
            yield
            tt("dve", sg[b][:, :], sg[b][:, :], sgt, ALU.mult, ["sg%d" % b, "sgt"], ["sg%d" % b])
            tt("dve", og[b][:, :], og[b][:, :], sg[b][:, :], ALU.mult, ["og%d" % b, "sg%d" % b], ["og%d" % b])
            yield
            for half in range(2):
                yield from need(1)
                bk, bn = nb()
                bkb = bk[:, :].bitcast(BF16)
                for c in range(8):
                    cc = half * 8 + c
                    tr(bkb[:, c * P:(c + 1) * P], og[b][:, cc * P:(cc + 1) * P], ["og%d" % b], [bn], inc=(c == 7))
                cp("act" if half == 0 else "dve", ogT[b][:, half * 1024:(half + 1) * 1024], bkb[:, :], [bn],
                   ["ogT%d_%d" % (b, half)])
                fb(bn)
                yield

        x_f3 = [x_f[0][:, :], x_f[1][:, :], x_f[2][:, :], sq[:, :]]
        x_f3n = ["x_f0", "x_f1", "x_f2", "sq"]
        y_bf2 = [y_bf, T4s[2][:, :].bitcast(BF16)[:, 0:1024]]
        sgt = T4s[2][:, :].bitcast(BF16)[:, 1024:3072]

        def gen_P2a(t, b):
            yb = y_bf2[b]
            for j in range(2):
                yield from need(2)
                bg, bgn = nb()
                for kc in range(8):
                    mm(bg[:, :], ogT[b][:, kc * P:(kc + 1) * P], WB[:, kc, j * 512:(j + 1) * 512], kc == 0, kc == 7,
                       ["ogT%d_0" % b, "WBlo"], [bgn], inc=(kc == 7))
                bh, bhn = nb()
                for kc in range(8):
                    mm(bh[:, :], ogT[b][:, (8 + kc) * P:(9 + kc) * P], WB[:, 8 + kc, j * 512:(j + 1) * 512],
                       kc == 0, kc == 7, ["ogT%d_1" % b, "WBhi"], [bhn], inc=(kc == 7))
                c0 = slice(j * 512, (j + 1) * 512)
                c1 = slice(1024 + j * 512, 1024 + (j + 1) * 512)
                tt("dve", S[:, c0], bg[:, :], sigm[b][:, c0], ALU.mult, [bgn, "sigm%d" % b], ["t1_%d" % j])
                tt("dve", S[:, c1], bh[:, :], sigm[b][:, c1], ALU.mult, [bhn, "sigm%d" % b], ["t2_%d" % j])
                fb(bgn, bhn)
                yield
                tt("pool" if j == 0 else "dve", yb[:, c0], S[:, c0], S[:, c1], ALU.add, ["t1_%d" % j, "t2_%d" % j],
                   ["y_bf%d_%d" % (b, j)])
                yield

        def gen_P2b(t, b):
            yb = y_bf2[b]
            xf, xfn = x_f3[t % 4], x_f3n[t % 4]
            yield from need(1)
            bk, bn = nb()
            bkb = bk[:, :].bitcast(BF16)
            for c in range(8):
                tr(bkb[:, c * P:(c + 1) * P], yb[:, c * P:(c + 1) * P], ["y_bf%d_%d" % (b, c // 4)], [bn], inc=(c == 7))
            cp("act", yT, bkb[:, :], [bn], ["yT"])
            fb(bn)
            yield
            for j in range(2):
                yield from need(1)
                bo, bon = nb()
                for kc in range(8):
                    mm(bo[:, :], yT[:, kc * P:(kc + 1) * P], WB[:, 16 + kc, j * 512:(j + 1) * 512],
                       kc == 0, kc == 7, ["yT", "WBhi"], [bon], inc=(kc == 7))
                c0 = slice(j * 512, (j + 1) * 512)
                sc.add("dve", lambda h, c0=c0, bo=bo, xf=xf: h.scalar_tensor_tensor(
                    out=otot[:, c0], in0=xf[:, c0], scalar=ALPHA, in1=bo[:, :], op0=ALU.mult, op1=ALU.add),
                    [xfn, bon], ["r%d" % j])
                sc.add("dve", lambda h, c0=c0, j=j: h.bn_stats(out=bst[:, j * 6:(j + 1) * 6], in_=otot[:, c0]),
                       ["r%d" % j], ["bst%d" % j])
                fb(bon)
                yield
            sc.add("dve", lambda h: h.bn_aggr(out=mv[:, :], in_=bst[:, :]), ["bst0", "bst1"], ["mv"])
            act(lnr[:, 0:1], mv[:, 1:2], AF.Ln, ["mv"], ["lnr"], scale=1.0, bias=LN_EPS)
            act(lnr[:, 0:1], lnr[:, 0:1], AF.Exp, ["lnr"], ["lnr"], scale=-0.5)
            sc.add("dve", lambda h: h.scalar_tensor_tensor(
                out=lnr[:, 1:2], in0=mv[:, 0:1], scalar=-1.0, in1=lnr[:, 0:1], op0=ALU.mult, op1=ALU.mult),
                ["mv", "lnr"], ["lnr"])
            yield
            sc.add("act", lambda h: h.activation(out=otot[:, 0:1024], in_=otot[:, 0:1024], func=AF.Identity,
                                                 scale=lnr[:, 0:1], bias=lnr[:, 1:2]),
                   ["r0", "r1", "lnr"], ["r0", "r1"])
            yield
            tt("pool", otot[:, 0:1024], otot[:, 0:1024], lng, ALU.mult, ["r0", "r1", "lng"], ["r0", "r1"])
            yield
            tt("dve", ores[b], otot[:, 0:1024], lnb, ALU.add, ["r0", "r1", "lnb"], ["ores%d" % b])
            dma("sp", out_d[t * P:(t + 1) * P, :], ores[b], ["ores%d" % b], [], "out%d" % b)
            yield

        def sweep3():
            sc.barrier()
            dma("sp", lng, lng_d, [], ["lng"], "c4")
            dma("sp", lnb, lnb_d, [], ["lnb"], "c5")
            for kc in range(16):
                gcol = (kc % 2) if kc < 8 else 2
                wr = "WBlo" if kc < 8 else "WBhi"
                sc.add("dve", lambda h, kc=kc, gcol=gcol: h.tensor_scalar(
                    out=WB[:, kc, :], in0=WB[:, kc, :], scalar1=gn[:, gcol:gcol + 1], scalar2=None, op0=ALU.mult),
                    [wr, "gn"], [wr])
            def p1_loads(t, b):
                dma_x(t, x_f3[t % 4], x_f3n[t % 4])
                dma("sp", og[b][:, :], og_s.ap()[t], ["og_s%d_%d" % (t, u) for u in range(3)], ["og%d" % b],
                    "log%d" % b)
                dma("sp", sg[b][:, :], g_s.ap()[t], ["g_s%d" % t], ["sg%d" % b], "lg%d" % b)

            p1_loads(0, 0)
            for k in range(NT + 2):
                gens = []
                if k + 1 < NT:
                    p1_loads(k + 1, (k + 1) % 2)
                if k - 2 >= 0:
                    gens.append(gen_P2b(k - 2, (k - 2) % 2))
                if 0 <= k - 1 < NT:
                    gens.append(gen_P2a(k - 1, (k - 1) % 2))
                if k < NT:
                    gens.append(seq(gen_load_x(k, k % 2, x_f3[k % 4], x_f3n[k % 4], do_dma=False), gen_P1(k, k % 2)))
                run_rr(gens)

        def seq(*gens):
            for g in gens:
                yield from g

        if stop is None or stop >= 2:
            exchange()
            run_sweep(2, "s2", list(range(NT - 1, -1, -1)), pre_T2, gen_Tv2, after2, W2, 0, 16)
        if stop is None or stop >= 3:
            sweep3()

        block = es.enter_context(nc.Block())
        sc.emit(block)
    return nc


_COLS = dict(a_q=0, a_k=512, a_v=1024, a_gate=2048, lr_f=3072, lr_b=3088, h_q=3104, h_f_f=4128,
             h_f_b=5152, h_i=6176, h_gate=7200, m_gla=8224, m_hgrn=9248)


def _tile_rows(w):
    k = w.shape[0] // P
    return np.ascontiguousarray(w.reshape(k, P, w.shape[1]).transpose(1, 0, 2))


def make_in_maps(x, w_in, gla_gk_up_f, gla_gk_bias_f, gla_gk_up_b, gla_gk_bias_b, gla_norm_g,
                 hgrn_lb_logits_f, hgrn_lb_logits_b, hgrn_norm_g, w_branch_gla, w_branch_hgrn,
                 w_out, ln_g, ln_b):
    f32 = np.float32
    x = np.asarray(x, f32)
    win = np.asarray(w_in, f32)[0]
    c = _COLS

    def cols(name, n):
        return win[:, c[name]:c[name] + n]

    dirs = {
        "f": dict(lr=cols("lr_f", 16), hf=cols("h_f_f", 1024),
                  gk=np.concatenate([np.asarray(gla_gk_up_f, f32)[0], np.asarray(gla_gk_bias_f, f32)[0][None]], 0),
                  lbl=np.asarray(hgrn_lb_logits_f, f32)),
        "b": dict(lr=cols("lr_b", 16), hf=cols("h_f_b", 1024),
                  gk=np.concatenate([np.asarray(gla_gk_up_b, f32)[0], np.asarray(gla_gk_bias_b, f32)[0][None]], 0),
                  lbl=np.asarray(hgrn_lb_logits_b, f32)),
    }

    def lbl_layout(l):
        return np.ascontiguousarray(l.reshape(2, 8, P).transpose(2, 0, 1).reshape(P, 16))

    w3 = _tile_rows(np.concatenate([cols("m_gla", 1024), cols("m_hgrn", 1024)], 1))
    wb = _tile_rows(np.concatenate([np.asarray(w_branch_gla, f32)[0], np.asarray(w_branch_hgrn, f32)[0],
                                    np.asarray(w_out, f32)[0]], 0))
    gng = np.asarray(gla_norm_g, f32)[0]
    gn = np.stack([gng[0:128], gng[128:256], np.asarray(hgrn_norm_g, f32)[0]], 1)
    lng = np.ascontiguousarray(np.broadcast_to(np.asarray(ln_g, f32)[0][None], (P, D)))
    lnb = np.ascontiguousarray(np.broadcast_to(np.asarray(ln_b, f32)[0][None], (P, D)))
    ii = np.arange(P)
    smask = np.ones((P, 512), f32)
    smask[:, 0::128] = 0.0
    cst = np.concatenate([np.eye(P, dtype=f32), (ii[:, None] <= ii[None, :]).astype(f32),
                          (ii[:, None] >= ii[None, :]).astype(f32), smask], 1)

    per_dir = {}
    for d1, d2 in (("f", "b"), ("b", "f")):
        w1 = _tile_rows(np.concatenate([dirs[d1]["lr"], dirs[d1]["hf"], cols("a_v", 1024), cols("h_i", 1024),
                                        cols("h_q", 1024), cols("a_q", 512), cols("a_k", 512)], 1))
        w2 = _tile_rows(np.concatenate([dirs[d2]["lr"], dirs[d2]["hf"], cols("a_gate", 1024),
                                        cols("h_gate", 1024)], 1))
        per_dir[d1] = dict(w1=w1, w2=w2, gk1=dirs[d1]["gk"], gk2=dirs[d2]["gk"],
                           lbl1=lbl_layout(dirs[d1]["lbl"]), lbl2=lbl_layout(dirs[d2]["lbl"]))

    in_maps = []
    for core in range(NCORES):
        b, half = core // 2, core % 2
        TL = x.shape[1] // 2
        xl = x[b, half * TL:(half + 1) * TL]
        if half == 1:
            xl = xl[::-1]
        pd = per_dir["f" if half == 0 else "b"]
        sel = np.zeros((P, 2), f32)
        sel[:, 1 - half] = 1.0
        in_maps.append(dict(x=np.ascontiguousarray(xl), w1=pd["w1"], w2=pd["w2"], w3=w3, wb=wb,
                            gk1=pd["gk1"], gk2=pd["gk2"], lbl1=pd["lbl1"], lbl2=pd["lbl2"], gn=gn,
                            lng=lng, lnb=lnb, sel=sel, cst=cst))

    return in_maps


def assemble(outs, T):
    out = np.empty((4, T, D), np.float32)
    TL = T // 2
    for core in range(NCORES):
        b, half = core // 2, core % 2
        o = np.asarray(outs[core], np.float32)
        if half == 1:
            o = o[::-1]
        out[b, half * TL:(half + 1) * TL] = o
    return out


def kernel(**inputs):
    in_maps = make_in_maps(**inputs)
    nc = build_program()
    res = run_bass_kernel_spmd(nc, in_maps, core_ids=list(range(NCORES)))
    return assemble([res.results[c]["out"] for c in range(NCORES)], 4096)
```

```python
import math
from contextlib import ExitStack

import numpy as np
import concourse.bass as bass
import concourse.mybir as mybir
from concourse.bass_utils import run_bass_kernel_spmd

F32 = mybir.dt.float32
BF16 = mybir.dt.bfloat16
AF = mybir.ActivationFunctionType
ALU = mybir.AluOpType

P = 128
D = 1024
NT = 16
NCORES = 8
W1C, W2C, W3C = 5136, 3088, 2048
WFLAT = 8 * W1C
Q_SCALE = 128.0 ** -0.5
LN_QS = math.log(Q_SCALE)
ALPHA = 2.0 ** 0.25
RMS_EPS = 1e-6
LN_EPS = 1e-5
SAME_ENGINE_WAITS = True
PRIO = 3
RR_ORDER = [0, 1, 2, 3, 4, 5, 6, 7, 3]
RR_ORDER2 = [0, 1, 2, 3, 4, 5, 6, 3, 7, 0]
RR3 = [0, 1, 1, 2]


class Sched:
    def __init__(self, nc, es):
        self.nc = nc
        self.es = es
        self.ops = []
        self.last_w = {}
        self.readers = {}
        self.eng_sem = {}
        self.eng_cnt = {}
        self.chan = {}
        self.pending = {}
        self.last_access = {}
        self.bar_deps = []
        for e in ("pe", "act", "dve", "pool"):
            self.eng_sem[e] = es.enter_context(nc.semaphore("s_" + e))
            self.eng_cnt[e] = 0

    def _deps(self, reads, writes, skip_waw=False):
        deps = []
        for r in reads:
            if r in self.last_w:
                deps.append((self.last_w[r], "raw"))
        for w in writes:
            if w in self.last_w and not skip_waw:
                deps.append((self.last_w[w], "waw"))
            for rd in self.readers.get(w, ()):
                deps.append((rd, "war"))
        return deps

    def add(self, eng, fn, reads=(), writes=(), chan=None, inc=True, cc=False, skip_waw=False):
        idx = len(self.ops)
        deps = self._deps(reads, writes, skip_waw) + list(self.bar_deps)
        for r in list(reads) + list(writes):
            if r.startswith("ps"):
                la = self.last_access.get(r)
                if la is not None and self.ops[la]["eng"] != eng:
                    deps.append((la, "psx"))
                self.last_access[r] = idx
        op = dict(eng=eng, fn=fn, deps=deps, done=None, kind="c", own=(chan is not None or inc))
        if chan is not None:
            if chan not in self.chan:
                self.chan[chan] = [self.es.enter_context(self.nc.semaphore("d_" + chan)), 0]
            c = self.chan[chan]
            step = 1 if cc else 16
            c[1] += step
            op["done"] = (c[0], c[1])
            op["kind"] = "cc" if cc else "dma"
        elif inc:
            self.eng_cnt[eng] += 1
            op["done"] = (self.eng_sem[eng], self.eng_cnt[eng])
            for po in self.pending.get(eng, ()):
                po["done"] = op["done"]
            self.pending[eng] = []
        else:
            self.pending.setdefault(eng, []).append(op)
        self.ops.append(op)
        for w in writes:
            self.last_w[w] = idx
            self.readers[w] = []
        for r in reads:
            self.readers.setdefault(r, []).append(idx)
        return idx

    def barrier(self):
        last = {}
        for i, o in enumerate(self.ops):
            if o["done"] is not None:
                last[(o["eng"], id(o["done"][0]))] = i
        self.bar_deps = [(i, "bar") for i in last.values()]

    def emit(self, block):
        nc = self.nc
        handles = {"pe": "tensor", "act": "scalar", "dve": "vector", "pool": "gpsimd", "sp": "sync"}
        for eng, hname in handles.items():
            my = [o for o in self.ops if o["eng"] == eng]

            def body(h, my=my, eng=eng):
                waited = {}
                for op in my:
                    needs = {}
                    for (di, kind) in op["deps"]:
                        dop = self.ops[di]
                        if dop["done"] is None:
                            continue
                        if dop["eng"] == eng and dop["kind"] == "c":
                            if not SAME_ENGINE_WAITS or eng == "pe":
                                continue
                        sem, val = dop["done"]
                        key = id(sem)
                        if key not in needs or needs[key][1] < val:
                            needs[key] = (sem, val)
                    for key, (sem, val) in needs.items():
                        if waited.get(key, 0) >= val:
                            continue
                        waited[key] = val
                        h.wait_ge(sem, val)
                    ins = op["fn"](h)
                    if op["done"] is not None and op["own"]:
                        sem, val = op["done"]
                        if op["kind"] == "dma":
                            ins.then_inc(sem, 16)
                        elif op["kind"] == "cc":
                            ins.then_inc(sem)
                        else:
                            ins.then_inc(sem, 1)
                if eng == "sp":
                    for name, (sem, val) in self.chan.items():
                        h.wait_ge(sem, val)

            getattr(block, hname)(body)


def build_program(NT=NT, stop=None):
    nc = bass.Bass("TRN2", target_bir_lowering=False)
    dt_in = lambda name, shape: nc.dram_tensor(name, shape, F32, kind="ExternalInput").ap()
    x_d = dt_in("x", [NT * P, D])
    w1_d = dt_in("w1", [P, 8, W1C])
    w2_d = dt_in("w2", [P, 8, W2C])
    w3_d = dt_in("w3", [P, 8, W3C])
    wb_d = dt_in("wb", [P, 24, D])
    gk_d = [dt_in("gk1", [17, 512]), dt_in("gk2", [17, 512])]
    lbl_d = [dt_in("lbl1", [P, 16]), dt_in("lbl2", [P, 16])]
    gn_d = dt_in("gn", [P, 3])
    lng_d = dt_in("lng", [P, D])
    lnb_d = dt_in("lnb", [P, D])
    sel_d = dt_in("sel", [P, 2])
    cst_d = dt_in("cst", [P, 3 * 128 + 512])
    out_d = nc.dram_tensor("out", [NT * P, D], F32, kind="ExternalOutput").ap()

    qT_s = nc.dram_tensor("qT_s", [NT, P, 512], BF16)
    kT_s = nc.dram_tensor("kT_s", [NT, P, 512], BF16)
    hq_s = nc.dram_tensor("hq_s", [NT, P, 1024], BF16)
    v_s = nc.dram_tensor("v_s", [NT, P, 1024], BF16)
    hi_s = nc.dram_tensor("hi_s", [NT, P, 1024], BF16)
    o1_s = nc.dram_tensor("o1_s", [NT, P, 2048], BF16)
    og_s = nc.dram_tensor("og_s", [NT, P, 2048], BF16)
    g_s = nc.dram_tensor("g_s", [NT, P, 2048], BF16)
    cc_in = nc.dram_tensor("cc_in", [P, 2048], F32)
    cc_out = nc.dram_tensor("cc_out", [2 * P, 2048], F32)

    with ExitStack() as es:
        sb = lambda name, shape, dt: es.enter_context(nc.sbuf_tensor(name, shape, dt))
        wflat = sb("wflat", [P, WFLAT], BF16)
        W1 = wflat[:, 0:8 * W1C].rearrange("p (k c) -> p k c", c=W1C)
        W2 = wflat[:, 0:8 * W2C].rearrange("p (k c) -> p k c", c=W2C)
        W3 = wflat[:, 0:8 * W3C].rearrange("p (k c) -> p k c", c=W3C)
        WB = wflat[:, WFLAT - 24 * D:WFLAT].rearrange("p (k c) -> p k c", c=D)

        cst_f = sb("cst_f", [P, 512], F32)
        cst_b = sb("cst_b", [P, 3 * 128], BF16)
        ident = cst_b[:, 0:128]
        tri = [cst_b[:, 128:256], cst_b[:, 256:384]]
        smask = cst_f[:, 0:512]
        gkb = [sb("gk1b", [17, 512], BF16), sb("gk2b", [17, 512], BF16)]
        lbl = [sb("lbl1s", [P, 16], F32), sb("lbl2s", [P, 16], F32)]
        lbv = [sb("lbv1", [P, 8], F32), sb("lbv2", [P, 8], F32)]
        omlv = [sb("omlv1", [P, 8], F32), sb("omlv2", [P, 8], F32)]
        lbt = sb("lbt", [P, 16], F32)
        gn = sb("gns", [P, 3], F32)
        sel = sb("sels", [P, 2], F32)

        x_f = [sb("x_f%d" % i, [P, D], F32) for i in range(3)]
        x_bf = [sb("x_bf%d" % i, [P, D], BF16) for i in range(2)]
        xT = [sb("xT%d" % i, [P, D], BF16) for i in range(2)]
        v_bf = [sb("v_bf%d" % i, [P, 1024], BF16) for i in range(2)]
        hi_bf = [sb("hi_bf%d" % i, [P, 1024], BF16) for i in range(2)]
        hq_raw = [sb("hq_raw%d" % i, [P, 1024], BF16) for i in range(2)]
        o1b = [sb("o1b%d" % i, [P, 2048], BF16) for i in range(2)]
        sg = [sb("sg%d" % i, [P, 2048], BF16) for i in range(2)]
        lrT = sb("lrT", [17, P], BF16)
        T4s = [sb("T4_%d" % i, [P, 2048], F32) for i in range(3)]
        TS = [[T4s[u][:, i * 512:(i + 1) * 512] for i in range(4)] for u in range(3)]
        TN = [["T%d_%d" % (u, i) for i in range(4)] for u in range(3)]
        lng, lnb = T4s[0][:, 0:1024], T4s[0][:, 1024:2048]
        U = sb("U", [P, 10752], BF16)
        QT = [U[:, (0 + u) * 512:(1 + u) * 512] for u in range(3)]
        KT = [U[:, (3 + u) * 512:(4 + u) * 512] for u in range(3)]
        KH = [U[:, (6 + u) * 512:(7 + u) * 512] for u in range(3)]
        KHt = [U[:, (9 + i) * 512:(10 + i) * 512] for i in range(3)]
        scT = [U[:, (12 + i) * 512:(13 + i) * 512] for i in range(3)]
        qT_raw = U[:, 15 * 512:16 * 512]
        kT_raw = U[:, 16 * 512:17 * 512]
        Sb = U[:, 17 * 512:21 * 512]
        T4b = T4s[1][:, :].bitcast(BF16)
        ogT = [U[:, 0:2048], T4b[:, 0:2048]]
        sigm = [U[:, 2048:4096], T4b[:, 2048:4096]]
        y_bf = U[:, 4096:5120]
        yT = U[:, 5120:6144]
        ores = [U[:, 6144 + i * 2048:6144 + (i + 1) * 2048].bitcast(F32) for i in range(2)]
        elast = [[sb("elast%d_%d" % (u, i), [P, 4], F32) for i in range(2)] for u in range(3)]
        nbl = [sb("nbl%d" % u, [P, 4], F32) for u in range(3)]
        S = sb("S", [P, 2048], F32)
        otot = sb("otot", [P, 2048], F32)
        sq = sb("sq", [P, 1024], F32)
        og = [sb("og%d" % i, [P, 2048], BF16) for i in range(2)]
        ss = [sb("ss%d" % i, [P, 4], F32) for i in range(3)]
        rstd = [sb("rstd%d" % i, [P, 4], F32) for i in range(3)]
        bst = sb("bst", [P, 12], F32)
        mv = sb("mv", [P, 2], F32)
        lnr = sb("lnr", [P, 2], F32)

        banks = [es.enter_context(nc.psum_tensor("ps%d" % i, [P, 512], F32)) for i in range(8)]
        sc = Sched(nc, es)
        free_banks = list(range(8))

        def nb():
            assert free_banks, "out of PSUM banks at emission time"
            i = free_banks.pop(0)
            return banks[i], "ps%d" % i

        def fb(*names):
            for n in names:
                i = int(n[2:])
                assert i not in free_banks
                free_banks.append(i)

        def dma(eng, out, in_, reads, writes, chan, skip_waw=False):
            sc.add(eng, lambda h: h.dma_start(out=out, in_=in_), reads, writes, chan=chan, skip_waw=skip_waw)

        def mm(out, lhsT, rhs, start, stop, reads, writes, inc):
            sc.add("pe", lambda h: h.matmul(out, lhsT=lhsT, rhs=rhs, start=start, stop=stop),
                   reads, writes, inc=inc)

        def tr(out, in_, reads, writes, inc):
            sc.add("pe", lambda h: h.transpose(out, in_, ident), reads + ["cst_b"], writes, inc=inc)

        def act(out, in_, func, reads, writes, scale=1.0, bias=0.0, accum=None):
            if accum is None:
                sc.add("act", lambda h: h.activation(out=out, in_=in_, func=func, scale=scale, bias=bias),
                       reads, writes)
            else:
                sc.add("act", lambda h: h.activation(out=out, in_=in_, func=func, scale=scale, bias=bias,
                                                     accum_out=accum), reads, writes)

        def tt(eng, out, in0, in1, op, reads, writes):
            sc.add(eng, lambda h: h.tensor_tensor(out=out, in0=in0, in1=in1, op=op), reads, writes)

        def cp(eng, out, in_, reads, writes):
            if eng == "act":
                sc.add(eng, lambda h: h.copy(out=out, in_=in_), reads, writes)
            else:
                sc.add(eng, lambda h: h.tensor_copy(out=out, in_=in_), reads, writes)

        def h3(ap):
            return ap.rearrange("p (h t) -> p h t", t=128)

        dma("sp", cst_f[:, :], cst_d[:, 384:896], [], ["cst_f"], "c0")
        dma("pool", cst_b[:, :], cst_d[:, 0:384], [], ["cst_b"], "c0b")
        for i in range(2):
            dma("pool", gkb[i][:, :], gk_d[i], [], ["gkb%d" % i], "c1_%d" % i)
            dma("sp", lbl[i][:, :], lbl_d[i], [], ["lbl%d" % i], "c2_%d" % i)
        dma("sp", gn[:, :], gn_d, [], ["gn"], "c3")
        dma("sp", sel[:, :], sel_d, [], ["sel"], "c6")
        sc.add("pool", lambda h: h.memset(lrT[:, :], 1.0), [], ["lrT"])
        sc.add("pool", lambda h: h.memset(S[:, :], 0.0), [], ["S0", "S1", "S2"])
        sc.add("pool", lambda h: h.memset(Sb, 0.0), [], ["Sb0", "Sb1", "Sb2"])
        for i in range(2):
            act(lbt[:, :], lbl[i][:, :], AF.Exp, ["lbl%d" % i], ["lbt"])
            tt("dve", lbv[i][:, :], lbt[:, 0:8], lbt[:, 8:16], ALU.add, ["lbt"], ["lbv%d" % i])
            sc.add("dve", lambda h, i=i: h.reciprocal(out=lbv[i][:, :], in_=lbv[i][:, :]),
                   ["lbv%d" % i], ["lbv%d" % i])
            tt("dve", omlv[i][:, :], lbt[:, 8:16], lbv[i][:, :], ALU.mult, ["lbt", "lbv%d" % i], ["omlv%d" % i])
            tt("dve", lbv[i][:, :], lbt[:, 0:8], lbv[i][:, :], ALU.mult, ["lbt", "lbv%d" % i], ["lbv%d" % i])

        W1P = [(0, 1040, "W1a"), (1040, 3088, "W1b"), (3088, 5136, "W1c")]
        W2P = [(0, 1040, "W2a"), (1040, 3088, "W2b")]
        W3P = [(0, 2048, "W3")]
        ALLW = ["W1a", "W1b", "W1c", "W2a", "W2b", "W3", "WBlo"]

        def wname(W, col):
            ps = W1P if W is W1 else (W2P if W is W2 else W3P)
            for c0, c1, nm in ps:
                if c0 <= col < c1:
                    return nm
            raise AssertionError

        def load_w(dst3, src, pieces, tag, kcs=None, war_all=True, nowaw=False):
            first = True
            for (p0, p1, nm) in pieces:
                c0 = p0
                pfirst = True
                while c0 < p1:
                    c1 = min(p1, c0 + 2048)
                    for kc in (kcs if kcs is not None else range(dst3.shape[1])):
                        wr = [nm] + ([x for x in ALLW if x != nm] if (first and war_all) else [])
                        dma("pool", dst3[:, kc, c0:c1], src[:, kc, c0:c1], [], wr, "w" + tag + nm,
                            skip_waw=(not first) or nowaw)
                        first = False
                        pfirst = False
                    c0 = c1

        load_w(W1, w1_d, W1P, "1", war_all=False)

        flags = {}

        def need(k):
            while len(free_banks) < k:
                yield ("nobank",)

        def run_rr(gens, prio=0):
            gens = list(gens)
            if prio and len(gens) >= 8:
                order = [gens[j] for j in (RR_ORDER if prio == 1 else RR_ORDER2)]
            else:
                order = gens + (gens[3:6] if prio and len(gens) >= 6 else [])
            pend = {}
            alive = set(id(g) for g in gens)
            while alive:
                progressed = False
                for g in order:
                    if id(g) not in alive:
                        continue
                    k = pend.get(id(g))
                    if k is not None:
                        if not flags.get(k):
                            continue
                        pend[id(g)] = None
                    try:
                        r = next(g)
                    except StopIteration:
                        alive.discard(id(g))
                        progressed = True
                        continue
                    if isinstance(r, tuple) and r[0] == "wait":
                        pend[id(g)] = r[1]
                        progressed = True
                    elif isinstance(r, tuple) and r[0] == "nobank":
                        pass
                    else:
                        progressed = True
                assert progressed, "emission deadlock"

        def dma_x(t, xf, xfn):
            dma("sp", xf, x_d[t * P:(t + 1) * P, :], [], [xfn], "x" + xfn)

        def gen_load_x(t, b, xf=None, xfn=None, do_dma=True, ceng="act"):
            if xf is None:
                xf, xfn = x_f[b][:, :], "x_f%d" % b
            if do_dma:
                dma_x(t, xf, xfn)
            cp(ceng, x_bf[b][:, :], xf, [xfn], ["x_bf%d" % b])
            yield from need(1)
            bk, bn = nb()
            bkb = bk[:, :].bitcast(BF16)
            for kc in range(8):
                tr(bkb[:, kc * P:(kc + 1) * P], x_bf[b][:, kc * P:(kc + 1) * P], ["x_bf%d" % b], [bn], inc=(kc == 7))
            cp("act", xT[b][:, :], bkb[:, :], [bn], ["xT%d" % b])
            fb(bn)
            yield

        def proj_fm(W, col0, nheads, b, m=P):
            bk, bn = nb()
            for h in range(nheads):
                for kc in range(8):
                    mm(bk[0:m, h * P:(h + 1) * P], W[:, kc, col0 + h * m:col0 + (h + 1) * m],
                       xT[b][:, kc * P:(kc + 1) * P], kc == 0, kc == 7, ["xT%d" % b, wname(W, col0)], [bn],
                       inc=(h == nheads - 1 and kc == 7))
            return bk, bn

        def proj_tm(W, col0, b):
            bk, bn = nb()
            for kc in range(8):
                mm(bk[:, :], xT[b][:, kc * P:(kc + 1) * P], W[:, kc, col0:col0 + 512], kc == 0, kc == 7,
                   ["xT%d" % b, wname(W, col0)], [bn], inc=(kc == 7))
            return bk, bn

        def decay_chain(sw, u, par, Lbuf, Lname, scale, other, oname):
            B, Bn = TS[u][1], TN[u][1]
            Dd, Dn = TS[u][3], TN[u][3]
            el, eln = elast[u][par], "elast%d_%d" % (u, par)
            if sw == 1:
                sc.add("dve", lambda h: h.tensor_tensor_scan(out=B, data0=smask, data1=Lbuf,
                                                             initial=0.0, op0=ALU.mult, op1=ALU.add),
                       [Lname, "cst_f"], [Bn])
                tot = h3(B)[:, :, 127]
            else:
                sc.add("dve", lambda h: h.tensor_tensor_scan(out=B[:, ::-1], data0=smask, data1=Lbuf[:, ::-1],
                                                             initial=0.0, op0=ALU.mult, op1=ALU.add),
                       [Lname, "cst_f"], [Bn])
                tot = h3(B)[:, :, 0]
            yield
            act(el[:, :], tot, AF.Exp, [Bn], [eln], scale=-scale)
            act(Dd, B, AF.Exp, [Bn], [Dn], scale=-scale)
            yield
            if sw == 1:
                act(B, B, AF.Exp, [Bn, eln], [Bn], scale=scale)
                yield
                el_bc = el[:, :].unsqueeze(2).broadcast_to([P, 4, P])
                tt("dve", h3(other), h3(B), el_bc, ALU.mult, [Bn, eln], [oname])
                yield
            else:
                act(nbl[u][:, :], tot, AF.Identity, [Bn], ["nbl%d" % u], scale=-scale)
                for h in range(4):
                    sc.add("act", lambda hh, h=h: hh.activation(
                        out=other[:, h * P:(h + 1) * P], in_=B[:, h * P:(h + 1) * P], func=AF.Exp, scale=scale,
                        bias=nbl[u][:, h:h + 1]), [Bn, "nbl%d" % u] + ([oname] if oname != Bn else []), [oname])
                yield
                act(B, B, AF.Exp, [Bn, eln, oname], [Bn], scale=scale)
                yield
            return (Dd, Dn), (B, Bn), (other, oname)

        def stage_A_gla(sw, key, i, t, b, W, lrcol):
            u = 0
            TA, TC = TS[u][0], TS[u][2]
            TAn, TCn = TN[u][0], TN[u][2]
            yield from need(1)
            bk, bn = nb()
            for kc in range(8):
                mm(bk[0:16, 0:P], W[:, kc, lrcol:lrcol + 16], xT[b][:, kc * P:(kc + 1) * P], kc == 0, kc == 7,
                   ["xT%d" % b, wname(W, lrcol)], [bn], inc=(kc == 7))
            cp("act", lrT[0:16, :], bk[0:16, 0:P], [bn], ["lrT"])
            fb(bn)
            yield
            yield from need(1)
            bz, bzn = nb()
            for h in range(4):
                mm(bz[:, h * P:(h + 1) * P], gkb[sw - 1][:, h * P:(h + 1) * P], lrT[:, :], True, True,
                   ["lrT", "gkb%d" % (sw - 1)], [bzn], inc=(h == 3))
            act(TA, bz[:, :], AF.Exp, [bzn], [TAn], scale=-1.0)
            fb(bzn)
            yield
            act(TA, TA, AF.Ln, [TAn], [TAn], scale=1.0, bias=1.0)
            yield
            (eq, eqn), (ek, ekn), (ekh, ekhn) = yield from decay_chain(sw, u, i % 2, TA, TAn, 1.0 / 16.0, TC, TCn)
            if i >= 1:
                yield ("wait", (key, "B", i - 1, u))
            if sw == 1:
                yield from need(1)
                bq, bqn = proj_fm(W, 4112, 4, b)
                cp("act", qT_raw, bq[:, :], [bqn], ["qT_raw"])
                fb(bqn)
                yield
                yield from need(1)
                bkk, bkn = proj_fm(W, 4624, 4, b)
                cp("act", kT_raw, bkk[:, :], [bkn], ["kT_raw"])
                fb(bkn)
                dma("sp", qT_s.ap()[t], qT_raw, ["qT_raw"], ["qT_s%d" % t], "sq")
                dma("sp", kT_s.ap()[t], kT_raw, ["kT_raw"], ["kT_s%d" % t], "sk")
                yield
            else:
                dma("sp", qT_raw, qT_s.ap()[t], ["qT_s%d" % t], ["qT_raw"], "lq")
                dma("sp", kT_raw, kT_s.ap()[t], ["kT_s%d" % t], ["kT_raw"], "lk")
            tt("dve", QT[u], qT_raw, eq, ALU.mult, ["qT_raw", eqn], ["QT%d" % u])
            yield
            tt("dve", KT[u], kT_raw, ek, ALU.mult, ["kT_raw", ekn], ["KT%d" % u])
            yield
            tt("dve", KH[u], kT_raw, ekh, ALU.mult, ["kT_raw", ekhn], ["KH%d" % u])
            yield

        def stage_A_hgrn(sw, key, i, t, b, u, W, hfcol):
            hh = u - 1
            TA, TB, TC, TD = TS[u]
            TAn, TBn, TCn, TDn = TN[u]
            yield from need(1)
            bf, bfn = proj_fm(W, hfcol + hh * 512, 4, b)
            lb_bc = lbv[sw - 1][:, hh * 4:(hh + 1) * 4].unsqueeze(2).broadcast_to([P, 4, P])
            oml_bc = omlv[sw - 1][:, hh * 4:(hh + 1) * 4].unsqueeze(2).broadcast_to([P, 4, P])
            lbn, omn = "lbv%d" % (sw - 1), "omlv%d" % (sw - 1)
            act(TA, bf[:, :], AF.Exp, [bfn], [TAn], scale=-1.0)
            yield
            act(TB, TA, AF.Ln, [TAn], [TBn], scale=1.0, bias=1.0)
            tt("pool", h3(TC), h3(TA), lb_bc, ALU.mult, [TAn, lbn], [TCn])
            yield
            tt("dve", TA, bf[:, :], TB, ALU.add, [bfn, TBn], [TAn])
            fb(bfn)
            act(TC, TC, AF.Ln, [TCn], [TCn], scale=1.0, bias=1.0)
            yield
            act(TA, TA, AF.Exp, [TAn], [TAn], scale=-1.0)
            tt("pool", TC, TB, TC, ALU.subtract, [TBn, TCn], [TCn])
            yield
            tt("pool", h3(TA), h3(TA), oml_bc, ALU.mult, [TAn, omn], [TAn])
            (eq, eqn), (ek, ekn), (ekh, ekhn) = yield from decay_chain(sw, u, i % 2, TC, TCn, 1.0, TC, TCn)
            if i >= 1:
                yield ("wait", (key, "B", i - 1, u))
            yield ("wait", (key, "hq", i))
            hqn = "hq_raw%d_%d" % (b, hh)
            tt("dve", QT[u], hq_raw[b][:, hh * 512:(hh + 1) * 512], eq, ALU.mult, [hqn, eqn], ["QT%d" % u])
            yield
            tt("dve", KT[u], TA, ek, ALU.mult, [TAn, ekn], ["KT%d" % u])
            yield
            tt("dve", KH[u], TA, ekh, ALU.mult, [TAn, ekhn], ["KH%d" % u])
            yield

        def stage_B(sw, key, i, t, b, u):
            dv = 256 if u == 0 else 128
            s0 = 0 if u == 0 else 1024 + (u - 1) * 512
            if u == 0:
                V, vname = v_bf[b][:, :], "v_bf%d" % b
            else:
                V, vname = hi_bf[b][:, (u - 1) * 512:u * 512], "hi_bf%d" % b
            Sn, Sbn = "S%d" % u, "Sb%d" % u
            qt, kt, kh, kht, sct, el = QT[u], KT[u], KH[u], KHt[u], scT[u], elast[u][i % 2]
            qn, kn, khn, khtn, scn, eln = ("QT%d" % u, "KT%d" % u, "KH%d" % u, "KHt%d" % u, "scT%d" % u,
                                           "elast%d_%d" % (u, i % 2))
            yield from need(2)
            bk, bn = nb()
            for h in range(4):
                mm(bk[:, h * P:(h + 1) * P], kt[:, h * P:(h + 1) * P], qt[:, h * P:(h + 1) * P], True, True,
                   [kn, qn], [bn], inc=(h == 3))
            bk2, bn2 = nb()
            bkb = bk2[:, :].bitcast(BF16)
            for h in range(4):
                tr(bkb[:, h * P:(h + 1) * P], kh[:, h * P:(h + 1) * P], [khn], [bn2], inc=(h == 3))
            tri_bc = tri[sw - 1].unsqueeze(1).broadcast_to([P, 4, P])
            tt("dve", h3(sct), h3(bk[:, :]), tri_bc, ALU.mult, [bn, "cst_b"], [scn])
            cp("act", kht, bkb[:, 0:512], [bn2], [khtn])
            fb(bn, bn2)
            yield
            nbk = 2 if u == 0 else 1
            yield from need(nbk)
            obanks = [nb() for _ in range(nbk)]
            for h in range(4):
                ob, obn = obanks[(h * dv) // 512]
                oc = (h * dv) % 512
                mm(ob[:, oc:oc + dv], sct[:, h * P:(h + 1) * P], V[:, h * dv:(h + 1) * dv], True, False,
                   [scn, vname], [obn], inc=False)
                mm(ob[:, oc:oc + dv], qt[:, h * P:(h + 1) * P], Sb[:, s0 + h * dv:s0 + (h + 1) * dv], False, sw == 1,
                   [qn, Sbn], [obn], inc=(sw == 1))
                if sw == 2:
                    mm(ob[:, oc:oc + dv], ident, o1b[b][:, s0 + h * dv:s0 + (h + 1) * dv], False, True,
                       ["cst_b", "o1b%d_%d" % (b, u)], [obn], inc=True)
            flags[(key, "B", i, u)] = True
            yield
            ucols = slice(s0, s0 + 4 * dv)
            for j, (ob, obn) in enumerate(obanks):
                cols = slice(s0 + j * 512, s0 + (j + 1) * 512)
                if sw == 1:
                    cp("act", o1b[b][:, cols], ob[:, :], [obn], ["o1b%d_%d" % (b, u)])
                else:
                    cp("act", otot[:, cols], ob[:, :], [obn], ["otot%d" % u])
                fb(obn)
            if sw == 1:
                dma("sp", o1_s.ap()[t][:, ucols], o1b[b][:, ucols], ["o1b%d_%d" % (b, u)], ["o1_s%d_%d" % (t, u)],
                    "so1%d_%d" % (b, u))
            yield
            yield from need(nbk)
            pbanks = [nb() for _ in range(nbk)]
            for h in range(4):
                pb, pbn = pbanks[(h * dv) // 512]
                pc = (h * dv) % 512
                mm(pb[:, pc:pc + dv], kht[:, h * P:(h + 1) * P], V[:, h * dv:(h + 1) * dv], True, True,
                   [khtn, vname], [pbn], inc=True)
            yield
            for h in range(4):
                pb, pbn = pbanks[(h * dv) // 512]
                pc = (h * dv) % 512
                scol = slice(s0 + h * dv, s0 + (h + 1) * dv)
                sc.add("dve", lambda hh, scol=scol, pb=pb, pc=pc, h=h, el=el: hh.scalar_tensor_tensor(
                    out=S[:, scol], in0=S[:, scol], scalar=el[:, h:h + 1], in1=pb[:, pc:pc + dv],
                    op0=ALU.mult, op1=ALU.add), [Sn, eln, pbn], [Sn])
                if h % 2 == 1:
                    yield
            fb(*[pbn for (_, pbn) in pbanks])
            cp("pool", Sb[:, s0:s0 + 4 * dv], S[:, s0:s0 + 4 * dv], [Sn], [Sbn])
            yield
            if sw == 2:
                on = "otot%d" % u
                act(sq[:, 0:4 * dv], otot[:, ucols], AF.Square, [on], ["sq"])
                sc.add("dve", lambda h: h.tensor_reduce(
                    out=ss[u][:, :], in_=sq[:, 0:4 * dv].rearrange("p (h e) -> p h e", e=dv),
                    axis=mybir.AxisListType.X, op=ALU.add), ["sq"], ["ss%d" % u])
                yield
                act(rstd[u][:, :], ss[u][:, :], AF.Ln, ["ss%d" % u], ["rstd%d" % u], scale=1.0 / dv, bias=RMS_EPS)
                act(rstd[u][:, :], rstd[u][:, :], AF.Exp, ["rstd%d" % u], ["rstd%d" % u], scale=-0.5)
                yield
                r_bc = rstd[u][:, :].unsqueeze(2).broadcast_to([P, 4, dv])
                o3 = otot[:, ucols].rearrange("p (h e) -> p h e", e=dv)
                g3 = og[b][:, ucols].rearrange("p (h e) -> p h e", e=dv)
                for h in range(4):
                    sc.add("act", lambda hh, h=h: hh.activation(
                        out=og[b][:, s0 + h * dv:s0 + (h + 1) * dv], in_=otot[:, s0 + h * dv:s0 + (h + 1) * dv],
                        func=AF.Identity, scale=rstd[u][:, h:h + 1]), [on, "rstd%d" % u], ["og%d_%d" % (b, u)])
                dma("sp", og_s.ap()[t][:, ucols], og[b][:, ucols], ["og%d_%d" % (b, u)], ["og_s%d_%d" % (t, u)],
                    "sog%d_%d" % (b, u))
                yield

        def run_sweep(sw, key, order, pre_T, gen_Tv, after_proj, W, lrcol, hfcol):
            n = len(order)
            xs = lambda j: (x_f[j % 3][:, :], "x_f%d" % (j % 3))
            dma_x(order[0], *xs(0))
            if n > 1:
                dma_x(order[1], *xs(1))
            ceng = "dve" if sw == 1 else "act"
            run_rr([gen_load_x(order[0], 0, *xs(0), do_dma=False, ceng=ceng)])
            for i in range(n + 1):
                gens = []
                if i + 2 < n:
                    dma_x(order[i + 2], *xs(i + 2))
                if i < n:
                    t, b = order[i], i % 2
                    pre_T(t, b, i)
                    gens += [stage_A_gla(sw, key, i, t, b, W, lrcol),
                             stage_A_hgrn(sw, key, i, t, b, 1, W, hfcol),
                             stage_A_hgrn(sw, key, i, t, b, 2, W, hfcol)]
                if i >= 1:
                    pt, pb = order[i - 1], (i - 1) % 2
                    gens += [stage_B(sw, key, i - 1, pt, pb, 0), stage_B(sw, key, i - 1, pt, pb, 1),
                             stage_B(sw, key, i - 1, pt, pb, 2)]
                if i == n:
                    after_proj()
                if i < n:
                    gens.append(gen_Tv(t, b, i))
                if i + 1 < n:
                    gens.append(gen_load_x(order[i + 1], (i + 1) % 2, *xs(i + 1), do_dma=False, ceng=ceng))
                run_rr(gens, prio=sw if i < n else 0)

        def pre_T1(t, b, i):
            pass

        def gen_Tv1(t, b, i):
            for j in range(2):
                yield from need(1)
                bv, bvn = proj_tm(W1, 1040 + j * 512, b)
                act(v_bf[b][:, j * 512:(j + 1) * 512], bv[:, :], AF.Identity, [bvn], ["v_bf%d" % b], scale=Q_SCALE)
                fb(bvn)
                yield
            for j in range(2):
                yield from need(1)
                bv, bvn = proj_tm(W1, 2064 + j * 512, b)
                act(hi_bf[b][:, j * 512:(j + 1) * 512], bv[:, :], AF.Identity, [bvn], ["hi_bf%d" % b], scale=Q_SCALE)
                fb(bvn)
                yield
            yield from need(2)
            hb = [proj_fm(W1, 3088 + hh * 512, 4, b) for hh in range(2)]
            for hh in range(2):
                act(hq_raw[b][:, hh * 512:(hh + 1) * 512], hb[hh][0][:, :], AF.Silu, [hb[hh][1]],
                    ["hq_raw%d_%d" % (b, hh)])
                fb(hb[hh][1])
            flags[("s1", "hq", i)] = True
            dma("sp", v_s.ap()[t], v_bf[b][:, :], ["v_bf%d" % b], ["v_s%d" % t], "sv%d" % b)
            dma("sp", hi_s.ap()[t], hi_bf[b][:, :], ["hi_bf%d" % b], ["hi_s%d" % t], "shi%d" % b)
            dma("sp", hq_s.ap()[t], hq_raw[b][:, :], ["hq_raw%d_0" % b, "hq_raw%d_1" % b], ["hq_s%d" % t],
                "shq%d" % b)
            yield

        def after1():
            load_w(W2, w2_d, W2P, "2")

        if stop is None or stop >= 1:
            run_sweep(1, "s1", list(range(NT)), pre_T1, gen_Tv1, after1, W1, 0, 16)

        def exchange():
            dma("sp", cc_in.ap(), S[:, :], ["S0", "S1", "S2"], ["cc_in"], "xs")
            sc.add("pool", lambda h: h.collective_compute(
                "AllGather", ALU.bypass, replica_groups=[[0, 1], [2, 3], [4, 5], [6, 7]],
                ins=[cc_in.ap().opt()], outs=[cc_out.ap().opt()]), ["cc_in"], ["cc_out"], chan="cc", cc=True)
            dma("sp", otot[:, :], cc_out.ap()[0:P, :], ["cc_out"], ["otot0", "otot1", "otot2"], "xl0")
            sc.add("dve", lambda h: h.tensor_scalar(out=S[:, :], in0=otot[:, :], scalar1=sel[:, 0:1], scalar2=None,
                                                    op0=ALU.mult), ["otot0", "otot1", "otot2", "sel"], ["S0", "S1", "S2"])
            dma("sp", otot[:, :], cc_out.ap()[P:2 * P, :], ["cc_out"], ["otot0", "otot1", "otot2"], "xl1")
            sc.add("dve", lambda h: h.scalar_tensor_tensor(out=S[:, :], in0=otot[:, :], scalar=sel[:, 1:2],
                                                           in1=S[:, :], op0=ALU.mult, op1=ALU.add),
                   ["otot0", "otot1", "otot2", "sel", "S0", "S1", "S2"], ["S0", "S1", "S2"])
            cp("act", Sb, S[:, :], ["S0", "S1", "S2"], ["Sb0", "Sb1", "Sb2"])

        def pre_T2(t, b, i):
            dma("sp", v_bf[b][:, :], v_s.ap()[t], ["v_s%d" % t], ["v_bf%d" % b], "lv%d" % b)
            dma("sp", hi_bf[b][:, :], hi_s.ap()[t], ["hi_s%d" % t], ["hi_bf%d" % b], "lhi%d" % b)
            dma("sp", hq_raw[b][:, :], hq_s.ap()[t], ["hq_s%d" % t], ["hq_raw%d_0" % b, "hq_raw%d_1" % b],
                "lhq%d" % b)
            dma("sp", o1b[b][:, :], o1_s.ap()[t], ["o1_s%d_%d" % (t, u) for u in range(3)],
                ["o1b%d_%d" % (b, u) for u in range(3)], "lo1%d" % b)
            flags[("s2", "hq", i)] = True
            if i == 1:
                load_w(WB, wb_d, [(0, D, "WBhi")], "bh", kcs=range(8, 24), war_all=False)

        def gen_Tv2(t, b, i):
            for j in range(4):
                yield from need(1)
                gbk, gbn = proj_tm(W2, 1040 + j * 512, b)
                cp("act", sg[b][:, j * 512:(j + 1) * 512], gbk[:, :], [gbn], ["sg%d" % b])
                fb(gbn)
                yield
            dma("sp", g_s.ap()[t], sg[b][:, :], ["sg%d" % b], ["g_s%d" % t], "sg_s%d" % b)
            yield

        def after2():
            load_w(W3, w3_d, W3P, "3")
            load_w(WB, wb_d, [(0, D, "WBlo")], "b", kcs=range(8), war_all=False, nowaw=True)

        def gen_P1(t, b):
            yield from need(4)
            gb = [proj_tm(W3, j * 512, b) for j in range(4)]
            for j in range(4):
                act(sigm[b][:, j * 512:(j + 1) * 512], gb[j][0][:, :], AF.Sigmoid, [gb[j][1]], ["sigm%d" % b])
                fb(gb[j][1])
            act(sgt, sg[b][:, :], AF.Sigmoid, ["sg%d" % b], ["sgt"])
            yield
            tt("dve", sg[b][:, :], sg[b][:, :], sgt, ALU.mult, ["sg%d" % b, "sgt"], ["sg%d" % b])
            tt("dve", og[b][:, :], og[b][:, :], sg[b][:, :], ALU.mult, ["og%d" % b, "sg%d" % b], ["og%d" % b])
            yield
            for half in range(2):
                yield from need(1)
                bk, bn = nb()
                bkb = bk[:, :].bitcast(BF16)
                for c in range(8):
                    cc = half * 8 + c
                    tr(bkb[:, c * P:(c + 1) * P], og[b][:, cc * P:(cc + 1) * P], ["og%d" % b], [bn], inc=(c == 7))
                cp("act" if half == 0 else "dve", ogT[b][:, half * 1024:(half + 1) * 1024], bkb[:, :], [bn],
                   ["ogT%d_%d" % (b, half)])
                fb(bn)
                yield

        x_f3 = [x_f[0][:, :], x_f[1][:, :], x_f[2][:, :], sq[:, :]]
        x_f3n = ["x_f0", "x_f1", "x_f2", "sq"]
        y_bf2 = [y_bf, T4s[2][:, :].bitcast(BF16)[:, 0:1024]]
        sgt = T4s[2][:, :].bitcast(BF16)[:, 1024:3072]

        def gen_P2a(t, b):
            yb = y_bf2[b]
            for j in range(2):
                yield from need(2)
                bg, bgn = nb()
                for kc in range(8):
                    mm(bg[:, :], ogT[b][:, kc * P:(kc + 1) * P], WB[:, kc, j * 512:(j + 1) * 512], kc == 0, kc == 7,
                       ["ogT%d_0" % b, "WBlo"], [bgn], inc=(kc == 7))
                bh, bhn = nb()
                for kc in range(8):
                    mm(bh[:, :], ogT[b][:, (8 + kc) * P:(9 + kc) * P], WB[:, 8 + kc, j * 512:(j + 1) * 512],
                       kc == 0, kc == 7, ["ogT%d_1" % b, "WBhi"], [bhn], inc=(kc == 7))
                c0 = slice(j * 512, (j + 1) * 512)
                c1 = slice(1024 + j * 512, 1024 + (j + 1) * 512)
                tt("dve", S[:, c0], bg[:, :], sigm[b][:, c0], ALU.mult, [bgn, "sigm%d" % b], ["t1_%d" % j])
                tt("dve", S[:, c1], bh[:, :], sigm[b][:, c1], ALU.mult, [bhn, "sigm%d" % b], ["t2_%d" % j])
                fb(bgn, bhn)
                yield
                tt("pool" if j == 0 else "dve", yb[:, c0], S[:, c0], S[:, c1], ALU.add, ["t1_%d" % j, "t2_%d" % j],
                   ["y_bf%d_%d" % (b, j)])
                yield

        def gen_P2b(t, b):
            yb = y_bf2[b]
            xf, xfn = x_f3[t % 4], x_f3n[t % 4]
            yield from need(1)
            bk, bn = nb()
            bkb = bk[:, :].bitcast(BF16)
            for c in range(8):
                tr(bkb[:, c * P:(c + 1) * P], yb[:, c * P:(c + 1) * P], ["y_bf%d_%d" % (b, c // 4)], [bn], inc=(c == 7))
            cp("act", yT, bkb[:, :], [bn], ["yT"])
            fb(bn)
            yield
            for j in range(2):
                yield from need(1)
                bo, bon = nb()
                for kc in range(8):
                    mm(bo[:, :], yT[:, kc * P:(kc + 1) * P], WB[:, 16 + kc, j * 512:(j + 1) * 512],
                       kc == 0, kc == 7, ["yT", "WBhi"], [bon], inc=(kc == 7))
                c0 = slice(j * 512, (j + 1) * 512)
                sc.add("dve", lambda h, c0=c0, bo=bo, xf=xf: h.scalar_tensor_tensor(
                    out=otot[:, c0], in0=xf[:, c0], scalar=ALPHA, in1=bo[:, :], op0=ALU.mult, op1=ALU.add),
                    [xfn, bon], ["r%d" % j])
                sc.add("dve", lambda h, c0=c0, j=j: h.bn_stats(out=bst[:, j * 6:(j + 1) * 6], in_=otot[:, c0]),
                       ["r%d" % j], ["bst%d" % j])
                fb(bon)
                yield
            sc.add("dve", lambda h: h.bn_aggr(out=mv[:, :], in_=bst[:, :]), ["bst0", "bst1"], ["mv"])
            act(lnr[:, 0:1], mv[:, 1:2], AF.Ln, ["mv"], ["lnr"], scale=1.0, bias=LN_EPS)
            act(lnr[:, 0:1], lnr[:, 0:1], AF.Exp, ["lnr"], ["lnr"], scale=-0.5)
            sc.add("dve", lambda h: h.scalar_tensor_tensor(
                out=lnr[:, 1:2], in0=mv[:, 0:1], scalar=-1.0, in1=lnr[:, 0:1], op0=ALU.mult, op1=ALU.mult),
                ["mv", "lnr"], ["lnr"])
            yield
            sc.add("act", lambda h: h.activation(out=otot[:, 0:1024], in_=otot[:, 0:1024], func=AF.Identity,
                                                 scale=lnr[:, 0:1], bias=lnr[:, 1:2]),
                   ["r0", "r1", "lnr"], ["r0", "r1"])
            yield
            tt("pool", otot[:, 0:1024], otot[:, 0:1024], lng, ALU.mult, ["r0", "r1", "lng"], ["r0", "r1"])
            yield
            tt("dve", ores[b], otot[:, 0:1024], lnb, ALU.add, ["r0", "r1", "lnb"], ["ores%d" % b])
            dma("sp", out_d[t * P:(t + 1) * P, :], ores[b], ["ores%d" % b], [], "out%d" % b)
            yield

        def sweep3():
            sc.barrier()
            dma("sp", lng, lng_d, [], ["lng"], "c4")
            dma("sp", lnb, lnb_d, [], ["lnb"], "c5")
            for kc in range(16):
                gcol = (kc % 2) if kc < 8 else 2
                wr = "WBlo" if kc < 8 else "WBhi"
                sc.add("dve", lambda h, kc=kc, gcol=gcol: h.tensor_scalar(
                    out=WB[:, kc, :], in0=WB[:, kc, :], scalar1=gn[:, gcol:gcol + 1], scalar2=None, op0=ALU.mult),
                    [wr, "gn"], [wr])
            def p1_loads(t, b):
                dma_x(t, x_f3[t % 4], x_f3n[t % 4])
                dma("sp", og[b][:, :], og_s.ap()[t], ["og_s%d_%d" % (t, u) for u in range(3)], ["og%d" % b],
                    "log%d" % b)
                dma("sp", sg[b][:, :], g_s.ap()[t], ["g_s%d" % t], ["sg%d" % b], "lg%d" % b)

            p1_loads(0, 0)
            for k in range(NT + 2):
                gens = []
                if k + 1 < NT:
                    p1_loads(k + 1, (k + 1) % 2)
                if k - 2 >= 0:
                    gens.append(gen_P2b(k - 2, (k - 2) % 2))
                if 0 <= k - 1 < NT:
                    gens.append(gen_P2a(k - 1, (k - 1) % 2))
                if k < NT:
                    gens.append(seq(gen_load_x(k, k % 2, x_f3[k % 4], x_f3n[k % 4], do_dma=False), gen_P1(k, k % 2)))
                if len(gens) == 3:
                    gens = [gens[j] for j in RR3]
                run_rr(gens)

        def seq(*gens):
            for g in gens:
                yield from g

        if stop is None or stop >= 2:
            exchange()
            run_sweep(2, "s2", list(range(NT - 1, -1, -1)), pre_T2, gen_Tv2, after2, W2, 0, 16)
        if stop is None or stop >= 3:
            sweep3()

        block = es.enter_context(nc.Block())
        sc.emit(block)
    return nc


_COLS = dict(a_q=0, a_k=512, a_v=1024, a_gate=2048, lr_f=3072, lr_b=3088, h_q=3104, h_f_f=4128,
             h_f_b=5152, h_i=6176, h_gate=7200, m_gla=8224, m_hgrn=9248)


def _tile_rows(w):
    k = w.shape[0] // P
    return np.ascontiguousarray(w.reshape(k, P, w.shape[1]).transpose(1, 0, 2))


def make_in_maps(x, w_in, gla_gk_up_f, gla_gk_bias_f, gla_gk_up_b, gla_gk_bias_b, gla_norm_g,
                 hgrn_lb_logits_f, hgrn_lb_logits_b, hgrn_norm_g, w_branch_gla, w_branch_hgrn,
                 w_out, ln_g, ln_b):
    f32 = np.float32
    x = np.asarray(x, f32)
    win = np.asarray(w_in, f32)[0]
    c = _COLS

    def cols(name, n):
        return win[:, c[name]:c[name] + n]

    dirs = {
        "f": dict(lr=cols("lr_f", 16), hf=cols("h_f_f", 1024),
                  gk=np.concatenate([np.asarray(gla_gk_up_f, f32)[0], np.asarray(gla_gk_bias_f, f32)[0][None]], 0),
                  lbl=np.asarray(hgrn_lb_logits_f, f32)),
        "b": dict(lr=cols("lr_b", 16), hf=cols("h_f_b", 1024),
                  gk=np.concatenate([np.asarray(gla_gk_up_b, f32)[0], np.asarray(gla_gk_bias_b, f32)[0][None]], 0),
                  lbl=np.asarray(hgrn_lb_logits_b, f32)),
    }

    def lbl_layout(l):
        return np.ascontiguousarray(l.reshape(2, 8, P).transpose(2, 0, 1).reshape(P, 16))

    w3 = _tile_rows(np.concatenate([cols("m_gla", 1024), cols("m_hgrn", 1024)], 1))
    wb = _tile_rows(np.concatenate([np.asarray(w_branch_gla, f32)[0], np.asarray(w_branch_hgrn, f32)[0],
                                    np.asarray(w_out, f32)[0]], 0))
    gng = np.asarray(gla_norm_g, f32)[0]
    gn = np.stack([gng[0:128], gng[128:256], np.asarray(hgrn_norm_g, f32)[0]], 1)
    lng = np.ascontiguousarray(np.broadcast_to(np.asarray(ln_g, f32)[0][None], (P, D)))
    lnb = np.ascontiguousarray(np.broadcast_to(np.asarray(ln_b, f32)[0][None], (P, D)))
    ii = np.arange(P)
    smask = np.ones((P, 512), f32)
    smask[:, 0::128] = 0.0
    cst = np.concatenate([np.eye(P, dtype=f32), (ii[:, None] <= ii[None, :]).astype(f32),
                          (ii[:, None] >= ii[None, :]).astype(f32), smask], 1)

    per_dir = {}
    for d1, d2 in (("f", "b"), ("b", "f")):
        w1 = _tile_rows(np.concatenate([dirs[d1]["lr"], dirs[d1]["hf"], cols("a_v", 1024), cols("h_i", 1024),
                                        cols("h_q", 1024), cols("a_q", 512), cols("a_k", 512)], 1))
        w2 = _tile_rows(np.concatenate([dirs[d2]["lr"], dirs[d2]["hf"], cols("a_gate", 1024),
                                        cols("h_gate", 1024)], 1))
        per_dir[d1] = dict(w1=w1, w2=w2, gk1=dirs[d1]["gk"], gk2=dirs[d2]["gk"],
                           lbl1=lbl_layout(dirs[d1]["lbl"]), lbl2=lbl_layout(dirs[d2]["lbl"]))

    in_maps = []
    for core in range(NCORES):
        b, half = core // 2, core % 2
        TL = x.shape[1] // 2
        xl = x[b, half * TL:(half + 1) * TL]
        if half == 1:
            xl = xl[::-1]
        pd = per_dir["f" if half == 0 else "b"]
        sel = np.zeros((P, 2), f32)
        sel[:, 1 - half] = 1.0
        in_maps.append(dict(x=np.ascontiguousarray(xl), w1=pd["w1"], w2=pd["w2"], w3=w3, wb=wb,
                            gk1=pd["gk1"], gk2=pd["gk2"], lbl1=pd["lbl1"], lbl2=pd["lbl2"], gn=gn,
                            lng=lng, lnb=lnb, sel=sel, cst=cst))

    return in_maps


def assemble(outs, T):
    out = np.empty((4, T, D), np.float32)
    TL = T // 2
    for core in range(NCORES):
        b, half = core // 2, core % 2
        o = np.asarray(outs[core], np.float32)
        if half == 1:
            o = o[::-1]
        out[b, half * TL:(half + 1) * TL] = o
    return out


def kernel(**inputs):
    in_maps = make_in_maps(**inputs)
    nc = build_program()
    res = run_bass_kernel_spmd(nc, in_maps, core_ids=list(range(NCORES)))
    return assemble([res.results[c]["out"] for c in range(NCORES)], 4096)
```

```python
import math
from contextlib import ExitStack

import numpy as np
import concourse.bass as bass
import concourse.mybir as mybir
from concourse.bass_utils import run_bass_kernel_spmd

F32 = mybir.dt.float32
BF16 = mybir.dt.bfloat16
AF = mybir.ActivationFunctionType
ALU = mybir.AluOpType

P = 128
D = 1024
NT = 16
NCORES = 8
W1C, W2C, W3C = 5136, 3088, 2048
WFLAT = 8 * W1C
Q_SCALE = 128.0 ** -0.5
LN_QS = math.log(Q_SCALE)
ALPHA = 2.0 ** 0.25
RMS_EPS = 1e-6
LN_EPS = 1e-5
SAME_ENGINE_WAITS = True
PRIO = 3
RR_ORDER = [0, 1, 2, 3, 4, 5, 6, 3, 7, 3]
RR_ORDER2 = [0, 1, 2, 3, 4, 5, 6, 3, 7, 0]
RR3 = [0, 1, 1, 2]


class Sched:
    def __init__(self, nc, es):
        self.nc = nc
        self.es = es
        self.ops = []
        self.last_w = {}
        self.readers = {}
        self.eng_sem = {}
        self.eng_cnt = {}
        self.chan = {}
        self.pending = {}
        self.last_access = {}
        self.bar_deps = []
        for e in ("pe", "act", "dve", "pool"):
            self.eng_sem[e] = es.enter_context(nc.semaphore("s_" + e))
            self.eng_cnt[e] = 0

    def _deps(self, reads, writes, skip_waw=False):
        deps = []
        for r in reads:
            if r in self.last_w:
                deps.append((self.last_w[r], "raw"))
        for w in writes:
            if w in self.last_w and not skip_waw:
                deps.append((self.last_w[w], "waw"))
            for rd in self.readers.get(w, ()):
                deps.append((rd, "war"))
        return deps

    def add(self, eng, fn, reads=(), writes=(), chan=None, inc=True, cc=False, skip_waw=False):
        idx = len(self.ops)
        deps = self._deps(reads, writes, skip_waw) + list(self.bar_deps)
        for r in list(reads) + list(writes):
            if r.startswith("ps"):
                la = self.last_access.get(r)
                if la is not None and self.ops[la]["eng"] != eng:
                    deps.append((la, "psx"))
                self.last_access[r] = idx
        op = dict(eng=eng, fn=fn, deps=deps, done=None, kind="c", own=(chan is not None or inc))
        if chan is not None:
            if chan not in self.chan:
                self.chan[chan] = [self.es.enter_context(self.nc.semaphore("d_" + chan)), 0]
            c = self.chan[chan]
            step = 1 if cc else 16
            c[1] += step
            op["done"] = (c[0], c[1])
            op["kind"] = "cc" if cc else "dma"
        elif inc:
            self.eng_cnt[eng] += 1
            op["done"] = (self.eng_sem[eng], self.eng_cnt[eng])
            for po in self.pending.get(eng, ()):
                po["done"] = op["done"]
            self.pending[eng] = []
        else:
            self.pending.setdefault(eng, []).append(op)
        self.ops.append(op)
        for w in writes:
            self.last_w[w] = idx
            self.readers[w] = []
        for r in reads:
            self.readers.setdefault(r, []).append(idx)
        return idx

    def barrier(self):
        last = {}
        for i, o in enumerate(self.ops):
            if o["done"] is not None:
                last[(o["eng"], id(o["done"][0]))] = i
        self.bar_deps = [(i, "bar") for i in last.values()]

    def emit(self, block):
        nc = self.nc
        handles = {"pe": "tensor", "act": "scalar", "dve": "vector", "pool": "gpsimd", "sp": "sync"}
        for eng, hname in handles.items():
            my = [o for o in self.ops if o["eng"] == eng]

            def body(h, my=my, eng=eng):
                waited = {}
                for op in my:
                    needs = {}
                    for (di, kind) in op["deps"]:
                        dop = self.ops[di]
                        if dop["done"] is None:
                            continue
                        if dop["eng"] == eng and dop["kind"] == "c":
                            if not SAME_ENGINE_WAITS or eng == "pe":
                                continue
                        sem, val = dop["done"]
                        key = id(sem)
                        if key not in needs or needs[key][1] < val:
                            needs[key] = (sem, val)
                    for key, (sem, val) in needs.items():
                        if waited.get(key, 0) >= val:
                            continue
                        waited[key] = val
                        h.wait_ge(sem, val)
                    ins = op["fn"](h)
                    if op["done"] is not None and op["own"]:
                        sem, val = op["done"]
                        if op["kind"] == "dma":
                            ins.then_inc(sem, 16)
                        elif op["kind"] == "cc":
                            ins.then_inc(sem)
                        else:
                            ins.then_inc(sem, 1)
                if eng == "sp":
                    for name, (sem, val) in self.chan.items():
                        h.wait_ge(sem, val)

            getattr(block, hname)(body)


def build_program(NT=NT, stop=None):
    nc = bass.Bass("TRN2", target_bir_lowering=False)
    dt_in = lambda name, shape: nc.dram_tensor(name, shape, F32, kind="ExternalInput").ap()
    x_d = dt_in("x", [NT * P, D])
    w1_d = dt_in("w1", [P, 8, W1C])
    w2_d = dt_in("w2", [P, 8, W2C])
    w3_d = dt_in("w3", [P, 8, W3C])
    wb_d = dt_in("wb", [P, 24, D])
    gk_d = [dt_in("gk1", [17, 512]), dt_in("gk2", [17, 512])]
    lbl_d = [dt_in("lbl1", [P, 16]), dt_in("lbl2", [P, 16])]
    gn_d = dt_in("gn", [P, 3])
    lng_d = dt_in("lng", [P, D])
    lnb_d = dt_in("lnb", [P, D])
    sel_d = dt_in("sel", [P, 2])
    cst_d = dt_in("cst", [P, 3 * 128 + 512])
    out_d = nc.dram_tensor("out", [NT * P, D], F32, kind="ExternalOutput").ap()

    qT_s = nc.dram_tensor("qT_s", [NT, P, 512], BF16)
    kT_s = nc.dram_tensor("kT_s", [NT, P, 512], BF16)
    hq_s = nc.dram_tensor("hq_s", [NT, P, 1024], BF16)
    v_s = nc.dram_tensor("v_s", [NT, P, 1024], BF16)
    hi_s = nc.dram_tensor("hi_s", [NT, P, 1024], BF16)
    o1_s = nc.dram_tensor("o1_s", [NT, P, 2048], BF16)
    og_s = nc.dram_tensor("og_s", [NT, P, 2048], BF16)
    g_s = nc.dram_tensor("g_s", [NT, P, 2048], BF16)
    cc_in = nc.dram_tensor("cc_in", [P, 2048], F32)
    cc_out = nc.dram_tensor("cc_out", [2 * P, 2048], F32)

    with ExitStack() as es:
        sb = lambda name, shape, dt: es.enter_context(nc.sbuf_tensor(name, shape, dt))
        wflat = sb("wflat", [P, WFLAT], BF16)
        W1 = wflat[:, 0:8 * W1C].rearrange("p (k c) -> p k c", c=W1C)
        W2 = wflat[:, 0:8 * W2C].rearrange("p (k c) -> p k c", c=W2C)
        W3 = wflat[:, 0:8 * W3C].rearrange("p (k c) -> p k c", c=W3C)
        WB = wflat[:, WFLAT - 24 * D:WFLAT].rearrange("p (k c) -> p k c", c=D)

        cst_f = sb("cst_f", [P, 512], F32)
        cst_b = sb("cst_b", [P, 3 * 128], BF16)
        ident = cst_b[:, 0:128]
        tri = [cst_b[:, 128:256], cst_b[:, 256:384]]
        smask = cst_f[:, 0:512]
        gkb = [sb("gk1b", [17, 512], BF16), sb("gk2b", [17, 512], BF16)]
        lbl = [sb("lbl1s", [P, 16], F32), sb("lbl2s", [P, 16], F32)]
        lbv = [sb("lbv1", [P, 8], F32), sb("lbv2", [P, 8], F32)]
        omlv = [sb("omlv1", [P, 8], F32), sb("omlv2", [P, 8], F32)]
        lbt = sb("lbt", [P, 16], F32)
        gn = sb("gns", [P, 3], F32)
        sel = sb("sels", [P, 2], F32)

        x_f = [sb("x_f%d" % i, [P, D], F32) for i in range(3)]
        x_bf = [sb("x_bf%d" % i, [P, D], BF16) for i in range(2)]
        xT = [sb("xT%d" % i, [P, D], BF16) for i in range(2)]
        v_bf = [sb("v_bf%d" % i, [P, 1024], BF16) for i in range(2)]
        hi_bf = [sb("hi_bf%d" % i, [P, 1024], BF16) for i in range(2)]
        hq_raw = [sb("hq_raw%d" % i, [P, 1024], BF16) for i in range(2)]
        o1b = [sb("o1b%d" % i, [P, 2048], BF16) for i in range(2)]
        sg = [sb("sg%d" % i, [P, 2048], BF16) for i in range(2)]
        lrT = sb("lrT", [17, P], BF16)
        T4s = [sb("T4_%d" % i, [P, 2048], F32) for i in range(3)]
        TS = [[T4s[u][:, i * 512:(i + 1) * 512] for i in range(4)] for u in range(3)]
        TN = [["T%d_%d" % (u, i) for i in range(4)] for u in range(3)]
        lng, lnb = T4s[0][:, 0:1024], T4s[0][:, 1024:2048]
        U = sb("U", [P, 10752], BF16)
        QT = [U[:, (0 + u) * 512:(1 + u) * 512] for u in range(3)]
        KT = [U[:, (3 + u) * 512:(4 + u) * 512] for u in range(3)]
        KH = [U[:, (6 + u) * 512:(7 + u) * 512] for u in range(3)]
        KHt = [U[:, (9 + i) * 512:(10 + i) * 512] for i in range(3)]
        scT = [U[:, (12 + i) * 512:(13 + i) * 512] for i in range(3)]
        qT_raw = U[:, 15 * 512:16 * 512]
        kT_raw = U[:, 16 * 512:17 * 512]
        Sb = U[:, 17 * 512:21 * 512]
        T4b = T4s[1][:, :].bitcast(BF16)
        ogT = [U[:, 0:2048], T4b[:, 0:2048]]
        sigm = [U[:, 2048:4096], T4b[:, 2048:4096]]
        y_bf = U[:, 4096:5120]
        yT = U[:, 5120:6144]
        ores = [U[:, 6144 + i * 2048:6144 + (i + 1) * 2048].bitcast(F32) for i in range(2)]
        elast = [[sb("elast%d_%d" % (u, i), [P, 4], F32) for i in range(2)] for u in range(3)]
        nbl = [sb("nbl%d" % u, [P, 4], F32) for u in range(3)]
        S = sb("S", [P, 2048], F32)
        otot = sb("otot", [P, 2048], F32)
        sq = sb("sq", [P, 1024], F32)
        og = [sb("og%d" % i, [P, 2048], BF16) for i in range(2)]
        ss = [sb("ss%d" % i, [P, 4], F32) for i in range(3)]
        rstd = [sb("rstd%d" % i, [P, 4], F32) for i in range(3)]
        bst = sb("bst", [P, 12], F32)
        mv = sb("mv", [P, 2], F32)
        lnr = sb("lnr", [P, 2], F32)

        banks = [es.enter_context(nc.psum_tensor("ps%d" % i, [P, 512], F32)) for i in range(8)]
        sc = Sched(nc, es)
        free_banks = list(range(8))

        def nb():
            assert free_banks, "out of PSUM banks at emission time"
            i = free_banks.pop(0)
            return banks[i], "ps%d" % i

        def fb(*names):
            for n in names:
                i = int(n[2:])
                assert i not in free_banks
                free_banks.append(i)

        def dma(eng, out, in_, reads, writes, chan, skip_waw=False):
            sc.add(eng, lambda h: h.dma_start(out=out, in_=in_), reads, writes, chan=chan, skip_waw=skip_waw)

        def mm(out, lhsT, rhs, start, stop, reads, writes, inc):
            sc.add("pe", lambda h: h.matmul(out, lhsT=lhsT, rhs=rhs, start=start, stop=stop),
                   reads, writes, inc=inc)

        def tr(out, in_, reads, writes, inc):
            sc.add("pe", lambda h: h.transpose(out, in_, ident), reads + ["cst_b"], writes, inc=inc)

        def act(out, in_, func, reads, writes, scale=1.0, bias=0.0, accum=None):
            if accum is None:
                sc.add("act", lambda h: h.activation(out=out, in_=in_, func=func, scale=scale, bias=bias),
                       reads, writes)
            else:
                sc.add("act", lambda h: h.activation(out=out, in_=in_, func=func, scale=scale, bias=bias,
                                                     accum_out=accum), reads, writes)

        def tt(eng, out, in0, in1, op, reads, writes):
            sc.add(eng, lambda h: h.tensor_tensor(out=out, in0=in0, in1=in1, op=op), reads, writes)

        def cp(eng, out, in_, reads, writes):
            if eng == "act":
                sc.add(eng, lambda h: h.copy(out=out, in_=in_), reads, writes)
            else:
                sc.add(eng, lambda h: h.tensor_copy(out=out, in_=in_), reads, writes)

        def h3(ap):
            return ap.rearrange("p (h t) -> p h t", t=128)

        dma("sp", cst_f[:, :], cst_d[:, 384:896], [], ["cst_f"], "c0")
        dma("pool", cst_b[:, :], cst_d[:, 0:384], [], ["cst_b"], "c0b")
        for i in range(2):
            dma("pool", gkb[i][:, :], gk_d[i], [], ["gkb%d" % i], "c1_%d" % i)
            dma("sp", lbl[i][:, :], lbl_d[i], [], ["lbl%d" % i], "c2_%d" % i)
        dma("sp", gn[:, :], gn_d, [], ["gn"], "c3")
        dma("sp", sel[:, :], sel_d, [], ["sel"], "c6")
        sc.add("pool", lambda h: h.memset(lrT[:, :], 1.0), [], ["lrT"])
        sc.add("pool", lambda h: h.memset(S[:, :], 0.0), [], ["S0", "S1", "S2"])
        sc.add("pool", lambda h: h.memset(Sb, 0.0), [], ["Sb0", "Sb1", "Sb2"])
        for i in range(2):
            act(lbt[:, :], lbl[i][:, :], AF.Exp, ["lbl%d" % i], ["lbt"])
            tt("dve", lbv[i][:, :], lbt[:, 0:8], lbt[:, 8:16], ALU.add, ["lbt"], ["lbv%d" % i])
            sc.add("dve", lambda h, i=i: h.reciprocal(out=lbv[i][:, :], in_=lbv[i][:, :]),
                   ["lbv%d" % i], ["lbv%d" % i])
            tt("dve", omlv[i][:, :], lbt[:, 8:16], lbv[i][:, :], ALU.mult, ["lbt", "lbv%d" % i], ["omlv%d" % i])
            tt("dve", lbv[i][:, :], lbt[:, 0:8], lbv[i][:, :], ALU.mult, ["lbt", "lbv%d" % i], ["lbv%d" % i])

        W1P = [(0, 1040, "W1a"), (1040, 3088, "W1b"), (3088, 5136, "W1c")]
        W2P = [(0, 1040, "W2a"), (1040, 3088, "W2b")]
        W3P = [(0, 2048, "W3")]
        ALLW = ["W1a", "W1b", "W1c", "W2a", "W2b", "W3", "WBlo"]

        def wname(W, col):
            ps = W1P if W is W1 else (W2P if W is W2 else W3P)
            for c0, c1, nm in ps:
                if c0 <= col < c1:
                    return nm
            raise AssertionError

        def load_w(dst3, src, pieces, tag, kcs=None, war_all=True, nowaw=False):
            first = True
            for (p0, p1, nm) in pieces:
                c0 = p0
                pfirst = True
                while c0 < p1:
                    c1 = min(p1, c0 + 2048)
                    for kc in (kcs if kcs is not None else range(dst3.shape[1])):
                        wr = [nm] + ([x for x in ALLW if x != nm] if (first and war_all) else [])
                        dma("pool", dst3[:, kc, c0:c1], src[:, kc, c0:c1], [], wr, "w" + tag + nm,
                            skip_waw=(not first) or nowaw)
                        first = False
                        pfirst = False
                    c0 = c1

        load_w(W1, w1_d, W1P, "1", war_all=False)

        flags = {}

        def need(k):
            while len(free_banks) < k:
                yield ("nobank",)

        def run_rr(gens, prio=0):
            gens = list(gens)
            if prio and len(gens) >= 8:
                order = [gens[j] for j in (RR_ORDER if prio == 1 else RR_ORDER2)]
            else:
                order = gens + (gens[3:6] if prio and len(gens) >= 6 else [])
            pend = {}
            alive = set(id(g) for g in gens)
            while alive:
                progressed = False
                for g in order:
                    if id(g) not in alive:
                        continue
                    k = pend.get(id(g))
                    if k is not None:
                        if not flags.get(k):
                            continue
                        pend[id(g)] = None
                    try:
                        r = next(g)
                    except StopIteration:
                        alive.discard(id(g))
                        progressed = True
                        continue
                    if isinstance(r, tuple) and r[0] == "wait":
                        pend[id(g)] = r[1]
                        progressed = True
                    elif isinstance(r, tuple) and r[0] == "nobank":
                        pass
                    else:
                        progressed = True
                assert progressed, "emission deadlock"

        def dma_x(t, xf, xfn):
            dma("sp", xf, x_d[t * P:(t + 1) * P, :], [], [xfn], "x" + xfn)

        def gen_load_x(t, b, xf=None, xfn=None, do_dma=True, ceng="act"):
            if xf is None:
                xf, xfn = x_f[b][:, :], "x_f%d" % b
            if do_dma:
                dma_x(t, xf, xfn)
            cp(ceng, x_bf[b][:, :], xf, [xfn], ["x_bf%d" % b])
            yield from need(1)
            bk, bn = nb()
            bkb = bk[:, :].bitcast(BF16)
            for kc in range(8):
                tr(bkb[:, kc * P:(kc + 1) * P], x_bf[b][:, kc * P:(kc + 1) * P], ["x_bf%d" % b], [bn], inc=(kc == 7))
            cp("act", xT[b][:, :], bkb[:, :], [bn], ["xT%d" % b])
            fb(bn)
            yield

        def proj_fm(W, col0, nheads, b, m=P):
            bk, bn = nb()
            for h in range(nheads):
                for kc in range(8):
                    mm(bk[0:m, h * P:(h + 1) * P], W[:, kc, col0 + h * m:col0 + (h + 1) * m],
                       xT[b][:, kc * P:(kc + 1) * P], kc == 0, kc == 7, ["xT%d" % b, wname(W, col0)], [bn],
                       inc=(h == nheads - 1 and kc == 7))
            return bk, bn

        def proj_tm(W, col0, b):
            bk, bn = nb()
            for kc in range(8):
                mm(bk[:, :], xT[b][:, kc * P:(kc + 1) * P], W[:, kc, col0:col0 + 512], kc == 0, kc == 7,
                   ["xT%d" % b, wname(W, col0)], [bn], inc=(kc == 7))
            return bk, bn

        def decay_chain(sw, u, par, Lbuf, Lname, scale, other, oname):
            B, Bn = TS[u][1], TN[u][1]
            Dd, Dn = TS[u][3], TN[u][3]
            el, eln = elast[u][par], "elast%d_%d" % (u, par)
            if sw == 1:
                sc.add("dve", lambda h: h.tensor_tensor_scan(out=B, data0=smask, data1=Lbuf,
                                                             initial=0.0, op0=ALU.mult, op1=ALU.add),
                       [Lname, "cst_f"], [Bn])
                tot = h3(B)[:, :, 127]
            else:
                sc.add("dve", lambda h: h.tensor_tensor_scan(out=B[:, ::-1], data0=smask, data1=Lbuf[:, ::-1],
                                                             initial=0.0, op0=ALU.mult, op1=ALU.add),
                       [Lname, "cst_f"], [Bn])
                tot = h3(B)[:, :, 0]
            yield
            act(el[:, :], tot, AF.Exp, [Bn], [eln], scale=-scale)
            act(Dd, B, AF.Exp, [Bn], [Dn], scale=-scale)
            yield
            if sw == 1:
                act(B, B, AF.Exp, [Bn, eln], [Bn], scale=scale)
                yield
                el_bc = el[:, :].unsqueeze(2).broadcast_to([P, 4, P])
                tt("dve", h3(other), h3(B), el_bc, ALU.mult, [Bn, eln], [oname])
                yield
            else:
                act(nbl[u][:, :], tot, AF.Identity, [Bn], ["nbl%d" % u], scale=-scale)
                for h in range(4):
                    sc.add("act", lambda hh, h=h: hh.activation(
                        out=other[:, h * P:(h + 1) * P], in_=B[:, h * P:(h + 1) * P], func=AF.Exp, scale=scale,
                        bias=nbl[u][:, h:h + 1]), [Bn, "nbl%d" % u] + ([oname] if oname != Bn else []), [oname])
                yield
                act(B, B, AF.Exp, [Bn, eln, oname], [Bn], scale=scale)
                yield
            return (Dd, Dn), (B, Bn), (other, oname)

        def stage_A_gla(sw, key, i, t, b, W, lrcol):
            u = 0
            TA, TC = TS[u][0], TS[u][2]
            TAn, TCn = TN[u][0], TN[u][2]
            yield from need(1)
            bk, bn = nb()
            for kc in range(8):
                mm(bk[0:16, 0:P], W[:, kc, lrcol:lrcol + 16], xT[b][:, kc * P:(kc + 1) * P], kc == 0, kc == 7,
                   ["xT%d" % b, wname(W, lrcol)], [bn], inc=(kc == 7))
            cp("act", lrT[0:16, :], bk[0:16, 0:P], [bn], ["lrT"])
            fb(bn)
            yield
            yield from need(1)
            bz, bzn = nb()
            for h in range(4):
                mm(bz[:, h * P:(h + 1) * P], gkb[sw - 1][:, h * P:(h + 1) * P], lrT[:, :], True, True,
                   ["lrT", "gkb%d" % (sw - 1)], [bzn], inc=(h == 3))
            act(TA, bz[:, :], AF.Exp, [bzn], [TAn], scale=-1.0)
            fb(bzn)
            yield
            act(TA, TA, AF.Ln, [TAn], [TAn], scale=1.0, bias=1.0)
            yield
            (eq, eqn), (ek, ekn), (ekh, ekhn) = yield from decay_chain(sw, u, i % 2, TA, TAn, 1.0 / 16.0, TC, TCn)
            if i >= 1:
                yield ("wait", (key, "B", i - 1, u))
            if sw == 1:
                yield from need(1)
                bq, bqn = proj_fm(W, 4112, 4, b)
                cp("act", qT_raw, bq[:, :], [bqn], ["qT_raw"])
                fb(bqn)
                yield
                yield from need(1)
                bkk, bkn = proj_fm(W, 4624, 4, b)
                cp("act", kT_raw, bkk[:, :], [bkn], ["kT_raw"])
                fb(bkn)
                dma("sp", qT_s.ap()[t], qT_raw, ["qT_raw"], ["qT_s%d" % t], "sq")
                dma("sp", kT_s.ap()[t], kT_raw, ["kT_raw"], ["kT_s%d" % t], "sk")
                yield
            else:
                dma("sp", qT_raw, qT_s.ap()[t], ["qT_s%d" % t], ["qT_raw"], "lq")
                dma("sp", kT_raw, kT_s.ap()[t], ["kT_s%d" % t], ["kT_raw"], "lk")
            tt("dve", QT[u], qT_raw, eq, ALU.mult, ["qT_raw", eqn], ["QT%d" % u])
            yield
            tt("dve", KT[u], kT_raw, ek, ALU.mult, ["kT_raw", ekn], ["KT%d" % u])
            yield
            tt("dve", KH[u], kT_raw, ekh, ALU.mult, ["kT_raw", ekhn], ["KH%d" % u])
            yield

        def stage_A_hgrn(sw, key, i, t, b, u, W, hfcol):
            hh = u - 1
            TA, TB, TC, TD = TS[u]
            TAn, TBn, TCn, TDn = TN[u]
            yield from need(1)
            bf, bfn = proj_fm(W, hfcol + hh * 512, 4, b)
            lb_bc = lbv[sw - 1][:, hh * 4:(hh + 1) * 4].unsqueeze(2).broadcast_to([P, 4, P])
            oml_bc = omlv[sw - 1][:, hh * 4:(hh + 1) * 4].unsqueeze(2).broadcast_to([P, 4, P])
            lbn, omn = "lbv%d" % (sw - 1), "omlv%d" % (sw - 1)
            act(TA, bf[:, :], AF.Exp, [bfn], [TAn], scale=-1.0)
            yield
            act(TB, TA, AF.Ln, [TAn], [TBn], scale=1.0, bias=1.0)
            tt("pool", h3(TC), h3(TA), lb_bc, ALU.mult, [TAn, lbn], [TCn])
            yield
            tt("dve", TA, bf[:, :], TB, ALU.add, [bfn, TBn], [TAn])
            fb(bfn)
            act(TC, TC, AF.Ln, [TCn], [TCn], scale=1.0, bias=1.0)
            yield
            act(TA, TA, AF.Exp, [TAn], [TAn], scale=-1.0)
            tt("pool", TC, TB, TC, ALU.subtract, [TBn, TCn], [TCn])
            yield
            tt("pool", h3(TA), h3(TA), oml_bc, ALU.mult, [TAn, omn], [TAn])
            (eq, eqn), (ek, ekn), (ekh, ekhn) = yield from decay_chain(sw, u, i % 2, TC, TCn, 1.0, TC, TCn)
            if i >= 1:
                yield ("wait", (key, "B", i - 1, u))
            yield ("wait", (key, "hq", i))
            hqn = "hq_raw%d_%d" % (b, hh)
            tt("dve", QT[u], hq_raw[b][:, hh * 512:(hh + 1) * 512], eq, ALU.mult, [hqn, eqn], ["QT%d" % u])
            yield
            tt("dve", KT[u], TA, ek, ALU.mult, [TAn, ekn], ["KT%d" % u])
            yield
            tt("dve", KH[u], TA, ekh, ALU.mult, [TAn, ekhn], ["KH%d" % u])
            yield

        def stage_B(sw, key, i, t, b, u):
            dv = 256 if u == 0 else 128
            s0 = 0 if u == 0 else 1024 + (u - 1) * 512
            if u == 0:
                V, vname = v_bf[b][:, :], "v_bf%d" % b
            else:
                V, vname = hi_bf[b][:, (u - 1) * 512:u * 512], "hi_bf%d" % b
            Sn, Sbn = "S%d" % u, "Sb%d" % u
            qt, kt, kh, kht, sct, el = QT[u], KT[u], KH[u], KHt[u], scT[u], elast[u][i % 2]
            qn, kn, khn, khtn, scn, eln = ("QT%d" % u, "KT%d" % u, "KH%d" % u, "KHt%d" % u, "scT%d" % u,
                                           "elast%d_%d" % (u, i % 2))
            yield from need(2)
            bk, bn = nb()
            for h in range(4):
                mm(bk[:, h * P:(h + 1) * P], kt[:, h * P:(h + 1) * P], qt[:, h * P:(h + 1) * P], True, True,
                   [kn, qn], [bn], inc=(h == 3))
            bk2, bn2 = nb()
            bkb = bk2[:, :].bitcast(BF16)
            for h in range(4):
                tr(bkb[:, h * P:(h + 1) * P], kh[:, h * P:(h + 1) * P], [khn], [bn2], inc=(h == 3))
            tri_bc = tri[sw - 1].unsqueeze(1).broadcast_to([P, 4, P])
            tt("dve", h3(sct), h3(bk[:, :]), tri_bc, ALU.mult, [bn, "cst_b"], [scn])
            cp("act", kht, bkb[:, 0:512], [bn2], [khtn])
            fb(bn, bn2)
            yield
            nbk = 2 if u == 0 else 1
            yield from need(nbk)
            obanks = [nb() for _ in range(nbk)]
            for h in range(4):
                ob, obn = obanks[(h * dv) // 512]
                oc = (h * dv) % 512
                mm(ob[:, oc:oc + dv], sct[:, h * P:(h + 1) * P], V[:, h * dv:(h + 1) * dv], True, False,
                   [scn, vname], [obn], inc=False)
                mm(ob[:, oc:oc + dv], qt[:, h * P:(h + 1) * P], Sb[:, s0 + h * dv:s0 + (h + 1) * dv], False, sw == 1,
                   [qn, Sbn], [obn], inc=(sw == 1))
                if sw == 2:
                    mm(ob[:, oc:oc + dv], ident, o1b[b][:, s0 + h * dv:s0 + (h + 1) * dv], False, True,
                       ["cst_b", "o1b%d_%d" % (b, u)], [obn], inc=True)
            flags[(key, "B", i, u)] = True
            yield
            ucols = slice(s0, s0 + 4 * dv)
            for j, (ob, obn) in enumerate(obanks):
                cols = slice(s0 + j * 512, s0 + (j + 1) * 512)
                if sw == 1:
                    cp("act", o1b[b][:, cols], ob[:, :], [obn], ["o1b%d_%d" % (b, u)])
                else:
                    cp("act", otot[:, cols], ob[:, :], [obn], ["otot%d" % u])
                fb(obn)
            if sw == 1:
                dma("sp", o1_s.ap()[t][:, ucols], o1b[b][:, ucols], ["o1b%d_%d" % (b, u)], ["o1_s%d_%d" % (t, u)],
                    "so1%d_%d" % (b, u))
            yield
            yield from need(nbk)
            pbanks = [nb() for _ in range(nbk)]
            for h in range(4):
                pb, pbn = pbanks[(h * dv) // 512]
                pc = (h * dv) % 512
                mm(pb[:, pc:pc + dv], kht[:, h * P:(h + 1) * P], V[:, h * dv:(h + 1) * dv], True, True,
                   [khtn, vname], [pbn], inc=True)
            yield
            for h in range(4):
                pb, pbn = pbanks[(h * dv) // 512]
                pc = (h * dv) % 512
                scol = slice(s0 + h * dv, s0 + (h + 1) * dv)
                sc.add("dve", lambda hh, scol=scol, pb=pb, pc=pc, h=h, el=el: hh.scalar_tensor_tensor(
                    out=S[:, scol], in0=S[:, scol], scalar=el[:, h:h + 1], in1=pb[:, pc:pc + dv],
                    op0=ALU.mult, op1=ALU.add), [Sn, eln, pbn], [Sn])
                if h % 2 == 1:
                    yield
            fb(*[pbn for (_, pbn) in pbanks])
            cp("pool", Sb[:, s0:s0 + 4 * dv], S[:, s0:s0 + 4 * dv], [Sn], [Sbn])
            yield
            if sw == 2:
                on = "otot%d" % u
                act(sq[:, 0:4 * dv], otot[:, ucols], AF.Square, [on], ["sq"])
                sc.add("dve", lambda h: h.tensor_reduce(
                    out=ss[u][:, :], in_=sq[:, 0:4 * dv].rearrange("p (h e) -> p h e", e=dv),
                    axis=mybir.AxisListType.X, op=ALU.add), ["sq"], ["ss%d" % u])
                yield
                act(rstd[u][:, :], ss[u][:, :], AF.Ln, ["ss%d" % u], ["rstd%d" % u], scale=1.0 / dv, bias=RMS_EPS)
                act(rstd[u][:, :], rstd[u][:, :], AF.Exp, ["rstd%d" % u], ["rstd%d" % u], scale=-0.5)
                yield
                r_bc = rstd[u][:, :].unsqueeze(2).broadcast_to([P, 4, dv])
                o3 = otot[:, ucols].rearrange("p (h e) -> p h e", e=dv)
                g3 = og[b][:, ucols].rearrange("p (h e) -> p h e", e=dv)
                for h in range(4):
                    sc.add("act", lambda hh, h=h: hh.activation(
                        out=og[b][:, s0 + h * dv:s0 + (h + 1) * dv], in_=otot[:, s0 + h * dv:s0 + (h + 1) * dv],
                        func=AF.Identity, scale=rstd[u][:, h:h + 1]), [on, "rstd%d" % u], ["og%d_%d" % (b, u)])
                dma("sp", og_s.ap()[t][:, ucols], og[b][:, ucols], ["og%d_%d" % (b, u)], ["og_s%d_%d" % (t, u)],
                    "sog%d_%d" % (b, u))
                yield

        def run_sweep(sw, key, order, pre_T, gen_Tv, after_proj, W, lrcol, hfcol):
            n = len(order)
            xs = lambda j: (x_f[j % 3][:, :], "x_f%d" % (j % 3))
            dma_x(order[0], *xs(0))
            if n > 1:
                dma_x(order[1], *xs(1))
            ceng = "dve" if sw == 1 else "act"
            run_rr([gen_load_x(order[0], 0, *xs(0), do_dma=False, ceng=ceng)])
            for i in range(n + 1):
                gens = []
                if i + 2 < n:
                    dma_x(order[i + 2], *xs(i + 2))
                if i < n:
                    t, b = order[i], i % 2
                    pre_T(t, b, i)
                    gens += [stage_A_gla(sw, key, i, t, b, W, lrcol),
                             stage_A_hgrn(sw, key, i, t, b, 1, W, hfcol),
                             stage_A_hgrn(sw, key, i, t, b, 2, W, hfcol)]
                if i >= 1:
                    pt, pb = order[i - 1], (i - 1) % 2
                    gens += [stage_B(sw, key, i - 1, pt, pb, 0), stage_B(sw, key, i - 1, pt, pb, 1),
                             stage_B(sw, key, i - 1, pt, pb, 2)]
                if i == n:
                    after_proj()
                if i < n:
                    gens.append(gen_Tv(t, b, i))
                if i + 1 < n:
                    gens.append(gen_load_x(order[i + 1], (i + 1) % 2, *xs(i + 1), do_dma=False, ceng=ceng))
                run_rr(gens, prio=sw if i < n else 0)

        def pre_T1(t, b, i):
            pass

        def gen_Tv1(t, b, i):
            for j in range(2):
                yield from need(1)
                bv, bvn = proj_tm(W1, 1040 + j * 512, b)
                act(v_bf[b][:, j * 512:(j + 1) * 512], bv[:, :], AF.Identity, [bvn], ["v_bf%d" % b], scale=Q_SCALE)
                fb(bvn)
                yield
            for j in range(2):
                yield from need(1)
                bv, bvn = proj_tm(W1, 2064 + j * 512, b)
                act(hi_bf[b][:, j * 512:(j + 1) * 512], bv[:, :], AF.Identity, [bvn], ["hi_bf%d" % b], scale=Q_SCALE)
                fb(bvn)
                yield
            yield from need(2)
            hb = [proj_fm(W1, 3088 + hh * 512, 4, b) for hh in range(2)]
            for hh in range(2):
                act(hq_raw[b][:, hh * 512:(hh + 1) * 512], hb[hh][0][:, :], AF.Silu, [hb[hh][1]],
                    ["hq_raw%d_%d" % (b, hh)])
                fb(hb[hh][1])
            flags[("s1", "hq", i)] = True
            dma("sp", v_s.ap()[t], v_bf[b][:, :], ["v_bf%d" % b], ["v_s%d" % t], "sv%d" % b)
            dma("sp", hi_s.ap()[t], hi_bf[b][:, :], ["hi_bf%d" % b], ["hi_s%d" % t], "shi%d" % b)
            dma("sp", hq_s.ap()[t], hq_raw[b][:, :], ["hq_raw%d_0" % b, "hq_raw%d_1" % b], ["hq_s%d" % t],
                "shq%d" % b)
            yield

        def after1():
            load_w(W2, w2_d, W2P, "2")

        if stop is None or stop >= 1:
            run_sweep(1, "s1", list(range(NT)), pre_T1, gen_Tv1, after1, W1, 0, 16)

        def exchange():
            dma("sp", cc_in.ap(), S[:, :], ["S0", "S1", "S2"], ["cc_in"], "xs")
            sc.add("pool", lambda h: h.collective_compute(
                "AllGather", ALU.bypass, replica_groups=[[0, 1], [2, 3], [4, 5], [6, 7]],
                ins=[cc_in.ap().opt()], outs=[cc_out.ap().opt()]), ["cc_in"], ["cc_out"], chan="cc", cc=True)
            dma("sp", otot[:, :], cc_out.ap()[0:P, :], ["cc_out"], ["otot0", "otot1", "otot2"], "xl0")
            sc.add("dve", lambda h: h.tensor_scalar(out=S[:, :], in0=otot[:, :], scalar1=sel[:, 0:1], scalar2=None,
                                                    op0=ALU.mult), ["otot0", "otot1", "otot2", "sel"], ["S0", "S1", "S2"])
            dma("sp", otot[:, :], cc_out.ap()[P:2 * P, :], ["cc_out"], ["otot0", "otot1", "otot2"], "xl1")
            sc.add("dve", lambda h: h.scalar_tensor_tensor(out=S[:, :], in0=otot[:, :], scalar=sel[:, 1:2],
                                                           in1=S[:, :], op0=ALU.mult, op1=ALU.add),
                   ["otot0", "otot1", "otot2", "sel", "S0", "S1", "S2"], ["S0", "S1", "S2"])
            cp("act", Sb, S[:, :], ["S0", "S1", "S2"], ["Sb0", "Sb1", "Sb2"])

        def pre_T2(t, b, i):
            dma("sp", v_bf[b][:, :], v_s.ap()[t], ["v_s%d" % t], ["v_bf%d" % b], "lv%d" % b)
            dma("sp", hi_bf[b][:, :], hi_s.ap()[t], ["hi_s%d" % t], ["hi_bf%d" % b], "lhi%d" % b)
            dma("sp", hq_raw[b][:, :], hq_s.ap()[t], ["hq_s%d" % t], ["hq_raw%d_0" % b, "hq_raw%d_1" % b],
                "lhq%d" % b)
            dma("sp", o1b[b][:, :], o1_s.ap()[t], ["o1_s%d_%d" % (t, u) for u in range(3)],
                ["o1b%d_%d" % (b, u) for u in range(3)], "lo1%d" % b)
            flags[("s2", "hq", i)] = True
            if i == 1:
                load_w(WB, wb_d, [(0, D, "WBhi")], "bh", kcs=range(8, 24), war_all=False)

        def gen_Tv2(t, b, i):
            for j in range(4):
                yield from need(1)
                gbk, gbn = proj_tm(W2, 1040 + j * 512, b)
                cp("act", sg[b][:, j * 512:(j + 1) * 512], gbk[:, :], [gbn], ["sg%d" % b])
                fb(gbn)
                yield
            dma("sp", g_s.ap()[t], sg[b][:, :], ["sg%d" % b], ["g_s%d" % t], "sg_s%d" % b)
            yield

        def after2():
            load_w(W3, w3_d, W3P, "3")
            load_w(WB, wb_d, [(0, D, "WBlo")], "b", kcs=range(8), war_all=False, nowaw=True)

        def gen_P1(t, b):
            yield from need(4)
            gb = [proj_tm(W3, j * 512, b) for j in range(4)]
            for j in range(4):
                act(sigm[b][:, j * 512:(j + 1) * 512], gb[j][0][:, :], AF.Sigmoid, [gb[j][1]], ["sigm%d" % b])
                fb(gb[j][1])
            act(sgt, sg[b][:, :], AF.Sigmoid, ["sg%d" % b], ["sgt"])
            yield
            tt("dve", sg[b][:, :], sg[b][:, :], sgt, ALU.mult, ["sg%d" % b, "sgt"], ["sg%d" % b])
            tt("dve", og[b][:, :], og[b][:, :], sg[b][:, :], ALU.mult, ["og%d" % b, "sg%d" % b], ["og%d" % b])
            yield
            for half in range(2):
                yield from need(1)
                bk, bn = nb()
                bkb = bk[:, :].bitcast(BF16)
                for c in range(8):
                    cc = half * 8 + c
                    tr(bkb[:, c * P:(c + 1) * P], og[b][:, cc * P:(cc + 1) * P], ["og%d" % b], [bn], inc=(c == 7))
                cp("act" if half == 0 else "dve", ogT[b][:, half * 1024:(half + 1) * 1024], bkb[:, :], [bn],
                   ["ogT%d_%d" % (b, half)])
                fb(bn)
                yield

        x_f3 = [x_f[0][:, :], x_f[1][:, :], x_f[2][:, :], sq[:, :]]
        x_f3n = ["x_f0", "x_f1", "x_f2", "sq"]
        y_bf2 = [y_bf, T4s[2][:, :].bitcast(BF16)[:, 0:1024]]
        sgt = T4s[2][:, :].bitcast(BF16)[:, 1024:3072]

        def gen_P2a(t, b):
            yb = y_bf2[b]
            for j in range(2):
                yield from need(2)
                bg, bgn = nb()
                for kc in range(8):
                    mm(bg[:, :], ogT[b][:, kc * P:(kc + 1) * P], WB[:, kc, j * 512:(j + 1) * 512], kc == 0, kc == 7,
                       ["ogT%d_0" % b, "WBlo"], [bgn], inc=(kc == 7))
                bh, bhn = nb()
                for kc in range(8):
                    mm(bh[:, :], ogT[b][:, (8 + kc) * P:(9 + kc) * P], WB[:, 8 + kc, j * 512:(j + 1) * 512],
                       kc == 0, kc == 7, ["ogT%d_1" % b, "WBhi"], [bhn], inc=(kc == 7))
                c0 = slice(j * 512, (j + 1) * 512)
                c1 = slice(1024 + j * 512, 1024 + (j + 1) * 512)
                tt("dve", S[:, c0], bg[:, :], sigm[b][:, c0], ALU.mult, [bgn, "sigm%d" % b], ["t1_%d" % j])
                tt("dve", S[:, c1], bh[:, :], sigm[b][:, c1], ALU.mult, [bhn, "sigm%d" % b], ["t2_%d" % j])
                fb(bgn, bhn)
                yield
                tt("pool" if j == 0 else "dve", yb[:, c0], S[:, c0], S[:, c1], ALU.add, ["t1_%d" % j, "t2_%d" % j],
                   ["y_bf%d_%d" % (b, j)])
                yield

        def gen_P2b(t, b):
            yb = y_bf2[b]
            xf, xfn = x_f3[t % 4], x_f3n[t % 4]
            yield from need(1)
            bk, bn = nb()
            bkb = bk[:, :].bitcast(BF16)
            for c in range(8):
                tr(bkb[:, c * P:(c + 1) * P], yb[:, c * P:(c + 1) * P], ["y_bf%d_%d" % (b, c // 4)], [bn], inc=(c == 7))
            cp("act", yT, bkb[:, :], [bn], ["yT"])
            fb(bn)
            yield
            for j in range(2):
                yield from need(1)
                bo, bon = nb()
                for kc in range(8):
                    mm(bo[:, :], yT[:, kc * P:(kc + 1) * P], WB[:, 16 + kc, j * 512:(j + 1) * 512],
                       kc == 0, kc == 7, ["yT", "WBhi"], [bon], inc=(kc == 7))
                c0 = slice(j * 512, (j + 1) * 512)
                sc.add("dve", lambda h, c0=c0, bo=bo, xf=xf: h.scalar_tensor_tensor(
                    out=otot[:, c0], in0=xf[:, c0], scalar=ALPHA, in1=bo[:, :], op0=ALU.mult, op1=ALU.add),
                    [xfn, bon], ["r%d" % j])
                sc.add("dve", lambda h, c0=c0, j=j: h.bn_stats(out=bst[:, j * 6:(j + 1) * 6], in_=otot[:, c0]),
                       ["r%d" % j], ["bst%d" % j])
                fb(bon)
                yield
            sc.add("dve", lambda h: h.bn_aggr(out=mv[:, :], in_=bst[:, :]), ["bst0", "bst1"], ["mv"])
            act(lnr[:, 0:1], mv[:, 1:2], AF.Ln, ["mv"], ["lnr"], scale=1.0, bias=LN_EPS)
            act(lnr[:, 0:1], lnr[:, 0:1], AF.Exp, ["lnr"], ["lnr"], scale=-0.5)
            sc.add("dve", lambda h: h.scalar_tensor_tensor(
                out=lnr[:, 1:2], in0=mv[:, 0:1], scalar=-1.0, in1=lnr[:, 0:1], op0=ALU.mult, op1=ALU.mult),
                ["mv", "lnr"], ["lnr"])
            yield
            sc.add("act", lambda h: h.activation(out=otot[:, 0:1024], in_=otot[:, 0:1024], func=AF.Identity,
                                                 scale=lnr[:, 0:1], bias=lnr[:, 1:2]),
                   ["r0", "r1", "lnr"], ["r0", "r1"])
            yield
            tt("pool", otot[:, 0:1024], otot[:, 0:1024], lng, ALU.mult, ["r0", "r1", "lng"], ["r0", "r1"])
            yield
            tt("dve", ores[b], otot[:, 0:1024], lnb, ALU.add, ["r0", "r1", "lnb"], ["ores%d" % b])
            dma("sp", out_d[t * P:(t + 1) * P, :], ores[b], ["ores%d" % b], [], "out%d" % b)
            yield

        def sweep3():
            sc.barrier()
            dma("sp", lng, lng_d, [], ["lng"], "c4")
            dma("sp", lnb, lnb_d, [], ["lnb"], "c5")
            for kc in range(16):
                gcol = (kc % 2) if kc < 8 else 2
                wr = "WBlo" if kc < 8 else "WBhi"
                sc.add("dve", lambda h, kc=kc, gcol=gcol: h.tensor_scalar(
                    out=WB[:, kc, :], in0=WB[:, kc, :], scalar1=gn[:, gcol:gcol + 1], scalar2=None, op0=ALU.mult),
                    [wr, "gn"], [wr])
            def p1_loads(t, b):
                dma_x(t, x_f3[t % 4], x_f3n[t % 4])
                dma("sp", og[b][:, :], og_s.ap()[t], ["og_s%d_%d" % (t, u) for u in range(3)], ["og%d" % b],
                    "log%d" % b)
                dma("sp", sg[b][:, :], g_s.ap()[t], ["g_s%d" % t], ["sg%d" % b], "lg%d" % b)

            p1_loads(0, 0)
            for k in range(NT + 2):
                gens = []
                if k + 1 < NT:
                    p1_loads(k + 1, (k + 1) % 2)
                if k - 2 >= 0:
                    gens.append(gen_P2b(k - 2, (k - 2) % 2))
                if 0 <= k - 1 < NT:
                    gens.append(gen_P2a(k - 1, (k - 1) % 2))
                if k < NT:
                    gens.append(seq(gen_load_x(k, k % 2, x_f3[k % 4], x_f3n[k % 4], do_dma=False), gen_P1(k, k % 2)))
                if len(gens) == 3:
                    gens = [gens[j] for j in RR3]
                run_rr(gens)

        def seq(*gens):
            for g in gens:
                yield from g

        if stop is None or stop >= 2:
            exchange()
            run_sweep(2, "s2", list(range(NT - 1, -1, -1)), pre_T2, gen_Tv2, after2, W2, 0, 16)
        if stop is None or stop >= 3:
            sweep3()

        block = es.enter_context(nc.Block())
        sc.emit(block)
    return nc


_COLS = dict(a_q=0, a_k=512, a_v=1024, a_gate=2048, lr_f=3072, lr_b=3088, h_q=3104, h_f_f=4128,
             h_f_b=5152, h_i=6176, h_gate=7200, m_gla=8224, m_hgrn=9248)


def _tile_rows(w):
    k = w.shape[0] // P
    return np.ascontiguousarray(w.reshape(k, P, w.shape[1]).transpose(1, 0, 2))


def make_in_maps(x, w_in, gla_gk_up_f, gla_gk_bias_f, gla_gk_up_b, gla_gk_bias_b, gla_norm_g,
                 hgrn_lb_logits_f, hgrn_lb_logits_b, hgrn_norm_g, w_branch_gla, w_branch_hgrn,
                 w_out, ln_g, ln_b):
    f32 = np.float32
    x = np.asarray(x, f32)
    win = np.asarray(w_in, f32)[0]
    c = _COLS

    def cols(name, n):
        return win[:, c[name]:c[name] + n]

    dirs = {
        "f": dict(lr=cols("lr_f", 16), hf=cols("h_f_f", 1024),
                  gk=np.concatenate([np.asarray(gla_gk_up_f, f32)[0], np.asarray(gla_gk_bias_f, f32)[0][None]], 0),
                  lbl=np.asarray(hgrn_lb_logits_f, f32)),
        "b": dict(lr=cols("lr_b", 16), hf=cols("h_f_b", 1024),
                  gk=np.concatenate([np.asarray(gla_gk_up_b, f32)[0], np.asarray(gla_gk_bias_b, f32)[0][None]], 0),
                  lbl=np.asarray(hgrn_lb_logits_b, f32)),
    }

    def lbl_layout(l):
        return np.ascontiguousarray(l.reshape(2, 8, P).transpose(2, 0, 1).reshape(P, 16))

    w3 = _tile_rows(np.concatenate([cols("m_gla", 1024), cols("m_hgrn", 1024)], 1))
    wb = _tile_rows(np.concatenate([np.asarray(w_branch_gla, f32)[0], np.asarray(w_branch_hgrn, f32)[0],
                                    np.asarray(w_out, f32)[0]], 0))
    gng = np.asarray(gla_norm_g, f32)[0]
    gn = np.stack([gng[0:128], gng[128:256], np.asarray(hgrn_norm_g, f32)[0]], 1)
    lng = np.ascontiguousarray(np.broadcast_to(np.asarray(ln_g, f32)[0][None], (P, D)))
    lnb = np.ascontiguousarray(np.broadcast_to(np.asarray(ln_b, f32)[0][None], (P, D)))
    ii = np.arange(P)
    smask = np.ones((P, 512), f32)
    smask[:, 0::128] = 0.0
    cst = np.concatenate([np.eye(P, dtype=f32), (ii[:, None] <= ii[None, :]).astype(f32),
                          (ii[:, None] >= ii[None, :]).astype(f32), smask], 1)

    per_dir = {}
    for d1, d2 in (("f", "b"), ("b", "f")):
        w1 = _tile_rows(np.concatenate([dirs[d1]["lr"], dirs[d1]["hf"], cols("a_v", 1024), cols("h_i", 1024),
                                        cols("h_q", 1024), cols("a_q", 512), cols("a_k", 512)], 1))
        w2 = _tile_rows(np.concatenate([dirs[d2]["lr"], dirs[d2]["hf"], cols("a_gate", 1024),
                                        cols("h_gate", 1024)], 1))
        per_dir[d1] = dict(w1=w1, w2=w2, gk1=dirs[d1]["gk"], gk2=dirs[d2]["gk"],
                           lbl1=lbl_layout(dirs[d1]["lbl"]), lbl2=lbl_layout(dirs[d2]["lbl"]))

    in_maps = []
    for core in range(NCORES):
        b, half = core // 2, core % 2
        TL = x.shape[1] // 2
        xl = x[b, half * TL:(half + 1) * TL]
        if half == 1:
            xl = xl[::-1]
        pd = per_dir["f" if half == 0 else "b"]
        sel = np.zeros((P, 2), f32)
        sel[:, 1 - half] = 1.0
        in_maps.append(dict(x=np.ascontiguousarray(xl), w1=pd["w1"], w2=pd["w2"], w3=w3, wb=wb,
                            gk1=pd["gk1"], gk2=pd["gk2"], lbl1=pd["lbl1"], lbl2=pd["lbl2"], gn=gn,
                            lng=lng, lnb=lnb, sel=sel, cst=cst))

    return in_maps


def assemble(outs, T):
    out = np.empty((4, T, D), np.float32)
    TL = T // 2
    for core in range(NCORES):
        b, half = core // 2, core % 2
        o = np.asarray(outs[core], np.float32)
        if half == 1:
            o = o[::-1]
        out[b, half * TL:(half + 1) * TL] = o
    return out


def kernel(**inputs):
    in_maps = make_in_maps(**inputs)
    nc = build_program()
    res = run_bass_kernel_spmd(nc, in_maps, core_ids=list(range(NCORES)))
    return assemble([res.results[c]["out"] for c in range(NCORES)], 4096)
```
